# Optimizing a Trainium2 kernel written in Bass

```python
import jax
import jax.numpy as jnp
from jax import lax
import numpy as np


D_MODEL = 1024
BATCH = 16
SEQ = 2048
DEPTH = 4

HEAD_DIM = 64
RWKV_HEADS = 8
RWKV_DIM = RWKV_HEADS * HEAD_DIM
DECAY_LORA = 64
ICLR_LORA = 64
GATE_LORA = 128
GA_Q_HEADS = 8
GA_KV_HEADS = 2
WA_Q_HEADS = 8
WA_KV_HEADS = 2
WINDOW = 128
BLOCK = 128
GRID_W = 64
ROPE_THETA = 10000.0
D_FF = 2816
N_BRANCH = 3
NORM_EPS = 1e-6
LNX_EPS = HEAD_DIM * 1e-5
NEG_INF = -1e30

RWKV_SPLITS = (RWKV_DIM, RWKV_DIM, RWKV_DIM, DECAY_LORA, DECAY_LORA, ICLR_LORA, ICLR_LORA, GATE_LORA)
RWKV_COLS = 3 * RWKV_DIM + 2 * DECAY_LORA + 2 * ICLR_LORA + GATE_LORA
GA_COLS = (GA_Q_HEADS + 2 * GA_KV_HEADS) * HEAD_DIM
WA_COLS = (WA_Q_HEADS + 2 * WA_KV_HEADS) * HEAD_DIM
GATE_COLS = N_BRANCH * D_MODEL
IN_COLS = RWKV_COLS + GA_COLS + WA_COLS + GATE_COLS

kernel_name = 'hybrid_rwkv7_axial_gqa_swa_macaron'


def _split(t, sizes):
    return jnp.split(t, np.cumsum(sizes)[:-1].tolist(), axis=-1)


def rmsnorm(x, g):
    xf = x.astype(jnp.float32)
    y = xf * lax.rsqrt(jnp.mean(xf * xf, axis=-1, keepdims=True) + NORM_EPS)
    return (y * g.astype(jnp.float32)).astype(x.dtype)


def swiglu(h, w_up, w_down):
    gate, up = jnp.split(h @ w_up, 2, axis=-1)
    return (jax.nn.silu(gate) * up) @ w_down


def centred_shift(p):
    prev = jnp.pad(p[:, :-1], ((0, 0), (1, 0), (0, 0)))
    nxt = jnp.pad(p[:, 1:], ((0, 0), (0, 1), (0, 0)))
    return 0.5 * (prev + nxt)


def rope_angles(pos, dim):
    inv_freq = ROPE_THETA ** (-jnp.arange(0, dim, 2, dtype=jnp.float32) / dim)
    return pos.astype(jnp.float32)[:, None] * inv_freq[None, :]


def apply_rope(x, ang):
    ang = ang.reshape(ang.shape[0], *([1] * (x.ndim - 3)), ang.shape[-1])
    cos, sin = jnp.cos(ang), jnp.sin(ang)
    x1, x2 = jnp.split(x.astype(jnp.float32), 2, axis=-1)
    return jnp.concatenate([x1 * cos - x2 * sin, x2 * cos + x1 * sin], axis=-1).astype(x.dtype)


def apply_axial_rope(x, ang_row, ang_col):
    x_row, x_col = jnp.split(x, 2, axis=-1)
    return jnp.concatenate([apply_rope(x_row, ang_row), apply_rope(x_col, ang_col)], axis=-1)


def wkv_scan(r, w, k, v, a, b, reverse):
    bsz, _, nh, n = r.shape
    xs = tuple(jnp.moveaxis(t, 1, 0) for t in (r, w, k, v, a, b))

    def step(state, inp):
        r_t, w_t, k_t, v_t, a_t, b_t = inp
        sa = jnp.einsum('bhij,bhj->bhi', state, a_t)
        state = (state * w_t[:, :, None, :] + sa[..., None] * b_t[:, :, None, :]
                 + v_t[..., None] * k_t[:, :, None, :])
        return state, jnp.einsum('bhij,bhj->bhi', state, r_t)

    s0 = jnp.zeros((bsz, nh, n, n), jnp.float32)
    _, ys = lax.scan(step, s0, xs, reverse=reverse)
    return jnp.moveaxis(ys, 0, 1)


def rwkv_mixer(p, mu, w0, w2, a0, a2, g2, k_k, k_a, r_k, lnx_g, lnx_b):
    bsz, s, _ = p.shape
    p = p.astype(jnp.float32)
    p = p + mu * (centred_shift(p) - p)
    r, k, v, wl_f, wl_b, al_f, al_b, gl = _split(p, RWKV_SPLITS)
    heads = lambda t: t.reshape(bsz, s, RWKV_HEADS, HEAD_DIM)
    kk = heads(k * k_k)
    kk = kk / jnp.maximum(jnp.linalg.norm(kk, axis=-1, keepdims=True), 1e-12)
    kk = kk.reshape(bsz, s, RWKV_DIM)
    wkv = jnp.zeros((bsz, s, RWKV_HEADS, HEAD_DIM), jnp.float32)
    bonus = jnp.zeros((bsz, s, RWKV_HEADS, HEAD_DIM), jnp.float32)
    for d, (wl, al) in enumerate(((wl_f, al_f), (wl_b, al_b))):
        w_log = -jax.nn.softplus(-(w0[d] + jnp.tanh(wl) @ w2[d])) - 0.5
        decay = jnp.exp(-jnp.exp(w_log))
        a = jax.nn.sigmoid(a0[d] + al @ a2[d])
        k_d = k * (1.0 + (a - 1.0) * k_a)
        wkv = wkv + wkv_scan(heads(r), heads(decay), heads(k_d), heads(v),
                             heads(-kk), heads(kk * a), reverse=(d == 1))
        bonus = bonus + jnp.sum(heads(r) * heads(k_d) * r_k, axis=-1, keepdims=True) * heads(v)
    mean = jnp.mean(wkv, axis=-1, keepdims=True)
    var = jnp.mean(jnp.square(wkv - mean), axis=-1, keepdims=True)
    y = ((wkv - mean) * lax.rsqrt(var + LNX_EPS)).reshape(bsz, s, RWKV_DIM) * lnx_g + lnx_b
    y = y + bonus.reshape(bsz, s, RWKV_DIM)
    gate = jax.nn.sigmoid(gl) @ g2
    return y * gate


def global_attention(q, k, v):
    bsz, s, nkv, ng, hd = q.shape
    nblk = s // BLOCK
    scale = HEAD_DIM ** -0.5
    qb = jnp.moveaxis(q.reshape(bsz, nblk, BLOCK, nkv, ng, hd), 1, 0)

    def one_block(q_blk):
        sc = jnp.einsum('bqkgd,bskd->bkgqs', q_blk, k, preferred_element_type=jnp.float32) * scale
        pr = jax.nn.softmax(sc, axis=-1)
        return jnp.einsum('bkgqs,bskd->bqkgd', pr.astype(v.dtype), v)

    o = lax.map(one_block, qb)
    return jnp.moveaxis(o, 0, 1).reshape(bsz, s, nkv * ng * hd)


def window_attention(q, k, v, sink):
    bsz, s, nkv, ng, hd = q.shape
    nblk = s // BLOCK
    span = BLOCK + 2 * WINDOW
    scale = HEAD_DIM ** -0.5
    kp = jnp.pad(k, ((0, 0), (WINDOW, WINDOW), (0, 0), (0, 0)))
    vp = jnp.pad(v, ((0, 0), (WINDOW, WINDOW), (0, 0), (0, 0)))
    rel = jnp.arange(span)[None, :] - WINDOW - jnp.arange(BLOCK)[:, None]
    band = jnp.abs(rel) <= WINDOW
    sink_l = sink.astype(jnp.float32).reshape(nkv, ng)[None, :, :, None, None]

    def one_block(i):
        q_blk = lax.dynamic_slice_in_dim(q, i * BLOCK, BLOCK, axis=1)
        k_blk = lax.dynamic_slice_in_dim(kp, i * BLOCK, span, axis=1)
        v_blk = lax.dynamic_slice_in_dim(vp, i * BLOCK, span, axis=1)
        kpos = i * BLOCK - WINDOW + jnp.arange(span)
        valid = band & ((kpos >= 0) & (kpos < s))[None, :]
        sc = jnp.einsum('bqkgd,bjkd->bkgqj', q_blk, k_blk, preferred_element_type=jnp.float32) * scale
        sc = jnp.where(valid, sc, NEG_INF)
        m = jnp.maximum(jnp.max(sc, axis=-1, keepdims=True), sink_l)
        e = jnp.exp(sc - m)
        pr = e / (jnp.sum(e, axis=-1, keepdims=True) + jnp.exp(sink_l - m))
        return jnp.einsum('bkgqj,bjkd->bqkgd', pr.astype(v.dtype), v_blk)

    o = lax.map(one_block, jnp.arange(nblk))
    return jnp.moveaxis(o, 0, 1).reshape(bsz, s, nkv * ng * hd)


def setup_inputs(seed: int = 0) -> dict:
    key = jax.random.key(seed)
    ks = jax.random.split(key, 40)
    f32 = jnp.float32
    nrm = lambda kk, shape, scale: (jax.random.normal(kk, shape, f32) * scale).astype(f32)
    gain = lambda kk, shape: 1.0 + 0.02 * jax.random.normal(kk, shape, f32)
    L = DEPTH
    return {
        'x': jax.random.normal(ks[0], (BATCH, SEQ, D_MODEL), f32),
        'ffn1_norm': gain(ks[1], (L, D_MODEL)),
        'ffn1_w_up': nrm(ks[2], (L, D_MODEL, 2 * D_FF), D_MODEL ** -0.5),
        'ffn1_w_down': nrm(ks[3], (L, D_FF, D_MODEL), 0.5 * D_FF ** -0.5),
        'mix_norm': gain(ks[4], (L, D_MODEL)),
        'w_in': nrm(ks[5], (L, D_MODEL, IN_COLS), D_MODEL ** -0.5),
        'rwkv_mu': jax.random.uniform(ks[6], (L, RWKV_COLS), f32, 0.0, 1.0),
        'rwkv_w0': jax.random.uniform(ks[7], (L, 2, RWKV_DIM), f32, -6.0, -0.5),
        'rwkv_w2': nrm(ks[8], (L, 2, DECAY_LORA, RWKV_DIM), 0.1 * DECAY_LORA ** -0.5),
        'rwkv_a0': nrm(ks[9], (L, 2, RWKV_DIM), 0.1),
        'rwkv_a2': nrm(ks[10], (L, 2, ICLR_LORA, RWKV_DIM), 0.5 * ICLR_LORA ** -0.5),
        'rwkv_g2': nrm(ks[11], (L, GATE_LORA, RWKV_DIM), GATE_LORA ** -0.5),
        'rwkv_k_k': 0.85 + nrm(ks[12], (L, RWKV_DIM), 0.05),
        'rwkv_k_a': 1.0 + nrm(ks[13], (L, RWKV_DIM), 0.05),
        'rwkv_r_k': nrm(ks[14], (L, RWKV_HEADS, HEAD_DIM), 0.1),
        'rwkv_lnx_g': gain(ks[15], (L, RWKV_DIM)),
        'rwkv_lnx_b': nrm(ks[16], (L, RWKV_DIM), 0.02),
        'ga_q_norm': gain(ks[17], (L, HEAD_DIM)),
        'ga_k_norm': gain(ks[18], (L, HEAD_DIM)),
        'wa_sink': nrm(ks[19], (L, WA_Q_HEADS), 0.5),
        'w_o_rwkv': nrm(ks[20], (L, RWKV_DIM, D_MODEL), RWKV_DIM ** -0.5),
        'w_o_ga': nrm(ks[21], (L, GA_Q_HEADS * HEAD_DIM, D_MODEL), (GA_Q_HEADS * HEAD_DIM) ** -0.5),
        'w_o_wa': nrm(ks[22], (L, WA_Q_HEADS * HEAD_DIM, D_MODEL), (WA_Q_HEADS * HEAD_DIM) ** -0.5),
        'w_out': nrm(ks[23], (L, D_MODEL, D_MODEL), 0.5 * D_MODEL ** -0.5),
        'ffn2_norm': gain(ks[24], (L, D_MODEL)),
        'ffn2_w_up': nrm(ks[25], (L, D_MODEL, 2 * D_FF), D_MODEL ** -0.5),
        'ffn2_w_down': nrm(ks[26], (L, D_FF, D_MODEL), 0.5 * D_FF ** -0.5),
        'final_norm': gain(ks[27], (D_MODEL,)),
    }


def reference(x, ffn1_norm, ffn1_w_up, ffn1_w_down, mix_norm, w_in, rwkv_mu, rwkv_w0, rwkv_w2,
              rwkv_a0, rwkv_a2, rwkv_g2, rwkv_k_k, rwkv_k_a, rwkv_r_k, rwkv_lnx_g, rwkv_lnx_b,
              ga_q_norm, ga_k_norm, wa_sink, w_o_rwkv, w_o_ga, w_o_wa, w_out,
              ffn2_norm, ffn2_w_up, ffn2_w_down, final_norm):
    bsz, s, dm = x.shape
    rows = s // GRID_W
    t = jnp.arange(s)
    row = jnp.repeat(jnp.arange(rows), GRID_W)
    col = jnp.tile(jnp.arange(GRID_W), rows)
    ang_row = rope_angles(row, HEAD_DIM // 2)
    ang_col = rope_angles(col, HEAD_DIM // 2)
    ang_1d = rope_angles(t, HEAD_DIM)
    ga_g = GA_Q_HEADS // GA_KV_HEADS
    wa_g = WA_Q_HEADS // WA_KV_HEADS

    for l in range(DEPTH):
        x = x + 0.5 * swiglu(rmsnorm(x, ffn1_norm[l]), ffn1_w_up[l], ffn1_w_down[l])

        h = rmsnorm(x, mix_norm[l])
        p = h @ w_in[l]
        p_rwkv, p_ga, p_wa, p_gate = _split(p, (RWKV_COLS, GA_COLS, WA_COLS, GATE_COLS))

        y_a = rwkv_mixer(p_rwkv, rwkv_mu[l], rwkv_w0[l], rwkv_w2[l], rwkv_a0[l], rwkv_a2[l],
                         rwkv_g2[l], rwkv_k_k[l], rwkv_k_a[l], rwkv_r_k[l],
                         rwkv_lnx_g[l], rwkv_lnx_b[l]).astype(x.dtype)

        qb, kb, vb = _split(p_ga, (GA_Q_HEADS * HEAD_DIM, GA_KV_HEADS * HEAD_DIM, GA_KV_HEADS * HEAD_DIM))
        qb = qb.reshape(bsz, s, GA_KV_HEADS, ga_g, HEAD_DIM)
        kb = kb.reshape(bsz, s, GA_KV_HEADS, HEAD_DIM)
        vb = vb.reshape(bsz, s, GA_KV_HEADS, HEAD_DIM)
        qb = apply_axial_rope(rmsnorm(qb, ga_q_norm[l]), ang_row, ang_col)
        kb = apply_axial_rope(rmsnorm(kb, ga_k_norm[l]), ang_row, ang_col)
        y_b = global_attention(qb, kb, vb)

        qc, kc, vc = _split(p_wa, (WA_Q_HEADS * HEAD_DIM, WA_KV_HEADS * HEAD_DIM, WA_KV_HEADS * HEAD_DIM))
        qc = apply_rope(qc.reshape(bsz, s, WA_KV_HEADS, wa_g, HEAD_DIM), ang_1d)
        kc = apply_rope(kc.reshape(bsz, s, WA_KV_HEADS, HEAD_DIM), ang_1d)
        vc = vc.reshape(bsz, s, WA_KV_HEADS, HEAD_DIM)
        y_c = window_attention(qc, kc, vc, wa_sink[l])

        gates = jax.nn.sigmoid(p_gate).reshape(bsz, s, N_BRANCH, dm)
        merged = (gates[:, :, 0] * (y_a @ w_o_rwkv[l]) + gates[:, :, 1] * (y_b @ w_o_ga[l])
                  + gates[:, :, 2] * (y_c @ w_o_wa[l]))
        x = x + merged @ w_out[l]

        x = x + 0.5 * swiglu(rmsnorm(x, ffn2_norm[l]), ffn2_w_up[l], ffn2_w_down[l])

    return rmsnorm(x, final_norm)
```

```python
import numpy as np
from contextlib import ExitStack
import concourse.bass as bass
import concourse.mybir as mybir
from concourse.bass_utils import run_bass_kernel_spmd

F32 = mybir.dt.float32
BF16 = mybir.dt.bfloat16
AF = mybir.ActivationFunctionType
ALU = mybir.AluOpType
AX = mybir.AxisListType

D = 1024
DFF = 2816
NJ = DFF // 128
NDMA_SEM = 12
PSUM_KEYS = set()


class Sched:
    ENGS = ("pe", "act", "dve", "pool", "sp")

    def __init__(self, nc):
        self.nc = nc
        self.ops = []

    def op(self, eng, fn, r=(), w=(), dma=False):
        self.ops.append([eng, fn, tuple(r), tuple(w), dma])

    def dma(self, q, out, in_, r=(), w=(), **kw):
        self.op(q, lambda e: e.dma_start(out=out, in_=in_, **kw), r, w, dma=True)

    def emit_phase(self):
        nc = self.nc
        ops = self.ops
        self.ops = []
        n = len(ops)
        if n == 0:
            return
        if not hasattr(self, "eng_count"):
            self.eng_count = {e: 0 for e in self.ENGS}
            self.dma_count = {e: 0 for e in self.ENGS}
            self.waited = {e: {} for e in self.ENGS}
            self.sems = {}
            self.semstack = ExitStack()
            self.ninst = 0
        last_w = {}
        readers = {}
        deps = [None] * n
        for i, (eng, fn, r, w, dma) in enumerate(ops):
            d = set()
            for k in r:
                j = last_w.get(k)
                if j is not None:
                    d.add(j)
            for k in w:
                j = last_w.get(k)
                if j is not None:
                    d.add(j)
                for j in readers.get(k, ()):
                    d.add(j)
            d.discard(i)
            deps[i] = d
            for k in r:
                readers.setdefault(k, []).append(i)
            for k in w:
                last_w[k] = i
                readers[k] = []
        last_x = {}
        for i, (eng, fn, r, w, dma) in enumerate(ops):
            for k in set(r) | set(w):
                if k[0] in PSUM_KEYS:
                    lx = last_x.setdefault(k, {})
                    for e2, j in lx.items():
                        if e2 != eng:
                            deps[i].add(j)
                    lx[eng] = i
        for i in range(n):
            eng, fn, r, w, dma = ops[i]
            keep = set()
            rw = set(r) | set(w)
            for j in deps[i]:
                ej, _, rj, wj, dj = ops[j]
                if ej == eng and not dj:
                    if eng == "pe":
                        continue
                    if not (set(wj) & rw):
                        continue
                keep.add(j)
            deps[i] = keep
        signal = [False] * n
        for i in range(n):
            for j in deps[i]:
                signal[j] = True
        eng_count = self.eng_count
        dma_count = self.dma_count
        sig = [None] * n
        pre_wait = [None] * n
        for i, (eng, fn, r, w, dma) in enumerate(ops):
            if dma:
                c = dma_count[eng]
                sidx = c % NDMA_SEM
                v = 16 * (c // NDMA_SEM + 1)
                sig[i] = (("dma", eng, sidx), v)
                if c >= NDMA_SEM:
                    pre_wait[i] = (("dma", eng, sidx), v - 16)
                dma_count[eng] = c + 1
            elif signal[i]:
                eng_count[eng] += 1
                sig[i] = (("eng", eng), eng_count[eng])
        per_eng = {e: [] for e in self.ENGS}
        waited = self.waited
        for i, (eng, fn, r, w, dma) in enumerate(ops):
            waits = {}
            if pre_wait[i] is not None:
                sk, v = pre_wait[i]
                waits[sk] = max(waits.get(sk, 0), v)
            for j in deps[i]:
                sk, v = sig[j]
                waits[sk] = max(waits.get(sk, 0), v)
            wl = []
            for sk, v in waits.items():
                if waited[eng].get(sk, 0) >= v:
                    continue
                waited[eng][sk] = v
                wl.append((sk, v))
            per_eng[eng].append((wl, fn, sig[i], dma))
        final = {e: [] for e in self.ENGS}
        for e in self.ENGS:
            c = dma_count[e]
            for sidx in range(min(c, NDMA_SEM)):
                cnt = (c - sidx + NDMA_SEM - 1) // NDMA_SEM
                sk = ("dma", e, sidx)
                if waited[e].get(sk, 0) < 16 * cnt:
                    waited[e][sk] = 16 * cnt
                    final[e].append((sk, 16 * cnt))
        for e in self.ENGS:
            for wl, fn, sg, dma in per_eng[e]:
                self.ninst += 1 + len(wl)
                if sg is not None and sg[0] not in self.sems:
                    self.sems[sg[0]] = self.semstack.enter_context(
                        nc.semaphore("s_" + "_".join(str(x) for x in sg[0])))
        sems = self.sems
        with nc.Block() as block:
            def run(e, engobj):
                for wl, fn, sg, dma in per_eng[e]:
                    for sk, v in wl:
                        engobj.wait_ge(sems[sk], v)
                    ins = fn(engobj)
                    if sg is not None:
                        ins.then_inc(sems[sg[0]], 16 if dma else 1)
                for sk, v in final[e]:
                    engobj.wait_ge(sems[sk], v)

            @block.tensor
            def _(e):
                run("pe", e)

            @block.scalar
            def _(e):
                run("act", e)

            @block.vector
            def _(e):
                run("dve", e)

            @block.gpsimd
            def _(e):
                run("pool", e)

            @block.sync
            def _(e):
                run("sp", e)

    def close(self):
        if hasattr(self, "semstack"):
            self.semstack.close()


class Builder:
    def __init__(self):
        self.nc = bass.Bass("TRN2", target_bir_lowering=False)
        self.S = Sched(self.nc)
        self.st = None
        self.T = {}
        self.pid = 0

    def dram_in(self, name, shape, dt=F32):
        return self.nc.dram_tensor(name, list(shape), dt, kind="ExternalInput").ap()

    def dram_out(self, name, shape, dt=F32):
        return self.nc.dram_tensor(name, list(shape), dt, kind="ExternalOutput").ap()

    def dram_tmp(self, name, shape, dt=F32):
        kind = "ExternalOutput" if getattr(self, "debug_scratch", False) else "Internal"
        return self.nc.dram_tensor(name, list(shape), dt, kind=kind).ap()

    def begin(self):
        self.st = ExitStack()
        self.T = {}
        self.pid += 1

    def end(self):
        self.S.emit_phase()
        self.st.close()
        self.st = None
        self.T = {}

    def sb(self, name, shape, dt):
        if name not in self.T:
            self.T[name] = self.st.enter_context(
                self.nc.sbuf_tensor("sb%d_%s" % (self.pid, name), list(shape), dt))
        return self.T[name]

    def ps(self, name, shape, dt=F32):
        if name not in self.T:
            self.T[name] = self.st.enter_context(
                self.nc.psum_tensor("ps%d_%s" % (self.pid, name), list(shape), dt))
        return self.T[name]

    def finish(self):
        self.S.close()
        return self.nc


def load_consts(B, ident_d):
    S = B.S
    ident = B.sb("ident", [128, 128], BF16)
    S.dma("sp", ident[:], ident_d, w=[("ident",)])
    m05 = B.sb("m05", [128, 8], F32)
    S.op("pool", lambda e: e.memset(m05[:], -0.5), w=[("m05",)])


def load_gT(B, name, vec_d):
    S = B.S
    t = B.sb(name, [128, 8], F32)
    S.dma("sp", t[:], vec_d.rearrange("(c p) -> p c", p=128), w=[(name,)],
          allow_slow_non_contiguous=True)
    return t


def norm_tile(B, x_d, gname, gT, r0, t, dst, dst_key, tag, part="both"):
    S = B.S
    xt = B.sb("xt", [128, 2, D], F32)
    junk = B.sb("junk", [128, D], BF16)
    ss = B.sb("ss", [128, 8], F32)
    rs = B.sb("rs", [128, 8], F32)
    xn = B.sb("xn", [128, 2, D], BF16)
    ident = B.T["ident"]
    m05 = B.T["m05"]
    ptr = B.ps("ptr", [128, 1024], BF16)
    sl = t % 2
    c8 = t % 8
    if part in ("both", "front"):
        norm_tile_front(S, x_d, r0, tag, xt, junk, ss, rs, xn, m05, sl, c8)
    if part == "front":
        return
    for c in range(8):
        S.op("pe", lambda e, c=c: e.transpose(out=ptr[:, c * 128:(c + 1) * 128], in_=xn[:, sl, c * 128:(c + 1) * 128],
                                              identity=ident[:]), r=[("xn", sl), ("ident",)], w=[("ptr",)])
    S.op("dve", lambda e: e.tensor_tensor(out=dst, in0=ptr[:].rearrange("p (c n) -> p c n", c=8),
                                          in1=gT[:].unsqueeze(2).to_broadcast([128, 8, 128]), op=ALU.mult),
         r=[("ptr",), (gname,)], w=[dst_key])


def norm_tile_front(S, x_d, r0, tag, xt, junk, ss, rs, xn, m05, sl, c8):
    S.dma("sp", xt[:, sl, :], x_d[r0:r0 + 128, :], r=[(tag, "x", r0)], w=[("xt", sl)])
    S.op("act", lambda e: e.activation(out=junk[:], in_=xt[:, sl, :], func=AF.Square, accum_out=ss[:, c8:c8 + 1]),
         r=[("xt", sl)], w=[("junk",), ("ss", c8)])
    S.op("dve", lambda e: e.tensor_scalar(out=rs[:, c8:c8 + 1], in0=ss[:, c8:c8 + 1], scalar1=1.0 / D, scalar2=1e-6,
                                          op0=ALU.mult, op1=ALU.add), r=[("ss", c8)], w=[("rs", c8)])
    S.op("pool", lambda e: e.tensor_tensor(out=rs[:, c8:c8 + 1], in0=rs[:, c8:c8 + 1], in1=m05[:, 0:1], op=ALU.pow),
         r=[("rs", c8), ("m05",)], w=[("rs", c8)])
    S.op("act", lambda e: e.activation(out=xn[:, sl, :], in_=xt[:, sl, :], func=AF.Copy, scale=rs[:, c8:c8 + 1]),
         r=[("xt", sl), ("rs", c8)], w=[("xn", sl)])


def norm_to_featmajor(B, x_d, gname, gT, tok0, ntile, xnT, xnT_key, tag):
    for t in range(ntile):
        norm_tile(B, x_d, gname, gT, tok0 + t * 128, t, xnT[:, :, t * 128:(t + 1) * 128], (xnT_key, t), tag)


def ffn_phase(B, x_in, x_out, wup_d, wdn_d, gname, gT, ntok, tag_in, tag_out):
    S = B.S
    TB = 1024
    nblk = ntok // TB
    xnT2 = B.sb("xnT", [128, 2, 8, TB], BF16)
    hT = B.sb("hT", [128, NJ, TB], BF16)
    wd = B.sb("wd", [128, NJ, D], BF16)
    wgu = B.sb("wgu", [128, 2, 2, 8, 256], BF16)
    sg = B.sb("sg", [128, 2, 512], F32)
    yo = B.sb("yo", [128, 2, D], F32)
    xr = B.sb("xr", [128, 2, D], F32)
    pg = [B.ps("pg0", [128, 512]), B.ps("pg1", [128, 512])]
    pu = [B.ps("pu0", [128, 512]), B.ps("pu1", [128, 512])]
    pd = [B.ps("pd0", [128, 512]), B.ps("pd1", [128, 512])]
    wup_v = wup_d.rearrange("(kc p) n -> p kc n", p=128)
    wdn_v = wdn_d.rearrange("(j p) n -> p j n", p=128)
    for j0 in range(0, NJ, 2):
        S.dma("pool", wd[:, j0:j0 + 2, :], wdn_v[:, j0:j0 + 2, :], w=[("wd", j0)])
    gi = 0
    ei = 0
    for b in range(nblk):
        tok0 = b * TB
        xp = b % 2
        xnT = xnT2[:, xp, :, :]
        if b == 0:
            for t in range(TB // 128):
                norm_tile(B, x_in, gname, gT, t * 128, t, xnT2[:, 0, :, t * 128:(t + 1) * 128], ("xnT", 0, t), tag_in)
        for jg in range(NJ // 2):
            sl = gi % 2
            gi += 1
            S.dma("pool", wgu[:, sl, 0, :, :], wup_v[:, :, jg * 256:(jg + 1) * 256],
                  w=[("wgu", sl, 0)])
            S.dma("pool", wgu[:, sl, 1, :, :], wup_v[:, :, DFF + jg * 256:DFF + (jg + 1) * 256],
                  w=[("wgu", sl, 1)])
            for jj in range(2):
                j = jg * 2 + jj
                for tb in range(TB // 512):
                    e2 = ei % 2
                    ei += 1
                    for kc in range(8):
                        S.op("pe", lambda e, sl=sl, jj=jj, kc=kc, tb=tb, e2=e2, xnT=xnT: e.matmul(
                            pg[e2][:], lhsT=wgu[:, sl, 0, kc, jj * 128:(jj + 1) * 128],
                            rhs=xnT[:, kc, tb * 512:(tb + 1) * 512], start=(kc == 0), stop=(kc == 7)),
                            r=[("wgu", sl, 0)] + [("xnT", xp, tb * 4 + q) for q in range(4)],
                            w=[("pg", e2)])
                    for kc in range(8):
                        S.op("pe", lambda e, sl=sl, jj=jj, kc=kc, tb=tb, e2=e2, xnT=xnT: e.matmul(
                            pu[e2][:], lhsT=wgu[:, sl, 1, kc, jj * 128:(jj + 1) * 128],
                            rhs=xnT[:, kc, tb * 512:(tb + 1) * 512], start=(kc == 0), stop=(kc == 7)),
                            r=[("wgu", sl, 1)] + [("xnT", xp, tb * 4 + q) for q in range(4)],
                            w=[("pu", e2)])
                    S.op("act", lambda e, e2=e2: e.activation(
                        out=sg[:, e2, :], in_=pg[e2][:], func=AF.Silu),
                        r=[("pg", e2)], w=[("sg", e2)])
                    S.op("dve", lambda e, e2=e2, j=j, tb=tb: e.tensor_tensor(
                        out=hT[:, j, tb * 512:(tb + 1) * 512], in0=pu[e2][:], in1=sg[:, e2, :],
                        op=ALU.mult), r=[("pu", e2), ("sg", e2)], w=[("hT", j, tb)])
        for t in range(TB // 128):
            r0 = tok0 + t * 128
            xs = t % 2
            S.dma("sp", xr[:, xs, :], x_in[r0:r0 + 128, :], r=[(tag_in, "x", r0)], w=[("xr", xs)])
            for nh in range(2):
                e2 = ei % 2
                ei += 1
                for j in range(NJ):
                    S.op("pe", lambda e, j=j, t=t, nh=nh, e2=e2: e.matmul(
                        pd[e2][:], lhsT=hT[:, j, t * 128:(t + 1) * 128],
                        rhs=wd[:, j, nh * 512:(nh + 1) * 512], start=(j == 0), stop=(j == NJ - 1)),
                        r=[("hT", j, t // 4), ("wd", j - j % 2)], w=[("pd", e2)])
                S.op("dve", lambda e, e2=e2, xs=xs, nh=nh: e.scalar_tensor_tensor(
                    out=yo[:, xs, nh * 512:(nh + 1) * 512], in0=pd[e2][:], scalar=0.5,
                    in1=xr[:, xs, nh * 512:(nh + 1) * 512],
                    op0=ALU.mult, op1=ALU.add), r=[("pd", e2), ("xr", xs)], w=[("yo", xs)])
            S.dma("sp", x_out[r0:r0 + 128, :], yo[:, xs, :],
                  r=[("yo", xs)], w=[(tag_out, "x", r0)])
            if b + 1 < nblk:
                nb0 = (b + 1) * TB
                if t == 0:
                    norm_tile(B, x_in, gname, gT, nb0, 0, None, None, tag_in, part="front")
                if t + 1 < TB // 128:
                    norm_tile(B, x_in, gname, gT, nb0 + (t + 1) * 128, t + 1, None, None, tag_in, part="front")
                norm_tile(B, x_in, gname, gT, nb0 + t * 128, t,
                          xnT2[:, 1 - xp, :, t * 128:(t + 1) * 128], ("xnT", 1 - xp, t), tag_in, part="back")


def build_ffn_test(ntok):
    B = Builder()
    x = B.dram_in("x", [ntok, D])
    wup = B.dram_in("wup", [D, 2 * DFF])
    wdn = B.dram_in("wdn", [DFF, D])
    g = B.dram_in("g", [D])
    ident = B.dram_in("ident", [128, 128], BF16)
    y = B.dram_out("y", [ntok, D])
    B.begin()
    load_consts(B, ident)
    gT = load_gT(B, "g_ffn", g)
    ffn_phase(B, x, y, wup, wdn, "g_ffn", gT, ntok, "xin", "xout")
    B.end()
    return B.finish()


SEQ = 2048
LAM = float(np.exp(-0.5))


def make_consts():
    import ml_dtypes
    bf = ml_dtypes.bfloat16
    c = {}
    c["ident"] = np.eye(128, dtype=np.float32).astype(bf)
    t = np.arange(SEQ)
    row = (t // 64).astype(np.float32)
    col = (t % 64).astype(np.float32)
    f16 = (np.float32(10000.0) ** (-np.arange(0, 32, 2, dtype=np.float32) / np.float32(32))).astype(np.float32)
    f32_ = (np.float32(10000.0) ** (-np.arange(0, 64, 2, dtype=np.float32) / np.float32(64))).astype(np.float32)
    ar = row[:, None] * f16[None, :]
    ac = col[:, None] * f16[None, :]
    a1 = t.astype(np.float32)[:, None] * f32_[None, :]
    rope = np.zeros((SEQ, 4, 64), np.float32)
    rope[:, 0] = np.concatenate([np.cos(ar), np.cos(ar), np.cos(ac), np.cos(ac)], -1)
    rope[:, 1] = np.concatenate([-np.sin(ar), np.sin(ar), -np.sin(ac), np.sin(ac)], -1)
    rope[:, 2] = np.concatenate([np.cos(a1), np.cos(a1)], -1)
    rope[:, 3] = np.concatenate([-np.sin(a1), np.sin(a1)], -1)
    c["rope"] = rope
    i = np.arange(128)[:, None]
    j = np.arange(128)[None, :]
    c["wamask"] = np.concatenate([(i <= j), np.ones((128, 128), bool), (i >= j)], 1).astype(np.float32).astype(bf)
    s_ = np.arange(64)[:, None]
    t_ = np.arange(64)[None, :]

    def bd(m):
        z = np.zeros((128, 128), np.float32)
        z[0:64, 0:64] = m
        z[64:128, 64:128] = m
        return z
    mkf = np.concatenate([bd(s_ < t_), bd(s_ <= t_), bd(s_ < t_), bd(s_ <= t_), bd(t_ < s_)], 1)
    mkb = np.concatenate([bd(s_ > t_), bd(s_ >= t_), bd(s_ > t_), bd(s_ >= t_), bd(t_ > s_)], 1)
    c["mkf"] = mkf.astype(np.float32)
    c["mkb"] = mkb.astype(np.float32)
    c["bones"] = bd(np.ones((64, 64), np.float32)).astype(bf)
    return c


CONST_SPECS = [("ident", [128, 128], BF16), ("rope", [SEQ, 4, 64], F32), ("wamask", [128, 384], BF16),
               ("mkf", [128, 640], F32), ("mkb", [128, 640], F32), ("bones", [128, 128], BF16)]

W_SPECS = [
    ("ffn1_norm", [4, 1024]), ("ffn1_w_up", [4, 1024, 5632]), ("ffn1_w_down", [4, 2816, 1024]),
    ("mix_norm", [4, 1024]), ("w_in", [4, 1024, 6528]), ("rwkv_mu", [4, 1920]), ("rwkv_w0", [4, 2, 512]),
    ("rwkv_w2", [4, 2, 64, 512]), ("rwkv_a0", [4, 2, 512]), ("rwkv_a2", [4, 2, 64, 512]),
    ("rwkv_g2", [4, 128, 512]), ("rwkv_k_k", [4, 512]), ("rwkv_k_a", [4, 512]), ("rwkv_r_k", [4, 8, 64]),
    ("rwkv_lnx_g", [4, 512]), ("rwkv_lnx_b", [4, 512]), ("ga_q_norm", [4, 64]), ("ga_k_norm", [4, 64]),
    ("wa_sink", [4, 8]), ("w_o_rwkv", [4, 512, 1024]), ("w_o_ga", [4, 512, 1024]), ("w_o_wa", [4, 512, 1024]),
    ("w_out", [4, 1024, 1024]), ("ffn2_norm", [4, 1024]), ("ffn2_w_up", [4, 1024, 5632]),
    ("ffn2_w_down", [4, 2816, 1024]), ("final_norm", [1024]),
]


def declare_io(B, ntok):
    W = {name: B.dram_in(name, shp) for name, shp in W_SPECS}
    C = {name: B.dram_in("c_" + name, shp, dt) for name, shp, dt in CONST_SPECS}
    return W, C


def declare_scratch(B):
    scr = {}
    scr["hT"] = B.dram_tmp("scr_hT", [128, 8, SEQ], BF16)
    scr["psh"] = B.dram_tmp("scr_psh", [15, 128, SEQ], F32)
    scr["qkt"] = B.dram_tmp("scr_qkt", [2, 128, 5, SEQ], BF16)
    scr["v"] = B.dram_tmp("scr_v", [2, SEQ, 128], BF16)
    scr["yT"] = B.dram_tmp("scr_yT", [3, 128, 4, SEQ], BF16)
    return scr


def load_colvec(B, name, vec_d, ncol, q="sp"):
    t = B.sb(name, [128, ncol], F32)
    B.S.dma(q, t[:], vec_d.rearrange("(c p) -> p c", p=128), w=[(name,)], allow_slow_non_contiguous=True)
    return t


def mixnorm_phase(B, x_d, tok0, gvec_d, C, scr):
    S = B.S
    B.begin()
    load_consts(B, C["ident"])
    gT = load_colvec(B, "g_mix", gvec_d, 8)
    hTm = B.sb("hTm", [128, 8, SEQ], BF16)
    norm_to_featmajor(B, x_d, "g_mix", gT, tok0, SEQ // 128, hTm, "hTm", "xres")
    S.dma("sp", scr["hT"], hTm[:], r=[("hTm", t) for t in range(16)], w=[("scr_hT",)])
    B.end()


def rwkvproj_phase(B, win_d, mu_d, scr):
    S = B.S
    B.begin()
    hTm = B.sb("hTm", [128, 8, SEQ], BF16)
    S.dma("sp", hTm[:], scr["hT"], w=[("hTm",)])
    muT = load_colvec(B, "muT", mu_d, 15)
    hmu = B.sb("hmu", [128, 15], F32)
    omm = B.sb("omm", [128, 15], F32)
    S.op("dve", lambda e: e.tensor_scalar(out=hmu[:], in0=muT[:], scalar1=0.5, scalar2=None, op0=ALU.mult),
         r=[("muT",)], w=[("hmu",)])
    S.op("dve", lambda e: e.tensor_scalar(out=omm[:], in0=muT[:], scalar1=-1.0, scalar2=1.0,
                                          op0=ALU.mult, op1=ALU.add), r=[("muT",)], w=[("omm",)])
    wsl = B.sb("wsl", [128, 2, 8, 256], BF16)
    praw = B.sb("praw", [128, 2, SEQ + 2], F32)
    nb = B.sb("nb", [128, 2, SEQ], F32)
    psh = B.sb("psh", [128, 2, SEQ], F32)
    pr = [B.ps("pr0", [128, 512]), B.ps("pr1", [128, 512])]
    win_v = win_d.rearrange("(kc p) n -> p kc n", p=128)
    for sl in range(2):
        S.op("pool", lambda e, sl=sl: e.memset(praw[:, sl, :], 0.0), w=[("praw", sl)])
    ei = 0
    for cg in range(8):
        ncol = 256 if cg < 7 else 128
        ws = cg % 2
        S.dma("pool", wsl[:, ws, :, 0:ncol], win_v[:, :, cg * 256:cg * 256 + ncol], w=[("wsl", ws)])
        for cc in range(ncol // 128):
            ch = cg * 2 + cc
            sl = ch % 2
            for tb in range(4):
                e2 = ei % 2
                ei += 1
                for kc in range(8):
                    S.op("pe", lambda e, ws=ws, cc=cc, kc=kc, tb=tb, e2=e2: e.matmul(
                        pr[e2][:], lhsT=wsl[:, ws, kc, cc * 128:(cc + 1) * 128],
                        rhs=hTm[:, kc, tb * 512:(tb + 1) * 512], start=(kc == 0), stop=(kc == 7)),
                        r=[("wsl", ws), ("hTm",)], w=[("pr", e2)])
                S.op("act", lambda e, sl=sl, tb=tb, e2=e2: e.activation(
                    out=praw[:, sl, 1 + tb * 512:1 + (tb + 1) * 512], in_=pr[e2][:], func=AF.Copy),
                    r=[("pr", e2)], w=[("praw", sl)])
            S.op("pool", lambda e, sl=sl: e.tensor_tensor(
                out=nb[:, sl, :], in0=praw[:, sl, 0:SEQ], in1=praw[:, sl, 2:SEQ + 2], op=ALU.add),
                r=[("praw", sl)], w=[("nb", sl)])
            S.op("dve", lambda e, ch=ch, sl=sl: e.tensor_scalar(
                out=nb[:, sl, :], in0=nb[:, sl, :], scalar1=hmu[:, ch:ch + 1], scalar2=None, op0=ALU.mult),
                r=[("nb", sl), ("hmu",)], w=[("nb", sl)])
            S.op("dve", lambda e, sl=sl, ch=ch: e.scalar_tensor_tensor(
                out=psh[:, sl, :], in0=praw[:, sl, 1:SEQ + 1], scalar=omm[:, ch:ch + 1], in1=nb[:, sl, :],
                op0=ALU.mult, op1=ALU.add), r=[("praw", sl), ("nb", sl), ("omm",)], w=[("psh", sl)])
            S.dma("sp", scr["psh"][ch], psh[:, sl, :], r=[("psh", sl)], w=[("scr_psh", ch)])
    B.end()


def qkvproj_phase(B, win_d, gq_d, gk_d, C, scr):
    S = B.S
    B.begin()
    load_consts(B, C["ident"])
    ident = B.T["ident"]
    m05 = B.T["m05"]
    hTm = B.sb("hTm", [128, 8, SEQ], BF16)
    S.dma("sp", hTm[:], scr["hT"], w=[("hTm",)])
    win_v = win_d.rearrange("(kc p) n -> p kc n", p=128)
    wqkv = B.sb("wqkv", [128, 2, 8, 768], BF16)
    for ty in range(2):
        c0 = 1920 + ty * 768
        for half in range(2):
            S.dma("pool", wqkv[:, ty, half * 4:(half + 1) * 4, :], win_v[:, half * 4:(half + 1) * 4, c0:c0 + 768],
                  w=[("wqkv", ty, half)])
    g2 = B.sb("g2", [128, 2, 64], F32)
    S.dma("sp", g2[:, 0, :], gq_d.partition_broadcast(128), w=[("g2", 0)])
    S.dma("sp", g2[:, 1, :], gk_d.partition_broadcast(128), w=[("g2", 1)])
    gqk = B.sb("gqk", [128, 10, 64], F32)
    S.op("dve", lambda e: e.tensor_copy(out=gqk[:, 0:8, :], in_=g2[:, 0:1, :].to_broadcast([128, 8, 64])),
         r=[("g2", 0)], w=[("gqk", 0)])
    S.op("dve", lambda e: e.tensor_copy(out=gqk[:, 8:10, :], in_=g2[:, 1:2, :].to_broadcast([128, 2, 64])),
         r=[("g2", 1)], w=[("gqk", 1)])
    NW = 3
    rope = B.sb("rope", [128, 2, 4, 64], F32)
    sq = B.sb("sq", [128, NW, 640], F32)
    ssq = B.sb("ssq", [128, NW, 10], F32)
    qn = B.sb("qn", [128, NW, 640], F32)
    ra = B.sb("ra", [128, NW, 640], F32)
    rb = B.sb("rb", [128, NW, 640], F32)
    qrp = B.sb("qrp", [128, NW, 640], BF16)
    qkt = B.sb("qkt", [128, 2, 5, SEQ], BF16)
    vtok = B.sb("vtok", [128, 2, 16, 128], BF16)
    psA = [B.ps("psA%d" % i, [128, 512]) for i in range(NW)]
    psB = [B.ps("psB%d" % i, [128, 512]) for i in range(NW)]
    ptr = [B.ps("ptr0", [128, 1024], BF16), B.ps("ptr1", [128, 1024], BF16)]
    rope_v = C["rope"].rearrange("(t p) a d -> t p a d", p=128)

    def v3(ap):
        return ap.rearrange("p (h d) -> p h d", d=64)

    def qk_iter(it):
        t, ty = it // 2, it % 2
        sl = it % NW
        rs = t % 2
        p2 = it % 2
        if ty == 0:
            S.dma("sp", rope[:, rs, :, :], rope_v[t], w=[("rope", rs)])
        for kc in range(8):
            S.op("pe", lambda e, kc=kc: e.matmul(
                psA[sl][:], lhsT=hTm[:, kc, t * 128:(t + 1) * 128], rhs=wqkv[:, ty, kc, 0:512],
                start=(kc == 0), stop=(kc == 7)), r=[("hTm",), ("wqkv", ty, kc // 4)], w=[("psA", sl)])
        for kc in range(8):
            S.op("pe", lambda e, kc=kc: e.matmul(
                psB[sl][:, 0:256], lhsT=hTm[:, kc, t * 128:(t + 1) * 128], rhs=wqkv[:, ty, kc, 512:768],
                start=(kc == 0), stop=(kc == 7)), r=[("hTm",), ("wqkv", ty, kc // 4)], w=[("psB", sl)])
        yield
        S.op("act", lambda e: e.activation(out=vtok[:, ty, t, :], in_=psB[sl][:, 128:256], func=AF.Copy),
             r=[("psB", sl)], w=[("vtok", ty, t)])
        if ty == 0:
            S.op("act", lambda e: e.activation(out=sq[:, sl, 0:512], in_=psA[sl][:], func=AF.Square),
                 r=[("psA", sl)], w=[("sq", sl)])
            S.op("act", lambda e: e.activation(out=sq[:, sl, 512:640], in_=psB[sl][:, 0:128], func=AF.Square),
                 r=[("psB", sl)], w=[("sq", sl)])
            yield
            S.op("dve", lambda e: e.tensor_reduce(out=ssq[:, sl, :], in_=v3(sq[:, sl, :]), axis=AX.X, op=ALU.add),
                 r=[("sq", sl)], w=[("ssq", sl)])
            S.op("dve", lambda e: e.tensor_scalar(out=ssq[:, sl, :], in0=ssq[:, sl, :], scalar1=1.0 / 64, scalar2=1e-6,
                                                  op0=ALU.mult, op1=ALU.add), r=[("ssq", sl)], w=[("ssq", sl)])
            yield
            S.op("pool", lambda e: e.tensor_tensor(out=ssq[:, sl, :], in0=ssq[:, sl, :],
                                                   in1=m05[:, 0:1].to_broadcast([128, 10]), op=ALU.pow),
                 r=[("ssq", sl), ("m05",)], w=[("ssq", sl)])
            yield
            S.op("dve", lambda e: e.tensor_tensor(
                out=v3(qn[:, sl, 0:512]), in0=v3(psA[sl][:]),
                in1=ssq[:, sl, 0:8].unsqueeze(2).to_broadcast([128, 8, 64]), op=ALU.mult),
                r=[("psA", sl), ("ssq", sl)], w=[("qn", sl)])
            S.op("dve", lambda e: e.tensor_tensor(
                out=v3(qn[:, sl, 512:640]), in0=v3(psB[sl][:, 0:128]),
                in1=ssq[:, sl, 8:10].unsqueeze(2).to_broadcast([128, 2, 64]), op=ALU.mult),
                r=[("psB", sl), ("ssq", sl)], w=[("qn", sl)])
            yield
            S.op("pool", lambda e: e.tensor_tensor(
                out=qn[:, sl, :], in0=qn[:, sl, :], in1=gqk[:].rearrange("p h d -> p (h d)"), op=ALU.mult),
                r=[("qn", sl), ("gqk", 0), ("gqk", 1)], w=[("qn", sl)])
            yield
        else:
            S.op("act", lambda e: e.activation(out=qn[:, sl, 0:512], in_=psA[sl][:], func=AF.Copy),
                 r=[("psA", sl)], w=[("qn", sl)])
            S.op("act", lambda e: e.activation(out=qn[:, sl, 512:640], in_=psB[sl][:, 0:128], func=AF.Copy),
                 r=[("psB", sl)], w=[("qn", sl)])
            yield
        ct = rope[:, rs, 2 * ty, :]
        st_ = rope[:, rs, 2 * ty + 1, :]
        S.op("dve", lambda e: e.tensor_tensor(
            out=v3(ra[:, sl, :]), in0=v3(qn[:, sl, :]), in1=ct.unsqueeze(1).to_broadcast([128, 10, 64]), op=ALU.mult),
            r=[("qn", sl), ("rope", rs)], w=[("ra", sl)])
        hb = 16 if ty == 0 else 32
        nbk = 64 // (2 * hb)
        for hf in range(2):
            S.op("pool", lambda e, hf=hf: e.tensor_tensor(
                out=rb[:, sl, :].rearrange("p (h b f d) -> p h b f d", h=10, b=nbk, f=2)[:, :, :, hf, :],
                in0=qn[:, sl, :].rearrange("p (h b f d) -> p h b f d", h=10, b=nbk, f=2)[:, :, :, 1 - hf, :],
                in1=st_.rearrange("p (b f d) -> p b f d", b=nbk, f=2)[:, :, hf, :].unsqueeze(1)
                .to_broadcast([128, 10, nbk, hb]), op=ALU.mult),
                r=[("qn", sl), ("rope", rs)], w=[("rb", sl)])
        yield
        S.op("dve", lambda e: e.tensor_tensor(
            out=qrp[:, sl, 0:512].rearrange("p (j g d) -> p g j d", j=4, g=2),
            in0=ra[:, sl, 0:512].rearrange("p (g j d) -> p g j d", j=4, g=2),
            in1=rb[:, sl, 0:512].rearrange("p (g j d) -> p g j d", j=4, g=2), op=ALU.add),
            r=[("ra", sl), ("rb", sl)], w=[("qrp", sl)])
        S.op("dve", lambda e: e.tensor_tensor(
            out=qrp[:, sl, 512:640], in0=ra[:, sl, 512:640], in1=rb[:, sl, 512:640], op=ALU.add),
            r=[("ra", sl), ("rb", sl)], w=[("qrp", sl)])
        yield
        for j in range(5):
            S.op("pe", lambda e, j=j: e.transpose(
                out=ptr[p2][:, j * 128:(j + 1) * 128], in_=qrp[:, sl, j * 128:(j + 1) * 128], identity=ident[:]),
                r=[("qrp", sl), ("ident",)], w=[("ptr", p2)])
        yield
        S.op("act", lambda e: e.activation(
            out=qkt[:, ty, :, t * 128:(t + 1) * 128], in_=ptr[p2][:, 0:640].rearrange("p (c n) -> p c n", c=5),
            func=AF.Copy), r=[("ptr", p2)], w=[("qkt", ty, t)])
        yield

    live = []
    nxt = 0
    while live or nxt < 32:
        if nxt < 32 and len(live) < NW:
            live.append(qk_iter(nxt))
            nxt += 1
        keep = []
        for g in live:
            try:
                next(g)
                keep.append(g)
            except StopIteration:
                pass
        live = keep
    for ty in range(2):
        S.dma("sp", scr["qkt"][ty], qkt[:, ty, :, :], r=[("qkt", ty, t) for t in range(16)], w=[("scr_qkt", ty)])
        S.dma("sp", scr["v"][ty].rearrange("(t p) n -> p t n", p=128), vtok[:, ty, :, :],
              r=[("vtok", ty, t) for t in range(16)], w=[("scr_v", ty)])
    B.end()


def attn_phase(B, sink_d, C, scr):
    S = B.S
    B.begin()
    NE = 6
    QT = B.sb("QT", [128, 5, SEQ], BF16)
    Vtok = B.sb("Vtok", [128, 16, 128], BF16)
    Vz = B.sb("Vz", [128, 16, 2, 128], BF16)
    O01 = B.sb("O01", [128, 2, 128], BF16)
    yT = B.sb("yT", [128, 4, SEQ], BF16)
    E = B.sb("E", [128, NE, 512], BF16)
    Em = B.sb("Em", [128, 8, 384], BF16)
    rden = B.sb("rden", [128, 2, 512], F32)
    wam = B.sb("wam", [128, 384], BF16)
    esink = B.sb("esink", [128, 4], F32)
    st = [B.ps("st%d" % i, [128, 512]) for i in range(4)]
    pn = [B.ps("pn0", [128, 512]), B.ps("pn1", [128, 512])]
    pdn = [B.ps("pdn0", [128, 512]), B.ps("pdn1", [128, 512])]
    S.dma("sp", wam[:], C["wamask"], w=[("wam",)])
    S.dma("sp", esink[0:64, :], sink_d[0:4].partition_broadcast(64), w=[("esink", 0)])
    S.dma("sp", esink[64:128, :], sink_d[4:8].partition_broadcast(64), w=[("esink", 1)])
    S.op("act", lambda e: e.activation(out=esink[:], in_=esink[:], func=AF.Exp),
         r=[("esink", 0), ("esink", 1)], w=[("esink", 0), ("esink", 1)])
    S.op("pool", lambda e: e.memset(O01[:], 0.0), w=[("O01",)])
    S.op("pool", lambda e: e.memset(O01[:, 0, 0:64], 1.0), w=[("O01",)])
    S.op("pool", lambda e: e.memset(O01[:, 1, 64:128], 1.0), w=[("O01",)])
    S.op("pool", lambda e: e.memset(Vz[:], 0.0), w=[("Vz",)])
    cnt = {"s": 0, "e": 0, "g": 0}
    for ty in range(2):
        S.dma("sp", QT[:], scr["qkt"][ty], w=[("QT",)])
        S.dma("sp", Vtok[:], scr["v"][ty].rearrange("(t p) n -> p t n", p=128), w=[("Vtok",)])
        S.op("pool", lambda e: e.tensor_copy(out=Vz[:, :, 0, 0:64], in_=Vtok[:, :, 0:64]),
             r=[("Vtok",)], w=[("Vz",)])
        S.op("pool", lambda e: e.tensor_copy(out=Vz[:, :, 1, 64:128], in_=Vtok[:, :, 64:128]),
             r=[("Vtok",)], w=[("Vz",)])
        if ty == 0:
            its = [(j, qc, kb, g) for j in range(4) for qc in range(4) for kb in range(16) for g in range(2)]
            LA = 3
            slots = {}

            def emit_s(i):
                j, qc, kb, g = its[i]
                s2 = cnt["s"] % 4
                cnt["s"] += 1
                es = cnt["e"] % NE
                cnt["e"] += 1
                slots[i] = es
                S.op("pe", lambda e: e.matmul(
                    st[s2][:], lhsT=QT[g * 64:(g + 1) * 64, 4, kb * 128:(kb + 1) * 128],
                    rhs=QT[g * 64:(g + 1) * 64, j, qc * 512:(qc + 1) * 512], start=True, stop=True),
                    r=[("QT",)], w=[("st", s2)])
                S.op("act", lambda e: e.activation(out=E[:, es, :], in_=st[s2][:], func=AF.Exp, scale=0.125),
                     r=[("st", s2)], w=[("E", es)])

            def emit_pv(i):
                j, qc, kb, g = its[i]
                es = slots.pop(i)
                first = (kb == 0 and g == 0)
                last = (kb == 15 and g == 1)
                if first:
                    cnt["g"] += 1
                g2 = cnt["g"] % 2
                S.op("pe", lambda e: e.matmul(pn[g2][:], lhsT=Vz[:, kb, g, :], rhs=E[:, es, :], start=first, stop=last),
                     r=[("Vz",), ("E", es)], w=[("pn", g2)])
                S.op("pe", lambda e: e.matmul(pdn[g2][:], lhsT=O01[:, g, :], rhs=E[:, es, :], start=first, stop=last),
                     r=[("O01",), ("E", es)], w=[("pdn", g2)])
                if last:
                    S.op("dve", lambda e: e.reciprocal(out=rden[:, g2, :], in_=pdn[g2][:]),
                         r=[("pdn", g2)], w=[("rden", g2)])
                    S.op("dve", lambda e: e.tensor_tensor(
                        out=yT[:, j, qc * 512:(qc + 1) * 512], in0=pn[g2][:], in1=rden[:, g2, :], op=ALU.mult),
                        r=[("pn", g2), ("rden", g2)], w=[("yT", j)])

            nb = len(its) // 2
            for b in range(nb + 1):
                if b < nb:
                    emit_s(2 * b + 1)
                    emit_s(2 * b)
                if b - 1 >= 0:
                    bb = b - 1
                    if its[2 * bb + 1][2] == 15 and its[2 * bb + 1][3] == 1:
                        emit_pv(2 * bb)
                        emit_pv(2 * bb + 1)
                    elif its[2 * bb][2] == 0 and its[2 * bb][3] == 0:
                        emit_pv(2 * bb)
                        emit_pv(2 * bb + 1)
                    else:
                        emit_pv(2 * bb + 1)
                        emit_pv(2 * bb)
        else:
            for j in range(4):
                emslot = {}

                def make_em(kb, j=j):
                    lo = max(kb - 1, 0)
                    hi = min(kb + 1, 15)
                    n = (hi - lo + 1) * 128
                    moff = 128 if kb == 0 else 0
                    for g in range(2):
                        s2 = cnt["s"] % 4
                        cnt["s"] += 1
                        es = cnt["e"] % NE
                        cnt["e"] += 1
                        ms = (kb % 4) * 2 + g
                        S.op("pe", lambda e, g=g, s2=s2: e.matmul(
                            st[s2][:, 0:n], lhsT=QT[g * 64:(g + 1) * 64, 4, kb * 128:(kb + 1) * 128],
                            rhs=QT[g * 64:(g + 1) * 64, j, lo * 128:lo * 128 + n], start=True, stop=True),
                            r=[("QT",)], w=[("st", s2)])
                        S.op("act", lambda e, s2=s2, es=es: e.activation(
                            out=E[:, es, 0:n], in_=st[s2][:, 0:n], func=AF.Exp, scale=0.125),
                            r=[("st", s2)], w=[("E", es)])
                        S.op("dve", lambda e, es=es, ms=ms: e.tensor_tensor(
                            out=Em[:, ms, 0:n], in0=E[:, es, 0:n], in1=wam[:, moff:moff + n], op=ALU.mult),
                            r=[("E", es), ("wam",)], w=[("Em", ms)])
                    emslot[kb] = lo

                make_em(0)
                make_em(1)
                for qg in range(4):
                    cnt["g"] += 1
                    g2 = cnt["g"] % 2
                    for qq in range(4):
                        qb = qg * 4 + qq
                        if qb + 2 <= 15:
                            make_em(qb + 2)
                        kbs = [kb for kb in (qb - 1, qb, qb + 1) if 0 <= kb <= 15]
                        nmm = len(kbs) * 2
                        for (use_v, pst, nm) in ((True, pn, "pn"), (False, pdn, "pdn")):
                            ii = 0
                            for kb in kbs:
                                co = (qb - emslot[kb]) * 128
                                for g in range(2):
                                    ms = (kb % 4) * 2 + g
                                    lt = Vz[:, kb, g, :] if use_v else O01[:, g, :]
                                    S.op("pe", lambda e, lt=lt, ms=ms, co=co, pst=pst, g2=g2, qq=qq, ii=ii, nmm=nmm:
                                         e.matmul(pst[g2][:, qq * 128:(qq + 1) * 128], lhsT=lt,
                                                  rhs=Em[:, ms, co:co + 128], start=(ii == 0), stop=(ii == nmm - 1)),
                                         r=[("Vz",), ("O01",), ("Em", ms)], w=[(nm, g2)])
                                    ii += 1
                    S.op("dve", lambda e, g2=g2, j=j: e.tensor_scalar(
                        out=rden[:, g2, :], in0=pdn[g2][:], scalar1=esink[:, j:j + 1], scalar2=None, op0=ALU.add),
                        r=[("pdn", g2), ("esink", 0), ("esink", 1)], w=[("rden", g2)])
                    S.op("dve", lambda e, g2=g2: e.reciprocal(out=rden[:, g2, :], in_=rden[:, g2, :]),
                         r=[("rden", g2)], w=[("rden", g2)])
                    S.op("dve", lambda e, g2=g2, j=j, qg=qg: e.tensor_tensor(
                        out=yT[:, j, qg * 512:(qg + 1) * 512], in0=pn[g2][:], in1=rden[:, g2, :], op=ALU.mult),
                        r=[("pn", g2), ("rden", g2)], w=[("yT", j)])
        S.dma("sp", scr["yT"][ty + 1], yT[:], r=[("yT", j) for j in range(4)], w=[("scr_yT", ty + 1)])
    B.end()


def merge_phase(B, x_d, tok0, win_d, wo_ds, wout_d, scr, x_out=None):
    S = B.S
    if x_out is None:
        x_out = x_d
    B.begin()
    wg = B.sb("wg", [128, 8, 3072], BF16)
    wo = B.sb("wo", [128, 3, 4, D], BF16)
    wout = B.sb("wout", [128, 8, D], BF16)
    hTc = B.sb("hTc", [128, 2, 8, 512], BF16)
    y3 = B.sb("y3", [128, 2, 3, 4, 512], BF16)
    sgm = B.sb("sgm", [128, 2, 512], F32)
    acc = B.sb("acc", [128, 8, 512], F32)
    tmp = B.sb("tmp", [128, 2, 512], F32)
    mT = B.sb("mT", [128, 2, 8, 512], BF16)
    xr = B.sb("xr", [128, 2, D], F32)
    xo = B.sb("xo", [128, 2, D], F32)
    pg = [B.ps("pg0", [128, 512]), B.ps("pg1", [128, 512])]
    pb = [B.ps("pb0", [128, 512]), B.ps("pb1", [128, 512])]
    po = [B.ps("po0", [128, 512]), B.ps("po1", [128, 512])]
    win_v = win_d.rearrange("(kc p) n -> p kc n", p=128)
    S.dma("sp", hTc[:, 0, :, :], scr["hT"][:, :, 0:512], w=[("hTc", 0)])
    for i in range(3):
        S.dma("sp", y3[:, 0, i, :, :], scr["yT"][i][:, :, 0:512], w=[("y3", 0, i)])
    for i in range(3):
        if i == 0:
            S.dma("pool", wo[:, 0, :, :], wo_ds[0].rearrange("(c p) n -> p c n", p=128), w=[("wo", 0)])
        else:
            wv = wo_ds[i].rearrange("(g j d) n -> g d j n", g=2, j=4)
            for g in range(2):
                S.dma("pool", wo[g * 64:(g + 1) * 64, i, :, :], wv[g], w=[("wo", i)])
        for hf in range(2):
            c0 = i * 1024 + hf * 512
            S.dma("pool", wg[:, :, c0:c0 + 512], win_v[:, :, 3456 + c0:3456 + c0 + 512], w=[("wg", i, hf)])
    S.dma("pool", wout[:], wout_d.rearrange("(c p) n -> p c n", p=128), w=[("wout",)])
    ei = 0
    for tb in range(4):
        cs = tb % 2
        if tb > 0:
            S.dma("sp", hTc[:, cs, :, :], scr["hT"][:, :, tb * 512:(tb + 1) * 512], w=[("hTc", cs)])
            for i in range(3):
                S.dma("sp", y3[:, cs, i, :, :], scr["yT"][i][:, :, tb * 512:(tb + 1) * 512], w=[("y3", cs, i)])
        for i in range(3):
            for oc in range(8):
                e2 = ei % 2
                ei += 1
                for kc in range(8):
                    S.op("pe", lambda e, kc=kc, i=i, oc=oc, cs=cs, e2=e2: e.matmul(
                        pg[e2][:], lhsT=wg[:, kc, i * 1024 + oc * 128:i * 1024 + (oc + 1) * 128],
                        rhs=hTc[:, cs, kc, :], start=(kc == 0), stop=(kc == 7)),
                        r=[("wg", i, oc // 4), ("hTc", cs)], w=[("pg", e2)])
                for c in range(4):
                    S.op("pe", lambda e, c=c, i=i, oc=oc, cs=cs, e2=e2: e.matmul(
                        pb[e2][:], lhsT=wo[:, i, c, oc * 128:(oc + 1) * 128], rhs=y3[:, cs, i, c, :],
                        start=(c == 0), stop=(c == 3)),
                        r=[("wo", i), ("y3", cs, i)], w=[("pb", e2)])
                S.op("act", lambda e, e2=e2: e.activation(out=sgm[:, e2, :], in_=pg[e2][:], func=AF.Sigmoid),
                     r=[("pg", e2)], w=[("sgm", e2)])
                if i == 0:
                    S.op("dve", lambda e, e2=e2, oc=oc: e.tensor_tensor(
                        out=acc[:, oc, :], in0=pb[e2][:], in1=sgm[:, e2, :], op=ALU.mult),
                        r=[("pb", e2), ("sgm", e2)], w=[("acc", oc)])
                elif i == 1:
                    S.op("dve", lambda e, e2=e2: e.tensor_tensor(
                        out=tmp[:, e2, :], in0=pb[e2][:], in1=sgm[:, e2, :], op=ALU.mult),
                        r=[("pb", e2), ("sgm", e2)], w=[("tmp", e2)])
                    S.op("pool", lambda e, e2=e2, oc=oc: e.tensor_tensor(
                        out=acc[:, oc, :], in0=acc[:, oc, :], in1=tmp[:, e2, :], op=ALU.add),
                        r=[("acc", oc), ("tmp", e2)], w=[("acc", oc)])
                else:
                    S.op("dve", lambda e, e2=e2: e.tensor_tensor(
                        out=tmp[:, e2, :], in0=pb[e2][:], in1=sgm[:, e2, :], op=ALU.mult),
                        r=[("pb", e2), ("sgm", e2)], w=[("tmp", e2)])
                    S.op("pool", lambda e, e2=e2, cs=cs, oc=oc: e.tensor_tensor(
                        out=mT[:, cs, oc, :], in0=acc[:, oc, :], in1=tmp[:, e2, :], op=ALU.add),
                        r=[("acc", oc), ("tmp", e2)], w=[("mT", cs, oc)])
        for tt in range(4):
            r0 = tok0 + tb * 512 + tt * 128
            xs = tt % 2
            S.dma("sp", xr[:, xs, :], x_d[r0:r0 + 128, :], r=[("xres", "x", r0)], w=[("xr", xs)])
            for nh in range(2):
                e2 = ei % 2
                ei += 1
                for oc in range(8):
                    S.op("pe", lambda e, oc=oc, tt=tt, nh=nh, cs=cs, e2=e2: e.matmul(
                        po[e2][:], lhsT=mT[:, cs, oc, tt * 128:(tt + 1) * 128],
                        rhs=wout[:, oc, nh * 512:(nh + 1) * 512], start=(oc == 0), stop=(oc == 7)),
                        r=[("mT", cs, oc), ("wout",)], w=[("po", e2)])
                S.op("dve", lambda e, e2=e2, xs=xs, nh=nh: e.tensor_tensor(
                    out=xo[:, xs, nh * 512:(nh + 1) * 512], in0=po[e2][:], in1=xr[:, xs, nh * 512:(nh + 1) * 512],
                    op=ALU.add), r=[("po", e2), ("xr", xs)], w=[("xo", xs)])
            S.dma("sp", x_out[r0:r0 + 128, :], xo[:, xs, :], r=[("xo", xs)], w=[("xres", "x", r0)])
    B.end()


def declare_rwkv_scratch(B, scr):
    scr["AR"] = B.dram_tmp("scr_AR", [4, 2, 128, 2 * SEQ], BF16)
    scr["BK"] = B.dram_tmp("scr_BK", [4, 2, 128, 2, SEQ], BF16)
    scr["PC"] = B.dram_tmp("scr_PC", [4, 2, 128, 32], F32)
    scr["vT"] = B.dram_tmp("scr_vT", [4, 128, SEQ], BF16)
    scr["bonus"] = B.dram_tmp("scr_bonus", [4, 128, SEQ], F32)
    scr["gate"] = B.dram_tmp("scr_gate", [4, 128, SEQ], BF16)


def rwkvprep_phase(B, Wl, C, scr):
    S = B.S
    B.begin()
    bones = B.sb("bones", [128, 128], BF16)
    S.dma("sp", bones[:], C["bones"], w=[("bones",)])
    tiny = B.sb("tiny", [128, 1], F32)
    S.op("pool", lambda e: e.memset(tiny[:], 1e-24), w=[("tiny",)])
    resetF = B.sb("resetF", [128, SEQ], BF16)
    resetB = B.sb("resetB", [128, SEQ], BF16)
    S.op("pool", lambda e: e.memset(resetF[:], 1.0), w=[("resetF",)])
    S.op("pool", lambda e: e.memset(resetF[:].rearrange("p (n t) -> p n t", t=64)[:, :, 0:1], 0.0), w=[("resetF",)])
    S.op("pool", lambda e: e.memset(resetB[:], 1.0), w=[("resetB",)])
    S.op("pool", lambda e: e.memset(resetB[:].rearrange("p (n t) -> p n t", t=64)[:, :, 63:64], 0.0), w=[("resetB",)])
    w0T = load_colvec(B, "w0T", Wl["w0"].rearrange("d n -> (d n)"), 8)
    a0T = load_colvec(B, "a0T", Wl["a0"].rearrange("d n -> (d n)"), 8)
    kkT = load_colvec(B, "kkT", Wl["k_k"], 4)
    kaT = load_colvec(B, "kaT", Wl["k_a"], 4)
    rkT = load_colvec(B, "rkT", Wl["r_k"].rearrange("h d -> (h d)"), 4)
    omka = B.sb("omka", [128, 4], F32)
    tomk = B.sb("tomk", [128, 4], F32)
    S.op("dve", lambda e: e.tensor_scalar(out=omka[:], in0=kaT[:], scalar1=-1.0, scalar2=1.0, op0=ALU.mult,
                                          op1=ALU.add), r=[("kaT",)], w=[("omka",)])
    S.op("dve", lambda e: e.tensor_scalar(out=tomk[:], in0=kaT[:], scalar1=-2.0, scalar2=2.0, op0=ALU.mult,
                                          op1=ALU.add), r=[("kaT",)], w=[("tomk",)])
    w2sb = B.sb("w2sb", [128, 512], BF16)
    a2sb = B.sb("a2sb", [128, 512], BF16)
    g2sb = B.sb("g2sb", [128, 512], BF16)
    S.dma("pool", w2sb[:], Wl["w2"].rearrange("d k n -> (d k) n"), w=[("w2sb",)])
    S.dma("pool", a2sb[:], Wl["a2"].rearrange("d k n -> (d k) n"), w=[("a2sb",)])
    S.dma("pool", g2sb[:], Wl["g2"], w=[("g2sb",)])
    ld = B.sb("ld", [128, SEQ], F32)
    twl = B.sb("twl", [128, SEQ], BF16)
    alb = B.sb("alb", [128, SEQ], BF16)
    sgl = B.sb("sgl", [128, SEQ], BF16)
    for (ch, dst, fn, nm) in ((12, twl, AF.Tanh, "twl"), (13, alb, AF.Copy, "alb"), (14, sgl, AF.Sigmoid, "sgl")):
        S.dma("sp", ld[:], scr["psh"][ch], r=[("scr_psh", ch)], w=[("ld",)])
        S.op("act", lambda e, dst=dst, fn=fn: e.activation(out=dst[:], in_=ld[:], func=fn),
             r=[("ld",)], w=[(nm,)])
    rkv = B.sb("rkv", [128, 2, 3, SEQ], F32)

    def load_rkv(cc):
        p_ = cc % 2
        for i_, nm in enumerate(("r_", "k_", "v_")):
            S.dma("sp", rkv[:, p_, i_, :], scr["psh"][4 * i_ + cc], r=[("scr_psh", 4 * i_ + cc)], w=[(nm, p_)])
    kk = B.sb("kk", [128, SEQ], F32)
    sqb = B.sb("sqb", [128, SEQ], BF16)
    vbf = B.sb("vbf", [128, SEQ], BF16)
    gbf = B.sb("gbf", [128, SEQ], BF16)
    a_ = B.sb("a_", [128, 2, SEQ], F32)
    sig = B.sb("sig", [128, SEQ], F32)
    cs = B.sb("cs", [128, SEQ], F32)
    T_ = B.sb("T_", [128, SEQ], F32)
    X = B.sb("X", [128, 2, SEQ], F32)
    bd = B.sb("bd", [128, SEQ], F32)
    kd = B.sb("kd", [128, SEQ], F32)
    ARo = B.sb("ARo", [128, 32, 2, 64], BF16)
    BKo = B.sb("BKo", [128, 2, SEQ], BF16)
    pc = B.sb("pc", [128, 32], F32)
    pp = [B.ps("pp0", [128, 512]), B.ps("pp1", [128, 512])]
    pi = [0]

    def mm4(lhsT, rhs_fn, evac, rkeys):
        for tb in range(4):
            b2 = pi[0] % 2
            pi[0] += 1
            S.op("pe", lambda e, tb=tb, b2=b2: e.matmul(pp[b2][:], lhsT=lhsT, rhs=rhs_fn(tb), start=True, stop=True),
                 r=rkeys, w=[("pp", b2)])
            evac(tb, pp[b2], ("pp", b2))

    def v3(ap):
        return ap.rearrange("p (n t) -> p n t", t=64)

    def do_pair(c, r_, k_, v_):
        cpar = c % 2
        cc = slice(c * 128, (c + 1) * 128)
        if c == 0:
            load_rkv(0)
        if c + 1 < 4:
            load_rkv(c + 1)
        S.op("act", lambda e: e.activation(out=vbf[:], in_=v_[:], func=AF.Copy), r=[("v_", cpar)], w=[("vbf",)])
        S.dma("sp", scr["vT"][c], vbf[:], r=[("vbf",)], w=[("scr_vT", c)])
        S.op("dve", lambda e, c=c: e.tensor_scalar(out=kk[:], in0=k_[:], scalar1=kkT[:, c:c + 1], scalar2=None,
                                                   op0=ALU.mult), r=[("k_", cpar), ("kkT",)], w=[("kk",)])
        S.op("act", lambda e: e.activation(out=sqb[:], in_=kk[:], func=AF.Square), r=[("kk",)], w=[("sqb",)])
        mm4(bones[:], lambda tb: sqb[:, tb * 512:(tb + 1) * 512],
            lambda tb, ps, bk: S.op("act", lambda e, tb=tb, ps=ps: e.activation(
                out=X[:, 0, tb * 512:(tb + 1) * 512], in_=ps[:], func=AF.Ln, bias=tiny[:, 0:1]),
                r=[bk, ("tiny",)], w=[("X", 0)]), [("bones",), ("sqb",)])
        S.op("act", lambda e: e.activation(out=X[:, 0, :], in_=X[:, 0, :], func=AF.Exp, scale=-0.5),
             r=[("X", 0)], w=[("X", 0)])
        S.op("dve", lambda e: e.tensor_tensor(out=kk[:], in0=kk[:], in1=X[:, 0, :], op=ALU.mult),
             r=[("kk",), ("X", 0)], w=[("kk",)])
        for d in range(2):
            dd = slice(d * 64, (d + 1) * 64)
            mm4(a2sb[dd, cc], lambda tb, dd=dd: alb[dd, tb * 512:(tb + 1) * 512],
                lambda tb, ps, bk, d=d, c=c: S.op("act", lambda e, tb=tb, ps=ps: e.activation(
                    out=a_[:, d, tb * 512:(tb + 1) * 512], in_=ps[:], func=AF.Sigmoid,
                    bias=a0T[:, d * 4 + c:d * 4 + c + 1]), r=[bk, ("a0T",)], w=[("a_", d)]),
                [("a2sb",), ("alb",)])
        mm4(g2sb[:, cc], lambda tb: sgl[:, tb * 512:(tb + 1) * 512],
            lambda tb, ps, bk: S.op("act", lambda e, tb=tb, ps=ps: e.activation(
                out=gbf[:, tb * 512:(tb + 1) * 512], in_=ps[:], func=AF.Copy), r=[bk], w=[("gbf",)]),
            [("g2sb",), ("sgl",)])
        S.dma("sp", scr["gate"][c], gbf[:], r=[("gbf",)], w=[("scr_gate", c)])
        S.op("dve", lambda e: e.tensor_tensor(out=T_[:], in0=a_[:, 0, :], in1=a_[:, 1, :], op=ALU.add),
             r=[("a_", 0), ("a_", 1)], w=[("T_",)])
        S.op("dve", lambda e, c=c: e.tensor_scalar(out=T_[:], in0=T_[:], scalar1=kaT[:, c:c + 1],
                                                   scalar2=tomk[:, c:c + 1], op0=ALU.mult, op1=ALU.add),
             r=[("T_",), ("kaT",), ("tomk",)], w=[("T_",)])
        S.op("dve", lambda e: e.tensor_tensor(out=T_[:], in0=T_[:], in1=r_[:], op=ALU.mult),
             r=[("T_",), ("r_", cpar)], w=[("T_",)])
        S.op("dve", lambda e: e.tensor_tensor(out=T_[:], in0=T_[:], in1=k_[:], op=ALU.mult),
             r=[("T_",), ("k_", cpar)], w=[("T_",)])
        S.op("dve", lambda e, c=c: e.tensor_scalar(out=sqb[:], in0=T_[:], scalar1=rkT[:, c:c + 1], scalar2=None,
                                                   op0=ALU.mult), r=[("T_",), ("rkT",)], w=[("sqb",)])
        mm4(bones[:], lambda tb: sqb[:, tb * 512:(tb + 1) * 512],
            lambda tb, ps, bk: S.op("dve", lambda e, tb=tb, ps=ps: e.tensor_tensor(
                out=X[:, 1, tb * 512:(tb + 1) * 512], in0=ps[:], in1=v_[:, tb * 512:(tb + 1) * 512], op=ALU.mult),
                r=[bk, ("v_", cpar)], w=[("X", 1)]), [("bones",), ("sqb",)])
        S.dma("sp", scr["bonus"][c], X[:, 1, :], r=[("X", 1)], w=[("scr_bonus", c)])
        for d in range(2):
            dd = slice(d * 64, (d + 1) * 64)
            mm4(w2sb[dd, cc], lambda tb, dd=dd: twl[dd, tb * 512:(tb + 1) * 512],
                lambda tb, ps, bk, d=d, c=c: S.op("act", lambda e, tb=tb, ps=ps: e.activation(
                    out=sig[:, tb * 512:(tb + 1) * 512], in_=ps[:], func=AF.Sigmoid,
                    bias=w0T[:, d * 4 + c:d * 4 + c + 1]), r=[bk, ("w0T",)], w=[("sig",)]),
                [("w2sb",), ("twl",)])
            if d == 0:
                S.op("dve", lambda e: e.tensor_tensor_scan(out=cs[:], data0=resetF[:], data1=sig[:], initial=0.0,
                                                           op0=ALU.mult, op1=ALU.add),
                     r=[("resetF",), ("sig",)], w=[("cs",)])
                eidx = 63
            else:
                S.op("dve", lambda e: e.tensor_tensor_scan(out=cs[:, ::-1], data0=resetB[:, ::-1],
                                                           data1=sig[:, ::-1], initial=0.0,
                                                           op0=ALU.mult, op1=ALU.add),
                     r=[("resetB",), ("sig",)], w=[("cs",)])
                eidx = 0
            S.op("act", lambda e, eidx=eidx: e.activation(out=pc[:], in_=v3(cs[:])[:, :, eidx], func=AF.Exp,
                                                          scale=-LAM), r=[("cs",)], w=[("pc",)])
            S.dma("sp", scr["PC"][c][d], pc[:], r=[("pc",)], w=[("scr_PC", c, d)])
            S.op("act", lambda e: e.activation(out=X[:, 0, :], in_=cs[:], func=AF.Exp, scale=-LAM),
                 r=[("cs",)], w=[("X", 0)])
            S.op("dve", lambda e: e.tensor_tensor(out=ARo[:, :, 1, :], in0=v3(r_[:]), in1=v3(X[:, 0, :]), op=ALU.mult),
                 r=[("r_", cpar), ("X", 0)], w=[("ARo",)])
            S.op("dve", lambda e: e.tensor_tensor(out=T_[:], in0=cs[:], in1=sig[:], op=ALU.subtract),
                 r=[("cs",), ("sig",)], w=[("T_",)])
            S.op("act", lambda e: e.activation(out=X[:, 1, :], in_=T_[:], func=AF.Exp, scale=-LAM),
                 r=[("T_",)], w=[("X", 1)])
            S.op("dve", lambda e: e.scalar_tensor_tensor(out=ARo[:, :, 0, :], in0=v3(kk[:]), scalar=-1.0,
                                                         in1=v3(X[:, 1, :]), op0=ALU.mult, op1=ALU.mult),
                 r=[("kk",), ("X", 1)], w=[("ARo",)])
            S.dma("sp", scr["AR"][c][d], ARo[:].rearrange("p n a t -> p (n a t)"), r=[("ARo",)],
                  w=[("scr_AR", c, d)])
            S.op("dve", lambda e, d=d: e.tensor_tensor(out=bd[:], in0=kk[:], in1=a_[:, d, :], op=ALU.mult),
                 r=[("kk",), ("a_", d)], w=[("bd",)])
            S.op("dve", lambda e, d=d, c=c: e.tensor_scalar(out=kd[:], in0=a_[:, d, :], scalar1=kaT[:, c:c + 1],
                                                            scalar2=omka[:, c:c + 1], op0=ALU.mult, op1=ALU.add),
                 r=[("a_", d), ("kaT",), ("omka",)], w=[("kd",)])
            S.op("pool", lambda e: e.tensor_tensor(out=kd[:], in0=kd[:], in1=k_[:], op=ALU.mult),
                 r=[("kd",), ("k_", cpar)], w=[("kd",)])
            S.op("act", lambda e: e.activation(out=X[:, 0, :], in_=cs[:], func=AF.Exp, scale=LAM),
                 r=[("cs",)], w=[("X", 0)])
            S.op("dve", lambda e: e.tensor_tensor(out=BKo[:, 0, :], in0=bd[:], in1=X[:, 0, :], op=ALU.mult),
                 r=[("bd",), ("X", 0)], w=[("BKo", 0)])
            S.op("pool", lambda e: e.tensor_tensor(out=BKo[:, 1, :], in0=kd[:], in1=X[:, 0, :], op=ALU.mult),
                 r=[("kd",), ("X", 0)], w=[("BKo", 1)])
            S.dma("sp", scr["BK"][c][d], BKo[:], r=[("BKo", i) for i in range(2)], w=[("scr_BK", c, d)])
    for c in range(4):
        do_pair(c, rkv[:, c % 2, 0, :], rkv[:, c % 2, 1, :], rkv[:, c % 2, 2, :])
    B.end()


NSLOT = 4
LNX_EPS = 64 * 1e-5


def rwkvscan_phase(B, Wl, C, scr):
    S = B.S
    B.begin()
    load_consts(B, C["ident"])
    ident = B.T["ident"]
    mk = B.sb("mk", [128, 2, 640], F32)
    S.dma("sp", mk[:, 0, :], C["mkf"], w=[("mk",)])
    S.dma("sp", mk[:, 1, :], C["mkb"], w=[("mk",)])
    m05f = B.sb("m05f", [128, 1], F32)
    S.op("pool", lambda e: e.memset(m05f[:], -0.5), w=[("m05f",)])
    lgT = load_colvec(B, "lgT", Wl["lnx_g"], 4)
    lbT = load_colvec(B, "lbT", Wl["lnx_b"], 4)
    AR2 = B.sb("AR", [128, 2, 2, 32, 2, 64], BF16)
    BK2 = B.sb("BK", [128, 2, 2, 2, SEQ], BF16)
    HP = B.sb("HP", [128, 2, 64], F32)
    PCs2 = B.sb("PCs", [128, 2, 2, 32], F32)
    vT2 = B.sb("vT", [128, 2, SEQ], BF16)
    Vst = B.sb("Vst", [128, 32, 64], BF16)
    MKs = B.sb("MKs", [128, 2, NSLOT, 640], BF16)
    TT0 = B.sb("TT0", [128, 2, NSLOT, 128], BF16)
    RB = B.sb("RB", [128, 2, NSLOT, 2, 384], BF16)
    TTf = B.sb("TTf", [128, 2, NSLOT, 128], BF16)
    BKs = B.sb("BKs", [128, 2, NSLOT, 256], BF16)
    wkv = B.sb("wkv", [128, 32, 64], F32)
    wkvd = B.sb("wkvd", [128, 2, 32, 64], F32)
    H32 = B.sb("H32", [128, 2, 64], F32)
    Hbf = B.sb("Hbf", [128, 2, 2, 64], BF16)
    Wsb = B.sb("Wsb", [128, 2, 64], BF16)
    Usb = B.sb("Usb", [128, 2, 64], BF16)
    sqw = B.sb("sqw", [128, 32, 64], F32)
    lnb = B.sb("lnb", [128, 32, 64], BF16)
    st4 = B.sb("st4", [128, 4, 32], F32)
    bon2 = B.sb("bon", [128, 2, SEQ], F32)
    gat2 = B.sb("gat", [128, 2, SEQ], BF16)
    yf = B.sb("yf", [128, 2, 1024], F32)
    yTa = B.sb("yTa", [128, SEQ], BF16)
    psMK = B.ps("psMK", [128, 1024])
    psR = [B.ps("psR0", [128, 512]), B.ps("psR1", [128, 512])]
    psWY = [B.ps("psWY0", [128, 512]), B.ps("psWY1", [128, 512])]
    psUH = [B.ps("psUH0", [128, 512]), B.ps("psUH1", [128, 512])]
    psMKb = psMK[:].bitcast(BF16)
    psRb = [psR[0][:].bitcast(BF16), psR[1][:].bitcast(BF16)]
    kRb = [("psR", 0), ("psR", 1)]
    KMK = [("psMK0",), ("psMK1",)]
    S.op("dve", lambda e: e.memset(psMK[:], 0.0), w=KMK)
    R = [slice(0, 64), slice(64, 128)]
    CP = [0]
    ARv = [None]
    BKv = [None]
    PCv = [None]
    SG = dict(skip_group_check=True)

    def chain(d, n):
        AR = ARv[0]
        PCs = PCv[0]
        cpk = CP[0]
        cn = n if d == 0 else 31 - n
        sl = n % NSLOT
        par = n % 2
        vk = ("Vst", cn // 16)
        kWY = ("psWY", d)
        kUH = ("psUH", d)
        psW = psWY[d][:, 0:64]
        psY = psWY[d][:, 64:128]
        psU = psUH[d][:, 0:64]
        psH = psUH[d][:, 64:128]
        S.op("pool", lambda e: e.tensor_scalar(out=HP[:, d, :], in0=H32[:, d, :], scalar1=PCs[:, d, cn:cn + 1],
                                               scalar2=None, op0=ALU.mult),
             r=[("H32", d), ("PCs", cpk, d)], w=[("HP", d)])
        S.op("pe", lambda e: e.matmul(psW, lhsT=MKs[:, d, sl, 256:384], rhs=Vst[:, cn, :],
                                      start=True, stop=False, **SG), r=[("MKs", d, sl), vk], w=[kWY])
        for h in range(2):
            S.op("pe", lambda e, h=h: e.matmul(psWY[d][R[h], 0:64], lhsT=AR[R[h], d, cn, 0, :],
                                               rhs=Hbf[R[h], d, par, :], start=False, stop=(h == 1), **SG),
                 r=[("AR", cpk, d), ("Hbf", d, par)], w=[kWY])
        S.op("act", lambda e: e.activation(out=Wsb[:, d, :], in_=psW, func=AF.Copy),
             r=[kWY], w=[("Wsb", d)])
        yield
        S.op("pe", lambda e: e.matmul(psU, lhsT=TTf[:, d, sl, :], rhs=Wsb[:, d, :], start=True, stop=True),
             r=[("TTf", d, sl), ("Wsb", d)], w=[kUH])
        S.op("dve", lambda e: e.tensor_copy(out=Usb[:, d, :], in_=psU), r=[kUH], w=[("Usb", d)])
        yield
        S.op("pe", lambda e: e.matmul(psH, lhsT=BKs[:, d, sl, 0:128], rhs=Usb[:, d, :],
                                      start=True, stop=False), r=[("BKs", d, sl), ("Usb", d)], w=[kUH])
        S.op("pe", lambda e: e.matmul(psH, lhsT=BKs[:, d, sl, 128:256], rhs=Vst[:, cn, :],
                                      start=False, stop=True), r=[("BKs", d, sl), vk], w=[kUH])
        S.op("pe", lambda e: e.matmul(psY, lhsT=MKs[:, d, sl, 128:256], rhs=Usb[:, d, :],
                                      start=True, stop=False, **SG), r=[("MKs", d, sl), ("Usb", d)], w=[kWY])
        S.op("pe", lambda e: e.matmul(psY, lhsT=MKs[:, d, sl, 384:512], rhs=Vst[:, cn, :],
                                      start=False, stop=False, **SG), r=[("MKs", d, sl), vk], w=[kWY])
        for h in range(2):
            S.op("pe", lambda e, h=h: e.matmul(psWY[d][R[h], 64:128], lhsT=AR[R[h], d, cn, 1, :],
                                               rhs=Hbf[R[h], d, par, :], start=False, stop=(h == 1), **SG),
                 r=[("AR", cpk, d), ("Hbf", d, par)], w=[kWY])
        S.op("dve", lambda e: e.scalar_tensor_tensor(out=Hbf[:, d, 1 - par, :], in0=psH,
                                                     scalar=PCs[:, d, cn:cn + 1], in1=HP[:, d, :],
                                                     op0=ALU.mult, op1=ALU.add),
             r=[("HP", d), ("PCs", cpk, d), kUH], w=[("Hbf", d, 1 - par)])
        S.op("dve", lambda e: e.scalar_tensor_tensor(out=H32[:, d, :], in0=psH,
                                                     scalar=PCs[:, d, cn:cn + 1], in1=HP[:, d, :],
                                                     op0=ALU.mult, op1=ALU.add),
             r=[("HP", d), ("PCs", cpk, d), kUH], w=[("H32", d)])
        S.op("act", lambda e: e.activation(out=wkvd[:, d, cn, :], in_=psY, func=AF.Copy),
             r=[kWY], w=[("wkvd", d, cn)])
        yield

    def pre(d, n):
        AR = ARv[0]
        BK = BKv[0]
        cpk = CP[0]
        cn = n if d == 0 else 31 - n
        sl = n % NSLOT
        ck = slice(cn * 64, (cn + 1) * 64)
        for h in range(2):
            for a in range(2):
                S.op("pe", lambda e, h=h, a=a: e.matmul(
                    psMK[R[h], a * 128 + h * 64:a * 128 + (h + 1) * 64], lhsT=BK[R[h], d, 0, ck],
                    rhs=AR[R[h], d, cn, a, :], start=True, stop=True), r=[("BK", cpk, d), ("AR", cpk, d)], w=[KMK[0]])
                S.op("pe", lambda e, h=h, a=a: e.matmul(
                    psMK[R[h], 256 + a * 128 + h * 64:256 + a * 128 + (h + 1) * 64], lhsT=BK[R[h], d, 1, ck],
                    rhs=AR[R[h], d, cn, a, :], start=True, stop=True), r=[("BK", cpk, d), ("AR", cpk, d)], w=[KMK[0]])
            S.op("pe", lambda e, h=h: e.matmul(
                psMK[R[h], 512 + h * 64:512 + (h + 1) * 64], lhsT=AR[R[h], d, cn, 0, :], rhs=BK[R[h], d, 0, ck],
                start=True, stop=True), r=[("BK", cpk, d), ("AR", cpk, d)], w=[KMK[1]])
            S.op("pe", lambda e, h=h: e.transpose(
                out=psMKb[R[h], 1280 + h * 64:1280 + (h + 1) * 64], in_=BK[R[h], d, 0, ck],
                identity=ident[R[h], R[h]]), r=[("BK", cpk, d), ("ident",)], w=[KMK[1]])
            S.op("pe", lambda e, h=h: e.transpose(
                out=psMKb[R[h], 1408 + h * 64:1408 + (h + 1) * 64], in_=BK[R[h], d, 1, ck],
                identity=ident[R[h], R[h]]), r=[("BK", cpk, d), ("ident",)], w=[KMK[1]])
        S.op("dve", lambda e: e.tensor_tensor(out=MKs[:, d, sl, :], in0=psMK[:, 0:640], in1=mk[:, d, :], op=ALU.mult),
             r=KMK + [("mk",)], w=[("MKs", d, sl)])
        S.op("dve", lambda e: e.tensor_copy(out=BKs[:, d, sl, :], in_=psMKb[:, 1280:1536]),
             r=[KMK[1]], w=[("BKs", d, sl)])
        S.op("pool", lambda e: e.tensor_tensor(out=TT0[:, d, sl, :], in0=MKs[:, d, sl, 0:128], in1=ident[:], op=ALU.add),
             r=[("MKs", d, sl), ("ident",)], w=[("TT0", d, sl)])
        yield
        for r in range(1, 7):
            if r == 1:
                Mp = MKs[:, d, sl, 512:640]
                MTp = MKs[:, d, sl, 0:128]
                srck = [("MKs", d, sl)]
            else:
                Mp = RB[:, d, sl, (r - 1) % 2, 0:128]
                MTp = RB[:, d, sl, (r - 1) % 2, 128:256]
                srck = [("RB", d, sl, (r - 1) % 2)]
            if r == 2:
                TTs = TT0[:, d, sl, :]
                srck = srck + [("TT0", d, sl)]
            elif r >= 3:
                TTs = RB[:, d, sl, (r - 1) % 2, 256:384]
            if r <= 5:
                S.op("pe", lambda e, Mp=Mp, MTp=MTp: e.matmul(psR[d][:, 0:128], lhsT=MTp, rhs=Mp, start=True, stop=True),
                     r=srck, w=[("psR", d)])
            if r == 1:
                S.op("pe", lambda e, Mp=Mp, MTp=MTp: e.matmul(psR[d][:, 128:256], lhsT=Mp, rhs=MTp, start=True, stop=True),
                     r=srck, w=[("psR", d)])
            elif r == 2:
                S.op("pe", lambda e, Mp=Mp, MTp=MTp: e.matmul(psR[d][:, 128:256], lhsT=Mp, rhs=MTp, start=True, stop=True),
                     r=srck, w=[("psR", d)])
                S.op("pe", lambda e, TTs=TTs, Mp=Mp: e.matmul(psR[d][:, 256:384], lhsT=Mp, rhs=TTs, start=True, stop=True),
                     r=srck, w=[("psR", d)])
            elif r <= 4:
                S.op("pe", lambda e, Mp=Mp, r=r: e.matmul(psR[d][:, 128:384], lhsT=Mp, rhs=RB[:, d, sl, (r - 1) % 2, 128:384],
                                                          start=True, stop=True), r=srck, w=[("psR", d)])
            else:
                S.op("pe", lambda e, TTs=TTs, Mp=Mp: e.matmul(psR[d][:, 256:384], lhsT=Mp, rhs=TTs, start=True, stop=True),
                     r=srck, w=[("psR", d)])
            if r <= 5:
                if r % 2 == 0:
                    S.op("act", lambda e, r=r: e.activation(out=RB[:, d, sl, r % 2, :], in_=psR[d][:, 0:384], func=AF.Copy),
                         r=[("psR", d)], w=[("RB", d, sl, r % 2)])
                else:
                    S.op("dve", lambda e, r=r: e.tensor_copy(out=RB[:, d, sl, r % 2, :], in_=psR[d][:, 0:384]),
                         r=[("psR", d)], w=[("RB", d, sl, r % 2)])
                if r >= 2:
                    S.op("pool", lambda e, r=r, TTs=TTs: e.tensor_tensor(
                        out=RB[:, d, sl, r % 2, 256:384], in0=RB[:, d, sl, r % 2, 256:384], in1=TTs, op=ALU.add),
                        r=[("RB", d, sl, r % 2)] + srck, w=[("RB", d, sl, r % 2)])
            else:
                S.op("act", lambda e: e.activation(out=TTf[:, d, sl, :], in_=psR[d][:, 256:384], func=AF.Copy),
                     r=[("psR", d)], w=[("TTf", d, sl)])
                S.op("pool", lambda e, TTs=TTs: e.tensor_tensor(out=TTf[:, d, sl, :], in0=TTf[:, d, sl, :], in1=TTs,
                                                                op=ALU.add),
                     r=[("TTf", d, sl)] + srck, w=[("TTf", d, sl)])
            yield

    def zipper(gens):
        gens = list(gens)
        while gens:
            nxt = []
            for g in gens:
                try:
                    next(g)
                    nxt.append(g)
                except StopIteration:
                    pass
            gens = nxt

    for c in range(getattr(B, "npairs", 4)):
        cp = c % 2
        CP[0] = cp
        ARv[0] = AR2[:, cp]
        BKv[0] = BK2[:, cp]
        PCv[0] = PCs2[:, cp]

        def load_pair(cc):
            pp_ = cc % 2
            for d in range(2):
                S.dma("sp", AR2[:, pp_, d, :, :, :].rearrange("p n a t -> p (n a t)"), scr["AR"][cc][d],
                      r=[("scr_AR", cc, d)], w=[("AR", pp_, d)])
                S.dma("sp", BK2[:, pp_, d, :, :], scr["BK"][cc][d], r=[("scr_BK", cc, d)], w=[("BK", pp_, d)])
                S.dma("sp", PCs2[:, pp_, d, :], scr["PC"][cc][d], r=[("scr_PC", cc, d)], w=[("PCs", pp_, d)])
                if d == 0:
                    S.dma("sp", vT2[:, pp_, :], scr["vT"][cc], r=[("scr_vT", cc)], w=[("vT", pp_)])
            S.dma("sp", bon2[:, pp_, :], scr["bonus"][cc], r=[("scr_bonus", cc)], w=[("bon", pp_)])
            S.dma("sp", gat2[:, pp_, :], scr["gate"][cc], r=[("scr_gate", cc)], w=[("gat", pp_)])

        if c == 0:
            load_pair(0)
        if c + 1 < getattr(B, "npairs", 4):
            pass
        if c + 1 < getattr(B, "npairs", 4):
            load_pair(c + 1)
        for half in range(2):
            for q in range(16):
                cn = half * 16 + q
                for h in range(2):
                    S.op("pe", lambda e, h=h, cn=cn, q=q, half=half, cp=cp: e.transpose(
                        out=psRb[half][R[h], q * 64:(q + 1) * 64], in_=vT2[R[h], cp, cn * 64:(cn + 1) * 64],
                        identity=ident[R[h], R[h]]), r=[("vT", cp), ("ident",)], w=[kRb[half]])
            S.op("act", lambda e, half=half: e.activation(
                out=Vst[:, half * 16:(half + 1) * 16, :].rearrange("p n v -> p (n v)"), in_=psRb[half][:, 0:1024],
                func=AF.Copy), r=[kRb[half]], w=[("Vst", half)])
        for d in range(2):
            S.op("pool", lambda e, d=d: e.memset(H32[:, d, :], 0.0), w=[("H32", d)])
            S.op("pool", lambda e, d=d: e.memset(Hbf[:, d, 0, :], 0.0), w=[("Hbf", d, 0)])
        zipper([pre(0, 0), pre(1, 0), pre(0, 1), pre(1, 1), pre(0, 2), pre(1, 2)])
        cont = []
        for n in range(32):
            newp = [pre(0, n + 3), pre(1, n + 3)] if n + 3 < 32 else []
            ch = [chain(0, n), chain(1, n)]
            order = []
            for d in range(2):
                order.append(ch[d])
                if newp:
                    order.append(newp[d])
            order += cont
            for _ in range(4):
                for g in order:
                    try:
                        next(g)
                    except StopIteration:
                        pass
            cont = newp
        for g in cont:
            for _ in range(8):
                try:
                    next(g)
                except StopIteration:
                    break
        allw = [("wkv",)]
        S.op("dve", lambda e: e.tensor_tensor(out=wkv[:], in0=wkvd[:, 0, :, :], in1=wkvd[:, 1, :, :], op=ALU.add),
             r=[("wkvd", d, n) for d in range(2) for n in range(32)], w=allw)
        S.op("dve", lambda e: e.tensor_reduce(out=st4[:, 0, :], in_=wkv[:], axis=AX.X, op=ALU.add),
             r=allw, w=[("st4", 0)])
        S.op("act", lambda e: e.activation(out=sqw[:].rearrange("p n v -> p (n v)"),
                                           in_=wkv[:].rearrange("p n v -> p (n v)"), func=AF.Square),
             r=allw, w=[("sqw",)])
        S.op("dve", lambda e: e.tensor_reduce(out=st4[:, 1, :], in_=sqw[:], axis=AX.X, op=ALU.add),
             r=[("sqw",)], w=[("st4", 1)])
        S.op("dve", lambda e: e.tensor_scalar(out=st4[:, 0, :], in0=st4[:, 0, :], scalar1=1.0 / 64, scalar2=None,
                                              op0=ALU.mult), r=[("st4", 0)], w=[("st4", 0)])
        S.op("dve", lambda e: e.tensor_tensor(out=st4[:, 2, :], in0=st4[:, 0, :], in1=st4[:, 0, :], op=ALU.mult),
             r=[("st4", 0)], w=[("st4", 2)])
        S.op("dve", lambda e: e.scalar_tensor_tensor(out=st4[:, 3, :], in0=st4[:, 1, :], scalar=1.0 / 64,
                                                     in1=st4[:, 2, :], op0=ALU.mult, op1=ALU.subtract),
             r=[("st4", 1), ("st4", 2)], w=[("st4", 3)])
        S.op("dve", lambda e: e.tensor_scalar(out=st4[:, 3, :], in0=st4[:, 3, :], scalar1=LNX_EPS, scalar2=None,
                                              op0=ALU.add), r=[("st4", 3)], w=[("st4", 3)])
        S.op("pool", lambda e: e.tensor_tensor(out=st4[:, 3, :], in0=st4[:, 3, :],
                                               in1=m05f[:, 0:1].to_broadcast([128, 32]), op=ALU.pow),
             r=[("st4", 3), ("m05f",)], w=[("st4", 3)])
        S.op("dve", lambda e: e.tensor_tensor(out=sqw[:], in0=wkv[:],
                                              in1=st4[:, 0, :].unsqueeze(2).to_broadcast([128, 32, 64]),
                                              op=ALU.subtract), r=allw + [("st4", 0)], w=[("sqw",)])
        S.op("dve", lambda e: e.tensor_tensor(out=lnb[:], in0=sqw[:],
                                              in1=st4[:, 3, :].unsqueeze(2).to_broadcast([128, 32, 64]),
                                              op=ALU.mult), r=[("sqw",), ("st4", 3)], w=[("lnb",)])
        for half in range(2):
            pb_ = psRb[half]
            pk = kRb[half]
            for q in range(16):
                n = half * 16 + q
                for h in range(2):
                    S.op("pe", lambda e, h=h, n=n, q=q, pb_=pb_: e.transpose(
                        out=pb_[R[h], q * 64:(q + 1) * 64], in_=lnb[R[h], n, :], identity=ident[R[h], R[h]]),
                        r=[("lnb",), ("ident",)], w=[pk])
            hs = slice(half * 1024, (half + 1) * 1024)
            S.op("dve", lambda e, half=half, pb_=pb_, c=c: e.tensor_scalar(
                out=yf[:, half, :], in0=pb_[:, 0:1024], scalar1=lgT[:, c:c + 1], scalar2=lbT[:, c:c + 1],
                op0=ALU.mult, op1=ALU.add), r=[pk, ("lgT",), ("lbT",)], w=[("yf", half)])
            S.op("pool", lambda e, half=half, hs=hs, cp=cp: e.tensor_tensor(out=yf[:, half, :], in0=yf[:, half, :],
                                                                           in1=bon2[:, cp, hs], op=ALU.add),
                 r=[("yf", half), ("bon", cp)], w=[("yf", half)])
            S.op("dve", lambda e, half=half, hs=hs, cp=cp: e.tensor_tensor(out=yTa[:, hs], in0=yf[:, half, :],
                                                                          in1=gat2[:, cp, hs], op=ALU.mult),
                 r=[("yf", half), ("gat", cp)], w=[("yTa", half)])
        S.dma("sp", scr["yT"][0][:, c, :], yTa[:], r=[("yTa", 0), ("yTa", 1)], w=[("scr_yT0", c)])
    B.end()


def finalnorm_phase(B, x_d, out_d, g_d, ntok):
    S = B.S
    B.begin()
    m05 = B.sb("m05", [128, 8], F32)
    S.op("pool", lambda e: e.memset(m05[:], -0.5), w=[("m05",)])
    gb = B.sb("gb", [128, D], F32)
    S.dma("sp", gb[:], g_d.partition_broadcast(128), w=[("gb",)])
    xt = B.sb("xt", [128, 2, D], F32)
    junk = B.sb("junk", [128, D], BF16)
    ss = B.sb("ss", [128, 8], F32)
    xo = B.sb("xo", [128, 2, D], F32)
    for t in range(ntok // 128):
        sl = t % 2
        c8 = t % 8
        r0 = t * 128
        S.dma("sp", xt[:, sl, :], x_d[r0:r0 + 128, :], w=[("xt", sl)])
        S.op("act", lambda e, sl=sl, c8=c8: e.activation(
            out=junk[:], in_=xt[:, sl, :], func=AF.Square, accum_out=ss[:, c8:c8 + 1]),
            r=[("xt", sl)], w=[("junk",), ("ss", c8)])
        S.op("dve", lambda e, c8=c8: e.tensor_scalar(
            out=ss[:, c8:c8 + 1], in0=ss[:, c8:c8 + 1], scalar1=1.0 / D, scalar2=1e-6,
            op0=ALU.mult, op1=ALU.add), r=[("ss", c8)], w=[("ss", c8)])
        S.op("pool", lambda e, c8=c8: e.tensor_tensor(
            out=ss[:, c8:c8 + 1], in0=ss[:, c8:c8 + 1], in1=m05[:, 0:1], op=ALU.pow),
            r=[("ss", c8), ("m05",)], w=[("ss", c8)])
        S.op("act", lambda e, sl=sl, c8=c8: e.activation(
            out=xo[:, sl, :], in_=xt[:, sl, :], func=AF.Copy, scale=ss[:, c8:c8 + 1]),
            r=[("xt", sl), ("ss", c8)], w=[("xo", sl)])
        S.op("dve", lambda e, sl=sl: e.tensor_tensor(out=xo[:, sl, :], in0=xo[:, sl, :], in1=gb[:], op=ALU.mult),
             r=[("xo", sl), ("gb",)], w=[("xo", sl)])
        S.dma("sp", out_d[r0:r0 + 128, :], xo[:, sl, :], r=[("xo", sl)], w=[("out", r0)])
    B.end()


def ffn_full_phase(B, x_in, x_out, wup_d, wdn_d, g_d, C, ntok):
    B.begin()
    load_consts(B, C["ident"])
    gT = load_colvec(B, "g_ffn", g_d, 8)
    ffn_phase(B, x_in, x_out, wup_d, wdn_d, "g_ffn", gT, ntok, "xres", "xres")
    B.end()


def build_full(nlayers=4, nseq=2, debug=False):
    ntok = nseq * SEQ
    B = Builder()
    B.debug_scratch = False
    W, C = declare_io(B, ntok)
    x = B.dram_in("x", [ntok, D])
    out = B.dram_out("out", [ntok, D])
    xres = B.dram_tmp("xres", [ntok, D])
    scr = declare_scratch(B)
    declare_rwkv_scratch(B, scr)
    for l in range(nlayers):
        ffn_full_phase(B, x if l == 0 else xres, xres, W["ffn1_w_up"][l], W["ffn1_w_down"][l], W["ffn1_norm"][l], C, ntok)
        Wl = dict(w0=W["rwkv_w0"][l], w2=W["rwkv_w2"][l], a0=W["rwkv_a0"][l], a2=W["rwkv_a2"][l],
                  g2=W["rwkv_g2"][l], k_k=W["rwkv_k_k"][l], k_a=W["rwkv_k_a"][l], r_k=W["rwkv_r_k"][l],
                  lnx_g=W["rwkv_lnx_g"][l], lnx_b=W["rwkv_lnx_b"][l])
        for sq in range(nseq):
            tok0 = sq * SEQ
            mixnorm_phase(B, xres, tok0, W["mix_norm"][l], C, scr)
            rwkvproj_phase(B, W["w_in"][l], W["rwkv_mu"][l], scr)
            qkvproj_phase(B, W["w_in"][l], W["ga_q_norm"][l], W["ga_k_norm"][l], C, scr)
            attn_phase(B, W["wa_sink"][l], C, scr)
            rwkvprep_phase(B, Wl, C, scr)
            rwkvscan_phase(B, Wl, C, scr)
            merge_phase(B, xres, tok0, W["w_in"][l], [W["w_o_rwkv"][l], W["w_o_ga"][l], W["w_o_wa"][l]],
                        W["w_out"][l], scr)
        ffn_full_phase(B, xres, xres, W["ffn2_w_up"][l], W["ffn2_w_down"][l], W["ffn2_norm"][l], C, ntok)
    finalnorm_phase(B, xres, out, W["final_norm"], ntok)
    nc = B.finish()
    return nc, B


_CACHE = {}


def kernel(**inputs):
    x = np.ascontiguousarray(np.asarray(inputs["x"], dtype=np.float32))
    ncores = 8
    if "nc" not in _CACHE:
        _CACHE["nc"] = build_full()[0]
        _CACHE["consts"] = make_consts()
    nc = _CACHE["nc"]
    consts = _CACHE["consts"]
    shared = {name: np.ascontiguousarray(np.asarray(inputs[name], dtype=np.float32)) for name, _ in W_SPECS}
    shared.update({"c_" + k: v for k, v in consts.items()})
    xs = x.reshape(ncores, 2 * SEQ, D)
    in_maps = []
    for c in range(ncores):
        m = dict(shared)
        m["x"] = np.ascontiguousarray(xs[c])
        in_maps.append(m)
    res = run_bass_kernel_spmd(nc, in_maps, core_ids=list(range(ncores)))
    outs = [np.asarray(r["out"], dtype=np.float32) for r in res.results]
    return np.stack(outs, 0).reshape(16, SEQ, D)
```

```python
import numpy as np
from contextlib import ExitStack
import concourse.bass as bass
import concourse.mybir as mybir
from concourse.bass_utils import run_bass_kernel_spmd

F32 = mybir.dt.float32
BF16 = mybir.dt.bfloat16
AF = mybir.ActivationFunctionType
ALU = mybir.AluOpType
AX = mybir.AxisListType

D = 1024
DFF = 2816
NJ = DFF // 128
NDMA_SEM = 24
PSUM_KEYS = set()


class Sched:
    ENGS = ("pe", "act", "dve", "pool", "sp")

    def __init__(self, nc):
        self.nc = nc
        self.ops = []

    def op(self, eng, fn, r=(), w=(), dma=False):
        self.ops.append([eng, fn, tuple(r), tuple(w), dma])

    def dma(self, q, out, in_, r=(), w=(), **kw):
        self.op(q, lambda e: e.dma_start(out=out, in_=in_, **kw), r, w, dma=True)

    def emit_phase(self):
        nc = self.nc
        ops = self.ops
        self.ops = []
        n = len(ops)
        if n == 0:
            return
        if not hasattr(self, "eng_count"):
            self.eng_count = {e: 0 for e in self.ENGS}
            self.dma_count = {e: 0 for e in self.ENGS}
            self.waited = {e: {} for e in self.ENGS}
            self.sems = {}
            self.semstack = ExitStack()
            self.ninst = 0
        last_w = {}
        readers = {}
        deps = [None] * n
        for i, (eng, fn, r, w, dma) in enumerate(ops):
            d = set()
            for k in r:
                j = last_w.get(k)
                if j is not None:
                    d.add(j)
            for k in w:
                j = last_w.get(k)
                if j is not None:
                    d.add(j)
                for j in readers.get(k, ()):
                    d.add(j)
            d.discard(i)
            deps[i] = d
            for k in r:
                readers.setdefault(k, []).append(i)
            for k in w:
                last_w[k] = i
                readers[k] = []
        last_x = {}
        for i, (eng, fn, r, w, dma) in enumerate(ops):
            for k in set(r) | set(w):
                if k[0] in PSUM_KEYS:
                    lx = last_x.setdefault(k, {})
                    for e2, j in lx.items():
                        if e2 != eng:
                            deps[i].add(j)
                    lx[eng] = i
        for i in range(n):
            eng, fn, r, w, dma = ops[i]
            keep = set()
            rw = set(r) | set(w)
            for j in deps[i]:
                ej, _, rj, wj, dj = ops[j]
                if ej == eng and not dj:
                    if eng == "pe":
                        continue
                    if not (set(wj) & rw):
                        continue
                keep.add(j)
            deps[i] = keep
        signal = [False] * n
        for i in range(n):
            for j in deps[i]:
                signal[j] = True
        eng_count = self.eng_count
        dma_count = self.dma_count
        sig = [None] * n
        pre_wait = [None] * n
        for i, (eng, fn, r, w, dma) in enumerate(ops):
            if dma:
                c = dma_count[eng]
                sidx = c % NDMA_SEM
                v = 16 * (c // NDMA_SEM + 1)
                sig[i] = (("dma", eng, sidx), v)
                if c >= NDMA_SEM:
                    pre_wait[i] = (("dma", eng, sidx), v - 16)
                dma_count[eng] = c + 1
            elif signal[i]:
                eng_count[eng] += 1
                sig[i] = (("eng", eng), eng_count[eng])
        per_eng = {e: [] for e in self.ENGS}
        waited = self.waited
        for i, (eng, fn, r, w, dma) in enumerate(ops):
            waits = {}
            if pre_wait[i] is not None:
                sk, v = pre_wait[i]
                waits[sk] = max(waits.get(sk, 0), v)
            for j in deps[i]:
                sk, v = sig[j]
                waits[sk] = max(waits.get(sk, 0), v)
            wl = []
            for sk, v in waits.items():
                if waited[eng].get(sk, 0) >= v:
                    continue
                waited[eng][sk] = v
                wl.append((sk, v))
            per_eng[eng].append((wl, fn, sig[i], dma))
        final = {e: [] for e in self.ENGS}
        for e in self.ENGS:
            c = dma_count[e]
            for sidx in range(min(c, NDMA_SEM)):
                cnt = (c - sidx + NDMA_SEM - 1) // NDMA_SEM
                sk = ("dma", e, sidx)
                if waited[e].get(sk, 0) < 16 * cnt:
                    waited[e][sk] = 16 * cnt
                    final[e].append((sk, 16 * cnt))
        for e in self.ENGS:
            for wl, fn, sg, dma in per_eng[e]:
                self.ninst += 1 + len(wl)
                if sg is not None and sg[0] not in self.sems:
                    self.sems[sg[0]] = self.semstack.enter_context(
                        nc.semaphore("s_" + "_".join(str(x) for x in sg[0])))
        sems = self.sems
        with nc.Block() as block:
            def run(e, engobj):
                for wl, fn, sg, dma in per_eng[e]:
                    for sk, v in wl:
                        engobj.wait_ge(sems[sk], v)
                    ins = fn(engobj)
                    if sg is not None:
                        ins.then_inc(sems[sg[0]], 16 if dma else 1)
                for sk, v in final[e]:
                    engobj.wait_ge(sems[sk], v)

            @block.tensor
            def _(e):
                run("pe", e)

            @block.scalar
            def _(e):
                run("act", e)

            @block.vector
            def _(e):
                run("dve", e)

            @block.gpsimd
            def _(e):
                run("pool", e)

            @block.sync
            def _(e):
                run("sp", e)

    def close(self):
        if hasattr(self, "semstack"):
            self.semstack.close()


class Builder:
    def __init__(self):
        self.nc = bass.Bass("TRN2", target_bir_lowering=False)
        self.S = Sched(self.nc)
        self.st = None
        self.T = {}
        self.pid = 0

    def dram_in(self, name, shape, dt=F32):
        return self.nc.dram_tensor(name, list(shape), dt, kind="ExternalInput").ap()

    def dram_out(self, name, shape, dt=F32):
        return self.nc.dram_tensor(name, list(shape), dt, kind="ExternalOutput").ap()

    def dram_tmp(self, name, shape, dt=F32):
        kind = "ExternalOutput" if getattr(self, "debug_scratch", False) else "Internal"
        return self.nc.dram_tensor(name, list(shape), dt, kind=kind).ap()

    def begin(self):
        self.st = ExitStack()
        self.T = {}
        self.pid += 1

    def end(self):
        self.S.emit_phase()
        self.st.close()
        self.st = None
        self.T = {}

    def sb(self, name, shape, dt):
        if name not in self.T:
            self.T[name] = self.st.enter_context(
                self.nc.sbuf_tensor("sb%d_%s" % (self.pid, name), list(shape), dt))
        return self.T[name]

    def ps(self, name, shape, dt=F32):
        if name not in self.T:
            self.T[name] = self.st.enter_context(
                self.nc.psum_tensor("ps%d_%s" % (self.pid, name), list(shape), dt))
        return self.T[name]

    def finish(self):
        self.S.close()
        return self.nc


def load_consts(B, ident_d):
    S = B.S
    ident = B.sb("ident", [128, 128], BF16)
    S.dma("sp", ident[:], ident_d, w=[("ident",)])
    m05 = B.sb("m05", [128, 8], F32)
    S.op("pool", lambda e: e.memset(m05[:], -0.5), w=[("m05",)])


def load_gT(B, name, vec_d):
    S = B.S
    t = B.sb(name, [128, 8], F32)
    S.dma("sp", t[:], vec_d.rearrange("(c p) -> p c", p=128), w=[(name,)],
          allow_slow_non_contiguous=True)
    return t


def norm_tile(B, x_d, gname, gT, r0, t, dst, dst_key, tag, part="both"):
    S = B.S
    xt = B.sb("xt", [128, 2, D], F32)
    junk = B.sb("junk", [128, D], BF16)
    ss = B.sb("ss", [128, 8], F32)
    rs = B.sb("rs", [128, 8], F32)
    xn = B.sb("xn", [128, 2, D], BF16)
    ident = B.T["ident"]
    m05 = B.T["m05"]
    ptr = B.ps("ptr", [128, 1024], BF16)
    sl = t % 2
    c8 = t % 8
    if part in ("both", "front"):
        norm_tile_front(S, x_d, r0, tag, xt, junk, ss, rs, xn, m05, sl, c8)
    if part == "front":
        return
    for c in range(8):
        S.op("pe", lambda e, c=c: e.transpose(out=ptr[:, c * 128:(c + 1) * 128], in_=xn[:, sl, c * 128:(c + 1) * 128],
                                              identity=ident[:]), r=[("xn", sl), ("ident",)], w=[("ptr",)])
    S.op("dve", lambda e: e.tensor_tensor(out=dst, in0=ptr[:].rearrange("p (c n) -> p c n", c=8),
                                          in1=gT[:].unsqueeze(2).to_broadcast([128, 8, 128]), op=ALU.mult),
         r=[("ptr",), (gname,)], w=[dst_key])


def norm_tile_front(S, x_d, r0, tag, xt, junk, ss, rs, xn, m05, sl, c8):
    S.dma("sp", xt[:, sl, :], x_d[r0:r0 + 128, :], r=[(tag, "x", r0)], w=[("xt", sl)])
    S.op("act", lambda e: e.activation(out=junk[:], in_=xt[:, sl, :], func=AF.Square, accum_out=ss[:, c8:c8 + 1]),
         r=[("xt", sl)], w=[("junk",), ("ss", c8)])
    S.op("dve", lambda e: e.tensor_scalar(out=rs[:, c8:c8 + 1], in0=ss[:, c8:c8 + 1], scalar1=1.0 / D, scalar2=1e-6,
                                          op0=ALU.mult, op1=ALU.add), r=[("ss", c8)], w=[("rs", c8)])
    S.op("pool", lambda e: e.tensor_tensor(out=rs[:, c8:c8 + 1], in0=rs[:, c8:c8 + 1], in1=m05[:, 0:1], op=ALU.pow),
         r=[("rs", c8), ("m05",)], w=[("rs", c8)])
    S.op("act", lambda e: e.activation(out=xn[:, sl, :], in_=xt[:, sl, :], func=AF.Copy, scale=rs[:, c8:c8 + 1]),
         r=[("xt", sl), ("rs", c8)], w=[("xn", sl)])


def norm_to_featmajor(B, x_d, gname, gT, tok0, ntile, xnT, xnT_key, tag):
    for t in range(ntile):
        norm_tile(B, x_d, gname, gT, tok0 + t * 128, t, xnT[:, :, t * 128:(t + 1) * 128], (xnT_key, t), tag)


def ffn_phase(B, x_in, x_out, wup_d, wdn_d, gname, gT, ntok, tag_in, tag_out):
    S = B.S
    TB = 1024
    nblk = ntok // TB
    xnT2 = B.sb("xnT", [128, 2, 8, TB], BF16)
    hT = B.sb("hT", [128, NJ, TB], BF16)
    wd = B.sb("wd", [128, NJ, D], BF16)
    wgu = B.sb("wgu", [128, 2, 2, 8, 256], BF16)
    sg = B.sb("sg", [128, 2, 512], F32)
    yo = B.sb("yo", [128, 2, D], F32)
    xr = B.sb("xr", [128, 2, D], F32)
    pg = [B.ps("pg0", [128, 512]), B.ps("pg1", [128, 512])]
    pu = [B.ps("pu0", [128, 512]), B.ps("pu1", [128, 512])]
    pd = [B.ps("pd0", [128, 512]), B.ps("pd1", [128, 512])]
    wup_v = wup_d.rearrange("(kc p) n -> p kc n", p=128)
    wdn_v = wdn_d.rearrange("(j p) n -> p j n", p=128)
    for j0 in range(0, NJ, 2):
        S.dma("pool", wd[:, j0:j0 + 2, :], wdn_v[:, j0:j0 + 2, :], w=[("wd", j0)])
    gi = 0
    ei = 0
    for b in range(nblk):
        tok0 = b * TB
        xp = b % 2
        xnT = xnT2[:, xp, :, :]
        if b == 0:
            for t in range(TB // 128):
                norm_tile(B, x_in, gname, gT, t * 128, t, xnT2[:, 0, :, t * 128:(t + 1) * 128], ("xnT", 0, t), tag_in)
        for jg in range(NJ // 2):
            sl = gi % 2
            gi += 1
            S.dma("pool", wgu[:, sl, 0, :, :], wup_v[:, :, jg * 256:(jg + 1) * 256],
                  w=[("wgu", sl, 0)])
            S.dma("pool", wgu[:, sl, 1, :, :], wup_v[:, :, DFF + jg * 256:DFF + (jg + 1) * 256],
                  w=[("wgu", sl, 1)])
            for jj in range(2):
                j = jg * 2 + jj
                for tb in range(TB // 512):
                    e2 = ei % 2
                    ei += 1
                    for kc in range(8):
                        S.op("pe", lambda e, sl=sl, jj=jj, kc=kc, tb=tb, e2=e2, xnT=xnT: e.matmul(
                            pg[e2][:], lhsT=wgu[:, sl, 0, kc, jj * 128:(jj + 1) * 128],
                            rhs=xnT[:, kc, tb * 512:(tb + 1) * 512], start=(kc == 0), stop=(kc == 7)),
                            r=[("wgu", sl, 0)] + [("xnT", xp, tb * 4 + q) for q in range(4)],
                            w=[("pg", e2)])
                    for kc in range(8):
                        S.op("pe", lambda e, sl=sl, jj=jj, kc=kc, tb=tb, e2=e2, xnT=xnT: e.matmul(
                            pu[e2][:], lhsT=wgu[:, sl, 1, kc, jj * 128:(jj + 1) * 128],
                            rhs=xnT[:, kc, tb * 512:(tb + 1) * 512], start=(kc == 0), stop=(kc == 7)),
                            r=[("wgu", sl, 1)] + [("xnT", xp, tb * 4 + q) for q in range(4)],
                            w=[("pu", e2)])
                    S.op("act", lambda e, e2=e2: e.activation(
                        out=sg[:, e2, :], in_=pg[e2][:], func=AF.Silu),
                        r=[("pg", e2)], w=[("sg", e2)])
                    S.op("dve", lambda e, e2=e2, j=j, tb=tb: e.tensor_tensor(
                        out=hT[:, j, tb * 512:(tb + 1) * 512], in0=pu[e2][:], in1=sg[:, e2, :],
                        op=ALU.mult), r=[("pu", e2), ("sg", e2)], w=[("hT", j, tb)])
        for t in range(TB // 128):
            r0 = tok0 + t * 128
            xs = t % 2
            S.dma("sp", xr[:, xs, :], x_in[r0:r0 + 128, :], r=[(tag_in, "x", r0)], w=[("xr", xs)])
            for nh in range(2):
                e2 = ei % 2
                ei += 1
                for j in range(NJ):
                    S.op("pe", lambda e, j=j, t=t, nh=nh, e2=e2: e.matmul(
                        pd[e2][:], lhsT=hT[:, j, t * 128:(t + 1) * 128],
                        rhs=wd[:, j, nh * 512:(nh + 1) * 512], start=(j == 0), stop=(j == NJ - 1)),
                        r=[("hT", j, t // 4), ("wd", j - j % 2)], w=[("pd", e2)])
                S.op("dve", lambda e, e2=e2, xs=xs, nh=nh: e.scalar_tensor_tensor(
                    out=yo[:, xs, nh * 512:(nh + 1) * 512], in0=pd[e2][:], scalar=0.5,
                    in1=xr[:, xs, nh * 512:(nh + 1) * 512],
                    op0=ALU.mult, op1=ALU.add), r=[("pd", e2), ("xr", xs)], w=[("yo", xs)])
            S.dma("sp", x_out[r0:r0 + 128, :], yo[:, xs, :],
                  r=[("yo", xs)], w=[(tag_out, "x", r0)])
            if b + 1 < nblk:
                nb0 = (b + 1) * TB
                if t == 0:
                    norm_tile(B, x_in, gname, gT, nb0, 0, None, None, tag_in, part="front")
                if t + 1 < TB // 128:
                    norm_tile(B, x_in, gname, gT, nb0 + (t + 1) * 128, t + 1, None, None, tag_in, part="front")
                norm_tile(B, x_in, gname, gT, nb0 + t * 128, t,
                          xnT2[:, 1 - xp, :, t * 128:(t + 1) * 128], ("xnT", 1 - xp, t), tag_in, part="back")


def build_ffn_test(ntok):
    B = Builder()
    x = B.dram_in("x", [ntok, D])
    wup = B.dram_in("wup", [D, 2 * DFF])
    wdn = B.dram_in("wdn", [DFF, D])
    g = B.dram_in("g", [D])
    ident = B.dram_in("ident", [128, 128], BF16)
    y = B.dram_out("y", [ntok, D])
    B.begin()
    load_consts(B, ident)
    gT = load_gT(B, "g_ffn", g)
    ffn_phase(B, x, y, wup, wdn, "g_ffn", gT, ntok, "xin", "xout")
    B.end()
    return B.finish()


SEQ = 2048
LAM = float(np.exp(-0.5))


def make_consts():
    import ml_dtypes
    bf = ml_dtypes.bfloat16
    c = {}
    c["ident"] = np.eye(128, dtype=np.float32).astype(bf)
    t = np.arange(SEQ)
    row = (t // 64).astype(np.float32)
    col = (t % 64).astype(np.float32)
    f16 = (np.float32(10000.0) ** (-np.arange(0, 32, 2, dtype=np.float32) / np.float32(32))).astype(np.float32)
    f32_ = (np.float32(10000.0) ** (-np.arange(0, 64, 2, dtype=np.float32) / np.float32(64))).astype(np.float32)
    ar = row[:, None] * f16[None, :]
    ac = col[:, None] * f16[None, :]
    a1 = t.astype(np.float32)[:, None] * f32_[None, :]
    rope = np.zeros((SEQ, 4, 64), np.float32)
    rope[:, 0] = np.concatenate([np.cos(ar), np.cos(ar), np.cos(ac), np.cos(ac)], -1)
    rope[:, 1] = np.concatenate([-np.sin(ar), np.sin(ar), -np.sin(ac), np.sin(ac)], -1)
    rope[:, 2] = np.concatenate([np.cos(a1), np.cos(a1)], -1)
    rope[:, 3] = np.concatenate([-np.sin(a1), np.sin(a1)], -1)
    c["rope"] = rope
    i = np.arange(128)[:, None]
    j = np.arange(128)[None, :]
    c["wamask"] = np.concatenate([(i <= j), np.ones((128, 128), bool), (i >= j)], 1).astype(np.float32).astype(bf)
    s_ = np.arange(64)[:, None]
    t_ = np.arange(64)[None, :]

    def bd(m):
        z = np.zeros((128, 128), np.float32)
        z[0:64, 0:64] = m
        z[64:128, 64:128] = m
        return z
    mkf = np.concatenate([bd(s_ < t_), bd(s_ <= t_), bd(s_ < t_), bd(s_ <= t_), bd(t_ < s_)], 1)
    mkb = np.concatenate([bd(s_ > t_), bd(s_ >= t_), bd(s_ > t_), bd(s_ >= t_), bd(t_ > s_)], 1)
    c["mkf"] = mkf.astype(np.float32)
    c["mkb"] = mkb.astype(np.float32)
    c["bones"] = bd(np.ones((64, 64), np.float32)).astype(bf)
    return c


CONST_SPECS = [("ident", [128, 128], BF16), ("rope", [SEQ, 4, 64], F32), ("wamask", [128, 384], BF16),
               ("mkf", [128, 640], F32), ("mkb", [128, 640], F32), ("bones", [128, 128], BF16)]

W_SPECS = [
    ("ffn1_norm", [4, 1024]), ("ffn1_w_up", [4, 1024, 5632]), ("ffn1_w_down", [4, 2816, 1024]),
    ("mix_norm", [4, 1024]), ("w_in", [4, 1024, 6528]), ("rwkv_mu", [4, 1920]), ("rwkv_w0", [4, 2, 512]),
    ("rwkv_w2", [4, 2, 64, 512]), ("rwkv_a0", [4, 2, 512]), ("rwkv_a2", [4, 2, 64, 512]),
    ("rwkv_g2", [4, 128, 512]), ("rwkv_k_k", [4, 512]), ("rwkv_k_a", [4, 512]), ("rwkv_r_k", [4, 8, 64]),
    ("rwkv_lnx_g", [4, 512]), ("rwkv_lnx_b", [4, 512]), ("ga_q_norm", [4, 64]), ("ga_k_norm", [4, 64]),
    ("wa_sink", [4, 8]), ("w_o_rwkv", [4, 512, 1024]), ("w_o_ga", [4, 512, 1024]), ("w_o_wa", [4, 512, 1024]),
    ("w_out", [4, 1024, 1024]), ("ffn2_norm", [4, 1024]), ("ffn2_w_up", [4, 1024, 5632]),
    ("ffn2_w_down", [4, 2816, 1024]), ("final_norm", [1024]),
]


def declare_io(B, ntok):
    W = {name: B.dram_in(name, shp) for name, shp in W_SPECS}
    C = {name: B.dram_in("c_" + name, shp, dt) for name, shp, dt in CONST_SPECS}
    return W, C


def declare_scratch(B):
    scr = {}
    scr["hT"] = B.dram_tmp("scr_hT", [128, 8, SEQ], BF16)
    scr["psh"] = B.dram_tmp("scr_psh", [15, 128, SEQ], F32)
    scr["qkt"] = B.dram_tmp("scr_qkt", [2, 128, 5, SEQ], BF16)
    scr["v"] = B.dram_tmp("scr_v", [2, SEQ, 128], BF16)
    scr["yT"] = B.dram_tmp("scr_yT", [3, 128, 4, SEQ], BF16)
    return scr


def load_colvec(B, name, vec_d, ncol, q="sp"):
    t = B.sb(name, [128, ncol], F32)
    B.S.dma(q, t[:], vec_d.rearrange("(c p) -> p c", p=128), w=[(name,)], allow_slow_non_contiguous=True)
    return t


def mixnorm_phase(B, x_d, tok0, gvec_d, C, scr):
    S = B.S
    B.begin()
    load_consts(B, C["ident"])
    gT = load_colvec(B, "g_mix", gvec_d, 8)
    hTm = B.sb("hTm", [128, 8, SEQ], BF16)
    norm_to_featmajor(B, x_d, "g_mix", gT, tok0, SEQ // 128, hTm, "hTm", "xres")
    S.dma("sp", scr["hT"], hTm[:], r=[("hTm", t) for t in range(16)], w=[("scr_hT",)])
    B.end()


def rwkvproj_phase(B, win_d, mu_d, scr):
    S = B.S
    B.begin()
    hTm = B.sb("hTm", [128, 8, SEQ], BF16)
    S.dma("sp", hTm[:], scr["hT"], w=[("hTm",)])
    muT = load_colvec(B, "muT", mu_d, 15)
    hmu = B.sb("hmu", [128, 15], F32)
    omm = B.sb("omm", [128, 15], F32)
    S.op("dve", lambda e: e.tensor_scalar(out=hmu[:], in0=muT[:], scalar1=0.5, scalar2=None, op0=ALU.mult),
         r=[("muT",)], w=[("hmu",)])
    S.op("dve", lambda e: e.tensor_scalar(out=omm[:], in0=muT[:], scalar1=-1.0, scalar2=1.0,
                                          op0=ALU.mult, op1=ALU.add), r=[("muT",)], w=[("omm",)])
    wsl = B.sb("wsl", [128, 2, 8, 256], BF16)
    praw = B.sb("praw", [128, 2, SEQ + 2], F32)
    nb = B.sb("nb", [128, 2, SEQ], F32)
    psh = B.sb("psh", [128, 2, SEQ], F32)
    pr = [B.ps("pr0", [128, 512]), B.ps("pr1", [128, 512])]
    win_v = win_d.rearrange("(kc p) n -> p kc n", p=128)
    for sl in range(2):
        S.op("pool", lambda e, sl=sl: e.memset(praw[:, sl, :], 0.0), w=[("praw", sl)])
    ei = 0
    for cg in range(8):
        ncol = 256 if cg < 7 else 128
        ws = cg % 2
        S.dma("pool", wsl[:, ws, :, 0:ncol], win_v[:, :, cg * 256:cg * 256 + ncol], w=[("wsl", ws)])
        for cc in range(ncol // 128):
            ch = cg * 2 + cc
            sl = ch % 2
            for tb in range(4):
                e2 = ei % 2
                ei += 1
                for kc in range(8):
                    S.op("pe", lambda e, ws=ws, cc=cc, kc=kc, tb=tb, e2=e2: e.matmul(
                        pr[e2][:], lhsT=wsl[:, ws, kc, cc * 128:(cc + 1) * 128],
                        rhs=hTm[:, kc, tb * 512:(tb + 1) * 512], start=(kc == 0), stop=(kc == 7)),
                        r=[("wsl", ws), ("hTm",)], w=[("pr", e2)])
                S.op("act", lambda e, sl=sl, tb=tb, e2=e2: e.activation(
                    out=praw[:, sl, 1 + tb * 512:1 + (tb + 1) * 512], in_=pr[e2][:], func=AF.Copy),
                    r=[("pr", e2)], w=[("praw", sl)])
            S.op("pool", lambda e, sl=sl: e.tensor_tensor(
                out=nb[:, sl, :], in0=praw[:, sl, 0:SEQ], in1=praw[:, sl, 2:SEQ + 2], op=ALU.add),
                r=[("praw", sl)], w=[("nb", sl)])
            S.op("dve", lambda e, ch=ch, sl=sl: e.tensor_scalar(
                out=nb[:, sl, :], in0=nb[:, sl, :], scalar1=hmu[:, ch:ch + 1], scalar2=None, op0=ALU.mult),
                r=[("nb", sl), ("hmu",)], w=[("nb", sl)])
            S.op("dve", lambda e, sl=sl, ch=ch: e.scalar_tensor_tensor(
                out=psh[:, sl, :], in0=praw[:, sl, 1:SEQ + 1], scalar=omm[:, ch:ch + 1], in1=nb[:, sl, :],
                op0=ALU.mult, op1=ALU.add), r=[("praw", sl), ("nb", sl), ("omm",)], w=[("psh", sl)])
            S.dma("sp", scr["psh"][ch], psh[:, sl, :], r=[("psh", sl)], w=[("scr_psh", ch)])
    B.end()


def qkvproj_phase(B, win_d, gq_d, gk_d, C, scr):
    S = B.S
    B.begin()
    load_consts(B, C["ident"])
    ident = B.T["ident"]
    m05 = B.T["m05"]
    hTm = B.sb("hTm", [128, 8, SEQ], BF16)
    S.dma("sp", hTm[:], scr["hT"], w=[("hTm",)])
    win_v = win_d.rearrange("(kc p) n -> p kc n", p=128)
    wqkv = B.sb("wqkv", [128, 2, 8, 768], BF16)
    for ty in range(2):
        c0 = 1920 + ty * 768
        for half in range(2):
            S.dma("pool", wqkv[:, ty, half * 4:(half + 1) * 4, :], win_v[:, half * 4:(half + 1) * 4, c0:c0 + 768],
                  w=[("wqkv", ty, half)])
    g2 = B.sb("g2", [128, 2, 64], F32)
    S.dma("sp", g2[:, 0, :], gq_d.partition_broadcast(128), w=[("g2", 0)])
    S.dma("sp", g2[:, 1, :], gk_d.partition_broadcast(128), w=[("g2", 1)])
    gqk = B.sb("gqk", [128, 10, 64], F32)
    S.op("dve", lambda e: e.tensor_copy(out=gqk[:, 0:8, :], in_=g2[:, 0:1, :].to_broadcast([128, 8, 64])),
         r=[("g2", 0)], w=[("gqk", 0)])
    S.op("dve", lambda e: e.tensor_copy(out=gqk[:, 8:10, :], in_=g2[:, 1:2, :].to_broadcast([128, 2, 64])),
         r=[("g2", 1)], w=[("gqk", 1)])
    NW = 3
    rope = B.sb("rope", [128, 2, 4, 64], F32)
    sq = B.sb("sq", [128, NW, 640], F32)
    ssq = B.sb("ssq", [128, NW, 10], F32)
    qn = B.sb("qn", [128, NW, 640], F32)
    ra = B.sb("ra", [128, NW, 640], F32)
    rb = B.sb("rb", [128, NW, 640], F32)
    qrp = B.sb("qrp", [128, NW, 640], BF16)
    qkt = B.sb("qkt", [128, 2, 5, SEQ], BF16)
    vtok = B.sb("vtok", [128, 2, 16, 128], BF16)
    psA = [B.ps("psA%d" % i, [128, 512]) for i in range(NW)]
    psB = [B.ps("psB%d" % i, [128, 512]) for i in range(NW)]
    ptr = [B.ps("ptr0", [128, 1024], BF16), B.ps("ptr1", [128, 1024], BF16)]
    rope_v = C["rope"].rearrange("(t p) a d -> t p a d", p=128)

    def v3(ap):
        return ap.rearrange("p (h d) -> p h d", d=64)

    def qk_iter(it):
        t, ty = it // 2, it % 2
        sl = it % NW
        rs = t % 2
        p2 = it % 2
        if ty == 0:
            S.dma("sp", rope[:, rs, :, :], rope_v[t], w=[("rope", rs)])
        for kc in range(8):
            S.op("pe", lambda e, kc=kc: e.matmul(
                psA[sl][:], lhsT=hTm[:, kc, t * 128:(t + 1) * 128], rhs=wqkv[:, ty, kc, 0:512],
                start=(kc == 0), stop=(kc == 7)), r=[("hTm",), ("wqkv", ty, kc // 4)], w=[("psA", sl)])
        for kc in range(8):
            S.op("pe", lambda e, kc=kc: e.matmul(
                psB[sl][:, 0:256], lhsT=hTm[:, kc, t * 128:(t + 1) * 128], rhs=wqkv[:, ty, kc, 512:768],
                start=(kc == 0), stop=(kc == 7)), r=[("hTm",), ("wqkv", ty, kc // 4)], w=[("psB", sl)])
        yield
        S.op("act", lambda e: e.activation(out=vtok[:, ty, t, :], in_=psB[sl][:, 128:256], func=AF.Copy),
             r=[("psB", sl)], w=[("vtok", ty, t)])
        if ty == 0:
            S.op("act", lambda e: e.activation(out=sq[:, sl, 0:512], in_=psA[sl][:], func=AF.Square),
                 r=[("psA", sl)], w=[("sq", sl)])
            S.op("act", lambda e: e.activation(out=sq[:, sl, 512:640], in_=psB[sl][:, 0:128], func=AF.Square),
                 r=[("psB", sl)], w=[("sq", sl)])
            yield
            S.op("dve", lambda e: e.tensor_reduce(out=ssq[:, sl, :], in_=v3(sq[:, sl, :]), axis=AX.X, op=ALU.add),
                 r=[("sq", sl)], w=[("ssq", sl)])
            S.op("dve", lambda e: e.tensor_scalar(out=ssq[:, sl, :], in0=ssq[:, sl, :], scalar1=1.0 / 64, scalar2=1e-6,
                                                  op0=ALU.mult, op1=ALU.add), r=[("ssq", sl)], w=[("ssq", sl)])
            yield
            S.op("pool", lambda e: e.tensor_tensor(out=ssq[:, sl, :], in0=ssq[:, sl, :],
                                                   in1=m05[:, 0:1].to_broadcast([128, 10]), op=ALU.pow),
                 r=[("ssq", sl), ("m05",)], w=[("ssq", sl)])
            yield
            S.op("dve", lambda e: e.tensor_tensor(
                out=v3(qn[:, sl, 0:512]), in0=v3(psA[sl][:]),
                in1=ssq[:, sl, 0:8].unsqueeze(2).to_broadcast([128, 8, 64]), op=ALU.mult),
                r=[("psA", sl), ("ssq", sl)], w=[("qn", sl)])
            S.op("dve", lambda e: e.tensor_tensor(
                out=v3(qn[:, sl, 512:640]), in0=v3(psB[sl][:, 0:128]),
                in1=ssq[:, sl, 8:10].unsqueeze(2).to_broadcast([128, 2, 64]), op=ALU.mult),
                r=[("psB", sl), ("ssq", sl)], w=[("qn", sl)])
            yield
            S.op("pool", lambda e: e.tensor_tensor(
                out=qn[:, sl, :], in0=qn[:, sl, :], in1=gqk[:].rearrange("p h d -> p (h d)"), op=ALU.mult),
                r=[("qn", sl), ("gqk", 0), ("gqk", 1)], w=[("qn", sl)])
            yield
        else:
            S.op("act", lambda e: e.activation(out=qn[:, sl, 0:512], in_=psA[sl][:], func=AF.Copy),
                 r=[("psA", sl)], w=[("qn", sl)])
            S.op("act", lambda e: e.activation(out=qn[:, sl, 512:640], in_=psB[sl][:, 0:128], func=AF.Copy),
                 r=[("psB", sl)], w=[("qn", sl)])
            yield
        ct = rope[:, rs, 2 * ty, :]
        st_ = rope[:, rs, 2 * ty + 1, :]
        S.op("dve", lambda e: e.tensor_tensor(
            out=v3(ra[:, sl, :]), in0=v3(qn[:, sl, :]), in1=ct.unsqueeze(1).to_broadcast([128, 10, 64]), op=ALU.mult),
            r=[("qn", sl), ("rope", rs)], w=[("ra", sl)])
        hb = 16 if ty == 0 else 32
        nbk = 64 // (2 * hb)
        for hf in range(2):
            S.op("pool", lambda e, hf=hf: e.tensor_tensor(
                out=rb[:, sl, :].rearrange("p (h b f d) -> p h b f d", h=10, b=nbk, f=2)[:, :, :, hf, :],
                in0=qn[:, sl, :].rearrange("p (h b f d) -> p h b f d", h=10, b=nbk, f=2)[:, :, :, 1 - hf, :],
                in1=st_.rearrange("p (b f d) -> p b f d", b=nbk, f=2)[:, :, hf, :].unsqueeze(1)
                .to_broadcast([128, 10, nbk, hb]), op=ALU.mult),
                r=[("qn", sl), ("rope", rs)], w=[("rb", sl)])
        yield
        S.op("dve", lambda e: e.tensor_tensor(
            out=qrp[:, sl, 0:512].rearrange("p (j g d) -> p g j d", j=4, g=2),
            in0=ra[:, sl, 0:512].rearrange("p (g j d) -> p g j d", j=4, g=2),
            in1=rb[:, sl, 0:512].rearrange("p (g j d) -> p g j d", j=4, g=2), op=ALU.add),
            r=[("ra", sl), ("rb", sl)], w=[("qrp", sl)])
        S.op("dve", lambda e: e.tensor_tensor(
            out=qrp[:, sl, 512:640], in0=ra[:, sl, 512:640], in1=rb[:, sl, 512:640], op=ALU.add),
            r=[("ra", sl), ("rb", sl)], w=[("qrp", sl)])
        yield
        for j in range(5):
            S.op("pe", lambda e, j=j: e.transpose(
                out=ptr[p2][:, j * 128:(j + 1) * 128], in_=qrp[:, sl, j * 128:(j + 1) * 128], identity=ident[:]),
                r=[("qrp", sl), ("ident",)], w=[("ptr", p2)])
        yield
        S.op("act", lambda e: e.activation(
            out=qkt[:, ty, :, t * 128:(t + 1) * 128], in_=ptr[p2][:, 0:640].rearrange("p (c n) -> p c n", c=5),
            func=AF.Copy), r=[("ptr", p2)], w=[("qkt", ty, t)])
        yield

    live = []
    nxt = 0
    while live or nxt < 32:
        if nxt < 32 and len(live) < NW:
            live.append(qk_iter(nxt))
            nxt += 1
        keep = []
        for g in live:
            try:
                next(g)
                keep.append(g)
            except StopIteration:
                pass
        live = keep
    for ty in range(2):
        S.dma("sp", scr["qkt"][ty], qkt[:, ty, :, :], r=[("qkt", ty, t) for t in range(16)], w=[("scr_qkt", ty)])
        S.dma("sp", scr["v"][ty].rearrange("(t p) n -> p t n", p=128), vtok[:, ty, :, :],
              r=[("vtok", ty, t) for t in range(16)], w=[("scr_v", ty)])
    B.end()


def attn_phase(B, sink_d, C, scr):
    S = B.S
    B.begin()
    NE = 6
    QT = B.sb("QT", [128, 5, SEQ], BF16)
    Vtok = B.sb("Vtok", [128, 16, 128], BF16)
    Vz = B.sb("Vz", [128, 16, 2, 128], BF16)
    O01 = B.sb("O01", [128, 2, 128], BF16)
    yT = B.sb("yT", [128, 4, SEQ], BF16)
    E = B.sb("E", [128, NE, 512], BF16)
    Em = B.sb("Em", [128, 8, 384], BF16)
    rden = B.sb("rden", [128, 2, 512], F32)
    wam = B.sb("wam", [128, 384], BF16)
    esink = B.sb("esink", [128, 4], F32)
    st = [B.ps("st%d" % i, [128, 512]) for i in range(4)]
    pn = [B.ps("pn0", [128, 512]), B.ps("pn1", [128, 512])]
    pdn = [B.ps("pdn0", [128, 512]), B.ps("pdn1", [128, 512])]
    S.dma("sp", wam[:], C["wamask"], w=[("wam",)])
    S.dma("sp", esink[0:64, :], sink_d[0:4].partition_broadcast(64), w=[("esink", 0)])
    S.dma("sp", esink[64:128, :], sink_d[4:8].partition_broadcast(64), w=[("esink", 1)])
    S.op("act", lambda e: e.activation(out=esink[:], in_=esink[:], func=AF.Exp),
         r=[("esink", 0), ("esink", 1)], w=[("esink", 0), ("esink", 1)])
    S.op("pool", lambda e: e.memset(O01[:], 0.0), w=[("O01",)])
    S.op("pool", lambda e: e.memset(O01[:, 0, 0:64], 1.0), w=[("O01",)])
    S.op("pool", lambda e: e.memset(O01[:, 1, 64:128], 1.0), w=[("O01",)])
    S.op("pool", lambda e: e.memset(Vz[:], 0.0), w=[("Vz",)])
    cnt = {"s": 0, "e": 0, "g": 0}
    for ty in range(2):
        S.dma("sp", QT[:], scr["qkt"][ty], w=[("QT",)])
        S.dma("sp", Vtok[:], scr["v"][ty].rearrange("(t p) n -> p t n", p=128), w=[("Vtok",)])
        S.op("pool", lambda e: e.tensor_copy(out=Vz[:, :, 0, 0:64], in_=Vtok[:, :, 0:64]),
             r=[("Vtok",)], w=[("Vz",)])
        S.op("pool", lambda e: e.tensor_copy(out=Vz[:, :, 1, 64:128], in_=Vtok[:, :, 64:128]),
             r=[("Vtok",)], w=[("Vz",)])
        if ty == 0:
            its = [(j, qc, kb, g) for j in range(4) for qc in range(4) for kb in range(16) for g in range(2)]
            LA = 3
            slots = {}

            def emit_s(i):
                j, qc, kb, g = its[i]
                s2 = cnt["s"] % 4
                cnt["s"] += 1
                es = cnt["e"] % NE
                cnt["e"] += 1
                slots[i] = es
                S.op("pe", lambda e: e.matmul(
                    st[s2][:], lhsT=QT[g * 64:(g + 1) * 64, 4, kb * 128:(kb + 1) * 128],
                    rhs=QT[g * 64:(g + 1) * 64, j, qc * 512:(qc + 1) * 512], start=True, stop=True),
                    r=[("QT",)], w=[("st", s2)])
                S.op("act", lambda e: e.activation(out=E[:, es, :], in_=st[s2][:], func=AF.Exp, scale=0.125),
                     r=[("st", s2)], w=[("E", es)])

            def emit_pv(i):
                j, qc, kb, g = its[i]
                es = slots.pop(i)
                first = (kb == 0 and g == 0)
                last = (kb == 15 and g == 1)
                if first:
                    cnt["g"] += 1
                g2 = cnt["g"] % 2
                S.op("pe", lambda e: e.matmul(pn[g2][:], lhsT=Vz[:, kb, g, :], rhs=E[:, es, :], start=first, stop=last),
                     r=[("Vz",), ("E", es)], w=[("pn", g2)])
                S.op("pe", lambda e: e.matmul(pdn[g2][:], lhsT=O01[:, g, :], rhs=E[:, es, :], start=first, stop=last),
                     r=[("O01",), ("E", es)], w=[("pdn", g2)])
                if last:
                    S.op("dve", lambda e: e.reciprocal(out=rden[:, g2, :], in_=pdn[g2][:]),
                         r=[("pdn", g2)], w=[("rden", g2)])
                    S.op("dve", lambda e: e.tensor_tensor(
                        out=yT[:, j, qc * 512:(qc + 1) * 512], in0=pn[g2][:], in1=rden[:, g2, :], op=ALU.mult),
                        r=[("pn", g2), ("rden", g2)], w=[("yT", j)])

            nb = len(its) // 2
            for b in range(nb + 1):
                if b < nb:
                    emit_s(2 * b + 1)
                    emit_s(2 * b)
                if b - 1 >= 0:
                    bb = b - 1
                    if its[2 * bb + 1][2] == 15 and its[2 * bb + 1][3] == 1:
                        emit_pv(2 * bb)
                        emit_pv(2 * bb + 1)
                    elif its[2 * bb][2] == 0 and its[2 * bb][3] == 0:
                        emit_pv(2 * bb)
                        emit_pv(2 * bb + 1)
                    else:
                        emit_pv(2 * bb + 1)
                        emit_pv(2 * bb)
        else:
            for j in range(4):
                emslot = {}

                def make_em(kb, j=j):
                    lo = max(kb - 1, 0)
                    hi = min(kb + 1, 15)
                    n = (hi - lo + 1) * 128
                    moff = 128 if kb == 0 else 0
                    for g in range(2):
                        s2 = cnt["s"] % 4
                        cnt["s"] += 1
                        es = cnt["e"] % NE
                        cnt["e"] += 1
                        ms = (kb % 4) * 2 + g
                        S.op("pe", lambda e, g=g, s2=s2: e.matmul(
                            st[s2][:, 0:n], lhsT=QT[g * 64:(g + 1) * 64, 4, kb * 128:(kb + 1) * 128],
                            rhs=QT[g * 64:(g + 1) * 64, j, lo * 128:lo * 128 + n], start=True, stop=True),
                            r=[("QT",)], w=[("st", s2)])
                        S.op("act", lambda e, s2=s2, es=es: e.activation(
                            out=E[:, es, 0:n], in_=st[s2][:, 0:n], func=AF.Exp, scale=0.125),
                            r=[("st", s2)], w=[("E", es)])
                        S.op("dve", lambda e, es=es, ms=ms: e.tensor_tensor(
                            out=Em[:, ms, 0:n], in0=E[:, es, 0:n], in1=wam[:, moff:moff + n], op=ALU.mult),
                            r=[("E", es), ("wam",)], w=[("Em", ms)])
                    emslot[kb] = lo

                make_em(0)
                make_em(1)
                for qg in range(4):
                    cnt["g"] += 1
                    g2 = cnt["g"] % 2
                    for qq in range(4):
                        qb = qg * 4 + qq
                        if qb + 2 <= 15:
                            make_em(qb + 2)
                        kbs = [kb for kb in (qb - 1, qb, qb + 1) if 0 <= kb <= 15]
                        nmm = len(kbs) * 2
                        for (use_v, pst, nm) in ((True, pn, "pn"), (False, pdn, "pdn")):
                            ii = 0
                            for kb in kbs:
                                co = (qb - emslot[kb]) * 128
                                for g in range(2):
                                    ms = (kb % 4) * 2 + g
                                    lt = Vz[:, kb, g, :] if use_v else O01[:, g, :]
                                    S.op("pe", lambda e, lt=lt, ms=ms, co=co, pst=pst, g2=g2, qq=qq, ii=ii, nmm=nmm:
                                         e.matmul(pst[g2][:, qq * 128:(qq + 1) * 128], lhsT=lt,
                                                  rhs=Em[:, ms, co:co + 128], start=(ii == 0), stop=(ii == nmm - 1)),
                                         r=[("Vz",), ("O01",), ("Em", ms)], w=[(nm, g2)])
                                    ii += 1
                    S.op("dve", lambda e, g2=g2, j=j: e.tensor_scalar(
                        out=rden[:, g2, :], in0=pdn[g2][:], scalar1=esink[:, j:j + 1], scalar2=None, op0=ALU.add),
                        r=[("pdn", g2), ("esink", 0), ("esink", 1)], w=[("rden", g2)])
                    S.op("dve", lambda e, g2=g2: e.reciprocal(out=rden[:, g2, :], in_=rden[:, g2, :]),
                         r=[("rden", g2)], w=[("rden", g2)])
                    S.op("dve", lambda e, g2=g2, j=j, qg=qg: e.tensor_tensor(
                        out=yT[:, j, qg * 512:(qg + 1) * 512], in0=pn[g2][:], in1=rden[:, g2, :], op=ALU.mult),
                        r=[("pn", g2), ("rden", g2)], w=[("yT", j)])
        S.dma("sp", scr["yT"][ty + 1], yT[:], r=[("yT", j) for j in range(4)], w=[("scr_yT", ty + 1)])
    B.end()


def merge_phase(B, x_d, tok0, win_d, wo_ds, wout_d, scr, x_out=None):
    S = B.S
    if x_out is None:
        x_out = x_d
    B.begin()
    wg = B.sb("wg", [128, 8, 3072], BF16)
    wo = B.sb("wo", [128, 3, 4, D], BF16)
    wout = B.sb("wout", [128, 8, D], BF16)
    hTc = B.sb("hTc", [128, 2, 8, 512], BF16)
    y3 = B.sb("y3", [128, 2, 3, 4, 512], BF16)
    sgm = B.sb("sgm", [128, 2, 512], F32)
    acc = B.sb("acc", [128, 8, 512], F32)
    tmp = B.sb("tmp", [128, 2, 512], F32)
    mT = B.sb("mT", [128, 2, 8, 512], BF16)
    xr = B.sb("xr", [128, 2, D], F32)
    xo = B.sb("xo", [128, 2, D], F32)
    pg = [B.ps("pg0", [128, 512]), B.ps("pg1", [128, 512])]
    pb = [B.ps("pb0", [128, 512]), B.ps("pb1", [128, 512])]
    po = [B.ps("po0", [128, 512]), B.ps("po1", [128, 512])]
    win_v = win_d.rearrange("(kc p) n -> p kc n", p=128)
    S.dma("sp", hTc[:, 0, :, :], scr["hT"][:, :, 0:512], w=[("hTc", 0)])
    for i in range(3):
        S.dma("sp", y3[:, 0, i, :, :], scr["yT"][i][:, :, 0:512], w=[("y3", 0, i)])
    for i in range(3):
        if i == 0:
            S.dma("pool", wo[:, 0, :, :], wo_ds[0].rearrange("(c p) n -> p c n", p=128), w=[("wo", 0)])
        else:
            wv = wo_ds[i].rearrange("(g j d) n -> g d j n", g=2, j=4)
            for g in range(2):
                S.dma("pool", wo[g * 64:(g + 1) * 64, i, :, :], wv[g], w=[("wo", i)])
        for hf in range(2):
            c0 = i * 1024 + hf * 512
            S.dma("pool", wg[:, :, c0:c0 + 512], win_v[:, :, 3456 + c0:3456 + c0 + 512], w=[("wg", i, hf)])
    S.dma("pool", wout[:], wout_d.rearrange("(c p) n -> p c n", p=128), w=[("wout",)])
    ei = 0
    for tb in range(4):
        cs = tb % 2
        if tb > 0:
            S.dma("sp", hTc[:, cs, :, :], scr["hT"][:, :, tb * 512:(tb + 1) * 512], w=[("hTc", cs)])
            for i in range(3):
                S.dma("sp", y3[:, cs, i, :, :], scr["yT"][i][:, :, tb * 512:(tb + 1) * 512], w=[("y3", cs, i)])
        for i in range(3):
            for oc in range(8):
                e2 = ei % 2
                ei += 1
                for kc in range(8):
                    S.op("pe", lambda e, kc=kc, i=i, oc=oc, cs=cs, e2=e2: e.matmul(
                        pg[e2][:], lhsT=wg[:, kc, i * 1024 + oc * 128:i * 1024 + (oc + 1) * 128],
                        rhs=hTc[:, cs, kc, :], start=(kc == 0), stop=(kc == 7)),
                        r=[("wg", i, oc // 4), ("hTc", cs)], w=[("pg", e2)])
                for c in range(4):
                    S.op("pe", lambda e, c=c, i=i, oc=oc, cs=cs, e2=e2: e.matmul(
                        pb[e2][:], lhsT=wo[:, i, c, oc * 128:(oc + 1) * 128], rhs=y3[:, cs, i, c, :],
                        start=(c == 0), stop=(c == 3)),
                        r=[("wo", i), ("y3", cs, i)], w=[("pb", e2)])
                S.op("act", lambda e, e2=e2: e.activation(out=sgm[:, e2, :], in_=pg[e2][:], func=AF.Sigmoid),
                     r=[("pg", e2)], w=[("sgm", e2)])
                if i == 0:
                    S.op("dve", lambda e, e2=e2, oc=oc: e.tensor_tensor(
                        out=acc[:, oc, :], in0=pb[e2][:], in1=sgm[:, e2, :], op=ALU.mult),
                        r=[("pb", e2), ("sgm", e2)], w=[("acc", oc)])
                elif i == 1:
                    S.op("dve", lambda e, e2=e2: e.tensor_tensor(
                        out=tmp[:, e2, :], in0=pb[e2][:], in1=sgm[:, e2, :], op=ALU.mult),
                        r=[("pb", e2), ("sgm", e2)], w=[("tmp", e2)])
                    S.op("pool", lambda e, e2=e2, oc=oc: e.tensor_tensor(
                        out=acc[:, oc, :], in0=acc[:, oc, :], in1=tmp[:, e2, :], op=ALU.add),
                        r=[("acc", oc), ("tmp", e2)], w=[("acc", oc)])
                else:
                    S.op("dve", lambda e, e2=e2: e.tensor_tensor(
                        out=tmp[:, e2, :], in0=pb[e2][:], in1=sgm[:, e2, :], op=ALU.mult),
                        r=[("pb", e2), ("sgm", e2)], w=[("tmp", e2)])
                    S.op("pool", lambda e, e2=e2, cs=cs, oc=oc: e.tensor_tensor(
                        out=mT[:, cs, oc, :], in0=acc[:, oc, :], in1=tmp[:, e2, :], op=ALU.add),
                        r=[("acc", oc), ("tmp", e2)], w=[("mT", cs, oc)])
        for tt in range(4):
            r0 = tok0 + tb * 512 + tt * 128
            xs = tt % 2
            S.dma("sp", xr[:, xs, :], x_d[r0:r0 + 128, :], r=[("xres", "x", r0)], w=[("xr", xs)])
            for nh in range(2):
                e2 = ei % 2
                ei += 1
                for oc in range(8):
                    S.op("pe", lambda e, oc=oc, tt=tt, nh=nh, cs=cs, e2=e2: e.matmul(
                        po[e2][:], lhsT=mT[:, cs, oc, tt * 128:(tt + 1) * 128],
                        rhs=wout[:, oc, nh * 512:(nh + 1) * 512], start=(oc == 0), stop=(oc == 7)),
                        r=[("mT", cs, oc), ("wout",)], w=[("po", e2)])
                S.op("dve", lambda e, e2=e2, xs=xs, nh=nh: e.tensor_tensor(
                    out=xo[:, xs, nh * 512:(nh + 1) * 512], in0=po[e2][:], in1=xr[:, xs, nh * 512:(nh + 1) * 512],
                    op=ALU.add), r=[("po", e2), ("xr", xs)], w=[("xo", xs)])
            S.dma("sp", x_out[r0:r0 + 128, :], xo[:, xs, :], r=[("xo", xs)], w=[("xres", "x", r0)])
    B.end()


def declare_rwkv_scratch(B, scr):
    scr["AR"] = B.dram_tmp("scr_AR", [4, 2, 128, 2 * SEQ], BF16)
    scr["BK"] = B.dram_tmp("scr_BK", [4, 2, 128, 2, SEQ], BF16)
    scr["PC"] = B.dram_tmp("scr_PC", [4, 2, 128, 32], F32)
    scr["vT"] = B.dram_tmp("scr_vT", [4, 128, SEQ], BF16)
    scr["bonus"] = B.dram_tmp("scr_bonus", [4, 128, SEQ], F32)
    scr["gate"] = B.dram_tmp("scr_gate", [4, 128, SEQ], BF16)


def rwkvprep_phase(B, Wl, C, scr):
    S = B.S
    B.begin()
    bones = B.sb("bones", [128, 128], BF16)
    S.dma("sp", bones[:], C["bones"], w=[("bones",)])
    tiny = B.sb("tiny", [128, 1], F32)
    S.op("pool", lambda e: e.memset(tiny[:], 1e-24), w=[("tiny",)])
    resetF = B.sb("resetF", [128, SEQ], BF16)
    resetB = B.sb("resetB", [128, SEQ], BF16)
    S.op("pool", lambda e: e.memset(resetF[:], 1.0), w=[("resetF",)])
    S.op("pool", lambda e: e.memset(resetF[:].rearrange("p (n t) -> p n t", t=64)[:, :, 0:1], 0.0), w=[("resetF",)])
    S.op("pool", lambda e: e.memset(resetB[:], 1.0), w=[("resetB",)])
    S.op("pool", lambda e: e.memset(resetB[:].rearrange("p (n t) -> p n t", t=64)[:, :, 63:64], 0.0), w=[("resetB",)])
    w0T = load_colvec(B, "w0T", Wl["w0"].rearrange("d n -> (d n)"), 8)
    a0T = load_colvec(B, "a0T", Wl["a0"].rearrange("d n -> (d n)"), 8)
    kkT = load_colvec(B, "kkT", Wl["k_k"], 4)
    kaT = load_colvec(B, "kaT", Wl["k_a"], 4)
    rkT = load_colvec(B, "rkT", Wl["r_k"].rearrange("h d -> (h d)"), 4)
    omka = B.sb("omka", [128, 4], F32)
    tomk = B.sb("tomk", [128, 4], F32)
    S.op("dve", lambda e: e.tensor_scalar(out=omka[:], in0=kaT[:], scalar1=-1.0, scalar2=1.0, op0=ALU.mult,
                                          op1=ALU.add), r=[("kaT",)], w=[("omka",)])
    S.op("dve", lambda e: e.tensor_scalar(out=tomk[:], in0=kaT[:], scalar1=-2.0, scalar2=2.0, op0=ALU.mult,
                                          op1=ALU.add), r=[("kaT",)], w=[("tomk",)])
    w2sb = B.sb("w2sb", [128, 512], BF16)
    a2sb = B.sb("a2sb", [128, 512], BF16)
    g2sb = B.sb("g2sb", [128, 512], BF16)
    S.dma("pool", w2sb[:], Wl["w2"].rearrange("d k n -> (d k) n"), w=[("w2sb",)])
    S.dma("pool", a2sb[:], Wl["a2"].rearrange("d k n -> (d k) n"), w=[("a2sb",)])
    S.dma("pool", g2sb[:], Wl["g2"], w=[("g2sb",)])
    ld = B.sb("ld", [128, SEQ], F32)
    twl = B.sb("twl", [128, SEQ], BF16)
    alb = B.sb("alb", [128, SEQ], BF16)
    sgl = B.sb("sgl", [128, SEQ], BF16)
    for (ch, dst, fn, nm) in ((12, twl, AF.Tanh, "twl"), (13, alb, AF.Copy, "alb"), (14, sgl, AF.Sigmoid, "sgl")):
        S.dma("sp", ld[:], scr["psh"][ch], r=[("scr_psh", ch)], w=[("ld",)])
        S.op("act", lambda e, dst=dst, fn=fn: e.activation(out=dst[:], in_=ld[:], func=fn),
             r=[("ld",)], w=[(nm,)])
    rkv = B.sb("rkv", [128, 2, 3, SEQ], F32)

    def load_rkv(cc):
        p_ = cc % 2
        for i_, nm in enumerate(("r_", "k_", "v_")):
            S.dma("sp", rkv[:, p_, i_, :], scr["psh"][4 * i_ + cc], r=[("scr_psh", 4 * i_ + cc)], w=[(nm, p_)])
    kk = B.sb("kk", [128, SEQ], F32)
    sqb = B.sb("sqb", [128, SEQ], BF16)
    vbf = B.sb("vbf", [128, SEQ], BF16)
    gbf = B.sb("gbf", [128, SEQ], BF16)
    a_ = B.sb("a_", [128, 2, SEQ], F32)
    sig = B.sb("sig", [128, SEQ], F32)
    cs = B.sb("cs", [128, SEQ], F32)
    T_ = B.sb("T_", [128, SEQ], F32)
    X = B.sb("X", [128, 2, SEQ], F32)
    bd = B.sb("bd", [128, SEQ], F32)
    kd = B.sb("kd", [128, SEQ], F32)
    ARo = B.sb("ARo", [128, 32, 2, 64], BF16)
    BKo = B.sb("BKo", [128, 2, SEQ], BF16)
    pc = B.sb("pc", [128, 32], F32)
    pp = [B.ps("pp0", [128, 512]), B.ps("pp1", [128, 512])]
    pi = [0]

    def mm4(lhsT, rhs_fn, evac, rkeys):
        for tb in range(4):
            b2 = pi[0] % 2
            pi[0] += 1
            S.op("pe", lambda e, tb=tb, b2=b2: e.matmul(pp[b2][:], lhsT=lhsT, rhs=rhs_fn(tb), start=True, stop=True),
                 r=rkeys, w=[("pp", b2)])
            evac(tb, pp[b2], ("pp", b2))

    def v3(ap):
        return ap.rearrange("p (n t) -> p n t", t=64)

    def do_pair(c, r_, k_, v_):
        cpar = c % 2
        cc = slice(c * 128, (c + 1) * 128)
        if c == 0:
            load_rkv(0)
        if c + 1 < 4:
            load_rkv(c + 1)
        S.op("act", lambda e: e.activation(out=vbf[:], in_=v_[:], func=AF.Copy), r=[("v_", cpar)], w=[("vbf",)])
        S.dma("sp", scr["vT"][c], vbf[:], r=[("vbf",)], w=[("scr_vT", c)])
        S.op("dve", lambda e, c=c: e.tensor_scalar(out=kk[:], in0=k_[:], scalar1=kkT[:, c:c + 1], scalar2=None,
                                                   op0=ALU.mult), r=[("k_", cpar), ("kkT",)], w=[("kk",)])
        S.op("act", lambda e: e.activation(out=sqb[:], in_=kk[:], func=AF.Square), r=[("kk",)], w=[("sqb",)])
        mm4(bones[:], lambda tb: sqb[:, tb * 512:(tb + 1) * 512],
            lambda tb, ps, bk: S.op("act", lambda e, tb=tb, ps=ps: e.activation(
                out=X[:, 0, tb * 512:(tb + 1) * 512], in_=ps[:], func=AF.Ln, bias=tiny[:, 0:1]),
                r=[bk, ("tiny",)], w=[("X", 0)]), [("bones",), ("sqb",)])
        S.op("act", lambda e: e.activation(out=X[:, 0, :], in_=X[:, 0, :], func=AF.Exp, scale=-0.5),
             r=[("X", 0)], w=[("X", 0)])
        S.op("dve", lambda e: e.tensor_tensor(out=kk[:], in0=kk[:], in1=X[:, 0, :], op=ALU.mult),
             r=[("kk",), ("X", 0)], w=[("kk",)])
        for d in range(2):
            dd = slice(d * 64, (d + 1) * 64)
            mm4(a2sb[dd, cc], lambda tb, dd=dd: alb[dd, tb * 512:(tb + 1) * 512],
                lambda tb, ps, bk, d=d, c=c: S.op("act", lambda e, tb=tb, ps=ps: e.activation(
                    out=a_[:, d, tb * 512:(tb + 1) * 512], in_=ps[:], func=AF.Sigmoid,
                    bias=a0T[:, d * 4 + c:d * 4 + c + 1]), r=[bk, ("a0T",)], w=[("a_", d)]),
                [("a2sb",), ("alb",)])
        mm4(g2sb[:, cc], lambda tb: sgl[:, tb * 512:(tb + 1) * 512],
            lambda tb, ps, bk: S.op("act", lambda e, tb=tb, ps=ps: e.activation(
                out=gbf[:, tb * 512:(tb + 1) * 512], in_=ps[:], func=AF.Copy), r=[bk], w=[("gbf",)]),
            [("g2sb",), ("sgl",)])
        S.dma("sp", scr["gate"][c], gbf[:], r=[("gbf",)], w=[("scr_gate", c)])
        S.op("dve", lambda e: e.tensor_tensor(out=T_[:], in0=a_[:, 0, :], in1=a_[:, 1, :], op=ALU.add),
             r=[("a_", 0), ("a_", 1)], w=[("T_",)])
        S.op("dve", lambda e, c=c: e.tensor_scalar(out=T_[:], in0=T_[:], scalar1=kaT[:, c:c + 1],
                                                   scalar2=tomk[:, c:c + 1], op0=ALU.mult, op1=ALU.add),
             r=[("T_",), ("kaT",), ("tomk",)], w=[("T_",)])
        S.op("dve", lambda e: e.tensor_tensor(out=T_[:], in0=T_[:], in1=r_[:], op=ALU.mult),
             r=[("T_",), ("r_", cpar)], w=[("T_",)])
        S.op("dve", lambda e: e.tensor_tensor(out=T_[:], in0=T_[:], in1=k_[:], op=ALU.mult),
             r=[("T_",), ("k_", cpar)], w=[("T_",)])
        S.op("dve", lambda e, c=c: e.tensor_scalar(out=sqb[:], in0=T_[:], scalar1=rkT[:, c:c + 1], scalar2=None,
                                                   op0=ALU.mult), r=[("T_",), ("rkT",)], w=[("sqb",)])
        mm4(bones[:], lambda tb: sqb[:, tb * 512:(tb + 1) * 512],
            lambda tb, ps, bk: S.op("dve", lambda e, tb=tb, ps=ps: e.tensor_tensor(
                out=X[:, 1, tb * 512:(tb + 1) * 512], in0=ps[:], in1=v_[:, tb * 512:(tb + 1) * 512], op=ALU.mult),
                r=[bk, ("v_", cpar)], w=[("X", 1)]), [("bones",), ("sqb",)])
        S.dma("sp", scr["bonus"][c], X[:, 1, :], r=[("X", 1)], w=[("scr_bonus", c)])
        for d in range(2):
            dd = slice(d * 64, (d + 1) * 64)
            mm4(w2sb[dd, cc], lambda tb, dd=dd: twl[dd, tb * 512:(tb + 1) * 512],
                lambda tb, ps, bk, d=d, c=c: S.op("act", lambda e, tb=tb, ps=ps: e.activation(
                    out=sig[:, tb * 512:(tb + 1) * 512], in_=ps[:], func=AF.Sigmoid,
                    bias=w0T[:, d * 4 + c:d * 4 + c + 1]), r=[bk, ("w0T",)], w=[("sig",)]),
                [("w2sb",), ("twl",)])
            if d == 0:
                S.op("dve", lambda e: e.tensor_tensor_scan(out=cs[:], data0=resetF[:], data1=sig[:], initial=0.0,
                                                           op0=ALU.mult, op1=ALU.add),
                     r=[("resetF",), ("sig",)], w=[("cs",)])
                eidx = 63
            else:
                S.op("dve", lambda e: e.tensor_tensor_scan(out=cs[:, ::-1], data0=resetB[:, ::-1],
                                                           data1=sig[:, ::-1], initial=0.0,
                                                           op0=ALU.mult, op1=ALU.add),
                     r=[("resetB",), ("sig",)], w=[("cs",)])
                eidx = 0
            S.op("act", lambda e, eidx=eidx: e.activation(out=pc[:], in_=v3(cs[:])[:, :, eidx], func=AF.Exp,
                                                          scale=-LAM), r=[("cs",)], w=[("pc",)])
            S.dma("sp", scr["PC"][c][d], pc[:], r=[("pc",)], w=[("scr_PC", c, d)])
            S.op("act", lambda e: e.activation(out=X[:, 0, :], in_=cs[:], func=AF.Exp, scale=-LAM),
                 r=[("cs",)], w=[("X", 0)])
            S.op("dve", lambda e: e.tensor_tensor(out=ARo[:, :, 1, :], in0=v3(r_[:]), in1=v3(X[:, 0, :]), op=ALU.mult),
                 r=[("r_", cpar), ("X", 0)], w=[("ARo",)])
            S.op("dve", lambda e: e.tensor_tensor(out=T_[:], in0=cs[:], in1=sig[:], op=ALU.subtract),
                 r=[("cs",), ("sig",)], w=[("T_",)])
            S.op("act", lambda e: e.activation(out=X[:, 1, :], in_=T_[:], func=AF.Exp, scale=-LAM),
                 r=[("T_",)], w=[("X", 1)])
            S.op("dve", lambda e: e.scalar_tensor_tensor(out=ARo[:, :, 0, :], in0=v3(kk[:]), scalar=-1.0,
                                                         in1=v3(X[:, 1, :]), op0=ALU.mult, op1=ALU.mult),
                 r=[("kk",), ("X", 1)], w=[("ARo",)])
            S.dma("sp", scr["AR"][c][d], ARo[:].rearrange("p n a t -> p (n a t)"), r=[("ARo",)],
                  w=[("scr_AR", c, d)])
            S.op("dve", lambda e, d=d: e.tensor_tensor(out=bd[:], in0=kk[:], in1=a_[:, d, :], op=ALU.mult),
                 r=[("kk",), ("a_", d)], w=[("bd",)])
            S.op("dve", lambda e, d=d, c=c: e.tensor_scalar(out=kd[:], in0=a_[:, d, :], scalar1=kaT[:, c:c + 1],
                                                            scalar2=omka[:, c:c + 1], op0=ALU.mult, op1=ALU.add),
                 r=[("a_", d), ("kaT",), ("omka",)], w=[("kd",)])
            S.op("pool", lambda e: e.tensor_tensor(out=kd[:], in0=kd[:], in1=k_[:], op=ALU.mult),
                 r=[("kd",), ("k_", cpar)], w=[("kd",)])
            S.op("act", lambda e: e.activation(out=X[:, 0, :], in_=cs[:], func=AF.Exp, scale=LAM),
                 r=[("cs",)], w=[("X", 0)])
            S.op("dve", lambda e: e.tensor_tensor(out=BKo[:, 0, :], in0=bd[:], in1=X[:, 0, :], op=ALU.mult),
                 r=[("bd",), ("X", 0)], w=[("BKo", 0)])
            S.op("pool", lambda e: e.tensor_tensor(out=BKo[:, 1, :], in0=kd[:], in1=X[:, 0, :], op=ALU.mult),
                 r=[("kd",), ("X", 0)], w=[("BKo", 1)])
            S.dma("sp", scr["BK"][c][d], BKo[:], r=[("BKo", i) for i in range(2)], w=[("scr_BK", c, d)])
    for c in range(4):
        do_pair(c, rkv[:, c % 2, 0, :], rkv[:, c % 2, 1, :], rkv[:, c % 2, 2, :])
    B.end()


NSLOT = 4
LNX_EPS = 64 * 1e-5


def rwkvscan_phase(B, Wl, C, scr):
    S = B.S
    B.begin()
    load_consts(B, C["ident"])
    ident = B.T["ident"]
    mk = B.sb("mk", [128, 2, 640], F32)
    S.dma("sp", mk[:, 0, :], C["mkf"], w=[("mk",)])
    S.dma("sp", mk[:, 1, :], C["mkb"], w=[("mk",)])
    m05f = B.sb("m05f", [128, 1], F32)
    S.op("pool", lambda e: e.memset(m05f[:], -0.5), w=[("m05f",)])
    lgT = load_colvec(B, "lgT", Wl["lnx_g"], 4)
    lbT = load_colvec(B, "lbT", Wl["lnx_b"], 4)
    AR2 = B.sb("AR", [128, 2, 2, 32, 2, 64], BF16)
    BK2 = B.sb("BK", [128, 2, 2, 2, SEQ], BF16)
    HP = B.sb("HP", [128, 2, 64], F32)
    PCs2 = B.sb("PCs", [128, 2, 2, 32], F32)
    vT2 = B.sb("vT", [128, 2, SEQ], BF16)
    Vst = B.sb("Vst", [128, 32, 64], BF16)
    MKs = B.sb("MKs", [128, 2, NSLOT, 640], BF16)
    TT0 = B.sb("TT0", [128, 2, NSLOT, 128], BF16)
    RB = B.sb("RB", [128, 2, NSLOT, 2, 384], BF16)
    TTf = B.sb("TTf", [128, 2, NSLOT, 128], BF16)
    BKs = B.sb("BKs", [128, 2, NSLOT, 256], BF16)
    wkv = B.sb("wkv", [128, 32, 64], F32)
    wkvd = B.sb("wkvd", [128, 2, 32, 64], F32)
    H32 = B.sb("H32", [128, 2, 64], F32)
    Hbf = B.sb("Hbf", [128, 2, 2, 64], BF16)
    Wsb = B.sb("Wsb", [128, 2, 64], BF16)
    Usb = B.sb("Usb", [128, 2, 64], BF16)
    sqw = B.sb("sqw", [128, 32, 64], F32)
    lnb = B.sb("lnb", [128, 32, 64], BF16)
    st4 = B.sb("st4", [128, 4, 32], F32)
    bon2 = B.sb("bon", [128, 2, SEQ], F32)
    gat2 = B.sb("gat", [128, 2, SEQ], BF16)
    yf = B.sb("yf", [128, 2, 1024], F32)
    yTa = B.sb("yTa", [128, SEQ], BF16)
    psMK = B.ps("psMK", [128, 1024])
    psR = [B.ps("psR0", [128, 512]), B.ps("psR1", [128, 512])]
    psWY = [B.ps("psWY0", [128, 512]), B.ps("psWY1", [128, 512])]
    psUH = [B.ps("psUH0", [128, 512]), B.ps("psUH1", [128, 512])]
    psMKb = psMK[:].bitcast(BF16)
    psRb = [psR[0][:].bitcast(BF16), psR[1][:].bitcast(BF16)]
    kRb = [("psR", 0), ("psR", 1)]
    KMK = [("psMK0",), ("psMK1",)]
    S.op("dve", lambda e: e.memset(psMK[:], 0.0), w=KMK)
    R = [slice(0, 64), slice(64, 128)]
    CP = [0]
    ARv = [None]
    BKv = [None]
    PCv = [None]
    SG = dict(skip_group_check=True)

    def chain(d, n):
        AR = ARv[0]
        PCs = PCv[0]
        cpk = CP[0]
        cn = n if d == 0 else 31 - n
        sl = n % NSLOT
        par = n % 2
        vk = ("Vst", cn // 16)
        kWY = ("psWY", d)
        kUH = ("psUH", d)
        psW = psWY[d][:, 0:64]
        psY = psWY[d][:, 64:128]
        psU = psUH[d][:, 0:64]
        psH = psUH[d][:, 64:128]
        S.op("pool", lambda e: e.tensor_scalar(out=HP[:, d, :], in0=H32[:, d, :], scalar1=PCs[:, d, cn:cn + 1],
                                               scalar2=None, op0=ALU.mult),
             r=[("H32", d), ("PCs", cpk, d)], w=[("HP", d)])
        S.op("pe", lambda e: e.matmul(psW, lhsT=MKs[:, d, sl, 256:384], rhs=Vst[:, cn, :],
                                      start=True, stop=False, **SG), r=[("MKs", d, sl), vk], w=[kWY])
        for h in range(2):
            S.op("pe", lambda e, h=h: e.matmul(psWY[d][R[h], 0:64], lhsT=AR[R[h], d, cn, 0, :],
                                               rhs=Hbf[R[h], d, par, :], start=False, stop=(h == 1), **SG),
                 r=[("AR", cpk, d), ("Hbf", d, par)], w=[kWY])
        S.op("act", lambda e: e.activation(out=Wsb[:, d, :], in_=psW, func=AF.Copy),
             r=[kWY], w=[("Wsb", d)])
        yield
        S.op("pe", lambda e: e.matmul(psU, lhsT=TTf[:, d, sl, :], rhs=Wsb[:, d, :], start=True, stop=True),
             r=[("TTf", d, sl), ("Wsb", d)], w=[kUH])
        S.op("dve", lambda e: e.tensor_copy(out=Usb[:, d, :], in_=psU), r=[kUH], w=[("Usb", d)])
        yield
        S.op("pe", lambda e: e.matmul(psH, lhsT=BKs[:, d, sl, 0:128], rhs=Usb[:, d, :],
                                      start=True, stop=False), r=[("BKs", d, sl), ("Usb", d)], w=[kUH])
        S.op("pe", lambda e: e.matmul(psH, lhsT=BKs[:, d, sl, 128:256], rhs=Vst[:, cn, :],
                                      start=False, stop=True), r=[("BKs", d, sl), vk], w=[kUH])
        S.op("pe", lambda e: e.matmul(psY, lhsT=MKs[:, d, sl, 128:256], rhs=Usb[:, d, :],
                                      start=True, stop=False, **SG), r=[("MKs", d, sl), ("Usb", d)], w=[kWY])
        S.op("pe", lambda e: e.matmul(psY, lhsT=MKs[:, d, sl, 384:512], rhs=Vst[:, cn, :],
                                      start=False, stop=False, **SG), r=[("MKs", d, sl), vk], w=[kWY])
        for h in range(2):
            S.op("pe", lambda e, h=h: e.matmul(psWY[d][R[h], 64:128], lhsT=AR[R[h], d, cn, 1, :],
                                               rhs=Hbf[R[h], d, par, :], start=False, stop=(h == 1), **SG),
                 r=[("AR", cpk, d), ("Hbf", d, par)], w=[kWY])
        S.op("dve", lambda e: e.scalar_tensor_tensor(out=Hbf[:, d, 1 - par, :], in0=psH,
                                                     scalar=PCs[:, d, cn:cn + 1], in1=HP[:, d, :],
                                                     op0=ALU.mult, op1=ALU.add),
             r=[("HP", d), ("PCs", cpk, d), kUH], w=[("Hbf", d, 1 - par)])
        S.op("dve", lambda e: e.scalar_tensor_tensor(out=H32[:, d, :], in0=psH,
                                                     scalar=PCs[:, d, cn:cn + 1], in1=HP[:, d, :],
                                                     op0=ALU.mult, op1=ALU.add),
             r=[("HP", d), ("PCs", cpk, d), kUH], w=[("H32", d)])
        S.op("act", lambda e: e.activation(out=wkvd[:, d, cn, :], in_=psY, func=AF.Copy),
             r=[kWY], w=[("wkvd", d, cn)])
        yield

    def pre(d, n):
        AR = ARv[0]
        BK = BKv[0]
        cpk = CP[0]
        cn = n if d == 0 else 31 - n
        sl = n % NSLOT
        ck = slice(cn * 64, (cn + 1) * 64)
        for h in range(2):
            for a in range(2):
                S.op("pe", lambda e, h=h, a=a: e.matmul(
                    psMK[R[h], a * 128 + h * 64:a * 128 + (h + 1) * 64], lhsT=BK[R[h], d, 0, ck],
                    rhs=AR[R[h], d, cn, a, :], start=True, stop=True), r=[("BK", cpk, d), ("AR", cpk, d)], w=[KMK[0]])
                S.op("pe", lambda e, h=h, a=a: e.matmul(
                    psMK[R[h], 256 + a * 128 + h * 64:256 + a * 128 + (h + 1) * 64], lhsT=BK[R[h], d, 1, ck],
                    rhs=AR[R[h], d, cn, a, :], start=True, stop=True), r=[("BK", cpk, d), ("AR", cpk, d)], w=[KMK[0]])
            S.op("pe", lambda e, h=h: e.matmul(
                psMK[R[h], 512 + h * 64:512 + (h + 1) * 64], lhsT=AR[R[h], d, cn, 0, :], rhs=BK[R[h], d, 0, ck],
                start=True, stop=True), r=[("BK", cpk, d), ("AR", cpk, d)], w=[KMK[1]])
            S.op("pe", lambda e, h=h: e.transpose(
                out=psMKb[R[h], 1280 + h * 64:1280 + (h + 1) * 64], in_=BK[R[h], d, 0, ck],
                identity=ident[R[h], R[h]]), r=[("BK", cpk, d), ("ident",)], w=[KMK[1]])
            S.op("pe", lambda e, h=h: e.transpose(
                out=psMKb[R[h], 1408 + h * 64:1408 + (h + 1) * 64], in_=BK[R[h], d, 1, ck],
                identity=ident[R[h], R[h]]), r=[("BK", cpk, d), ("ident",)], w=[KMK[1]])
        S.op("dve", lambda e: e.tensor_tensor(out=MKs[:, d, sl, :], in0=psMK[:, 0:640], in1=mk[:, d, :], op=ALU.mult),
             r=KMK + [("mk",)], w=[("MKs", d, sl)])
        S.op("dve", lambda e: e.tensor_copy(out=BKs[:, d, sl, :], in_=psMKb[:, 1280:1536]),
             r=[KMK[1]], w=[("BKs", d, sl)])
        S.op("pool", lambda e: e.tensor_tensor(out=TT0[:, d, sl, :], in0=MKs[:, d, sl, 0:128], in1=ident[:], op=ALU.add),
             r=[("MKs", d, sl), ("ident",)], w=[("TT0", d, sl)])
        yield
        for r in range(1, 7):
            if r == 1:
                Mp = MKs[:, d, sl, 512:640]
                MTp = MKs[:, d, sl, 0:128]
                srck = [("MKs", d, sl)]
            else:
                Mp = RB[:, d, sl, (r - 1) % 2, 0:128]
                MTp = RB[:, d, sl, (r - 1) % 2, 128:256]
                srck = [("RB", d, sl, (r - 1) % 2)]
            if r == 2:
                TTs = TT0[:, d, sl, :]
                srck = srck + [("TT0", d, sl)]
            elif r >= 3:
                TTs = RB[:, d, sl, (r - 1) % 2, 256:384]
            if r <= 5:
                S.op("pe", lambda e, Mp=Mp, MTp=MTp: e.matmul(psR[d][:, 0:128], lhsT=MTp, rhs=Mp, start=True, stop=True),
                     r=srck, w=[("psR", d)])
            if r == 1:
                S.op("pe", lambda e, Mp=Mp, MTp=MTp: e.matmul(psR[d][:, 128:256], lhsT=Mp, rhs=MTp, start=True, stop=True),
                     r=srck, w=[("psR", d)])
            elif r == 2:
                S.op("pe", lambda e, Mp=Mp, MTp=MTp: e.matmul(psR[d][:, 128:256], lhsT=Mp, rhs=MTp, start=True, stop=True),
                     r=srck, w=[("psR", d)])
                S.op("pe", lambda e, TTs=TTs, Mp=Mp: e.matmul(psR[d][:, 256:384], lhsT=Mp, rhs=TTs, start=True, stop=True),
                     r=srck, w=[("psR", d)])
            elif r <= 4:
                S.op("pe", lambda e, Mp=Mp, r=r: e.matmul(psR[d][:, 128:384], lhsT=Mp, rhs=RB[:, d, sl, (r - 1) % 2, 128:384],
                                                          start=True, stop=True), r=srck, w=[("psR", d)])
            else:
                S.op("pe", lambda e, TTs=TTs, Mp=Mp: e.matmul(psR[d][:, 256:384], lhsT=Mp, rhs=TTs, start=True, stop=True),
                     r=srck, w=[("psR", d)])
            if r <= 5:
                if r % 2 == 0:
                    S.op("act", lambda e, r=r: e.activation(out=RB[:, d, sl, r % 2, :], in_=psR[d][:, 0:384], func=AF.Copy),
                         r=[("psR", d)], w=[("RB", d, sl, r % 2)])
                else:
                    S.op("dve", lambda e, r=r: e.tensor_copy(out=RB[:, d, sl, r % 2, :], in_=psR[d][:, 0:384]),
                         r=[("psR", d)], w=[("RB", d, sl, r % 2)])
                if r >= 2:
                    S.op("pool", lambda e, r=r, TTs=TTs: e.tensor_tensor(
                        out=RB[:, d, sl, r % 2, 256:384], in0=RB[:, d, sl, r % 2, 256:384], in1=TTs, op=ALU.add),
                        r=[("RB", d, sl, r % 2)] + srck, w=[("RB", d, sl, r % 2)])
            else:
                S.op("act", lambda e: e.activation(out=TTf[:, d, sl, :], in_=psR[d][:, 256:384], func=AF.Copy),
                     r=[("psR", d)], w=[("TTf", d, sl)])
                S.op("pool", lambda e, TTs=TTs: e.tensor_tensor(out=TTf[:, d, sl, :], in0=TTf[:, d, sl, :], in1=TTs,
                                                                op=ALU.add),
                     r=[("TTf", d, sl)] + srck, w=[("TTf", d, sl)])
            yield

    def zipper(gens):
        gens = list(gens)
        while gens:
            nxt = []
            for g in gens:
                try:
                    next(g)
                    nxt.append(g)
                except StopIteration:
                    pass
            gens = nxt

    for c in range(getattr(B, "npairs", 4)):
        cp = c % 2
        CP[0] = cp
        ARv[0] = AR2[:, cp]
        BKv[0] = BK2[:, cp]
        PCv[0] = PCs2[:, cp]

        def load_pair(cc):
            pp_ = cc % 2
            for d in range(2):
                S.dma("sp", AR2[:, pp_, d, :, :, :].rearrange("p n a t -> p (n a t)"), scr["AR"][cc][d],
                      r=[("scr_AR", cc, d)], w=[("AR", pp_, d)])
                S.dma("sp", BK2[:, pp_, d, :, :], scr["BK"][cc][d], r=[("scr_BK", cc, d)], w=[("BK", pp_, d)])
                S.dma("sp", PCs2[:, pp_, d, :], scr["PC"][cc][d], r=[("scr_PC", cc, d)], w=[("PCs", pp_, d)])
                if d == 0:
                    S.dma("sp", vT2[:, pp_, :], scr["vT"][cc], r=[("scr_vT", cc)], w=[("vT", pp_)])
            S.dma("sp", bon2[:, pp_, :], scr["bonus"][cc], r=[("scr_bonus", cc)], w=[("bon", pp_)])
            S.dma("sp", gat2[:, pp_, :], scr["gate"][cc], r=[("scr_gate", cc)], w=[("gat", pp_)])

        if c == 0:
            load_pair(0)
        if c + 1 < getattr(B, "npairs", 4):
            pass
        if c + 1 < getattr(B, "npairs", 4):
            load_pair(c + 1)
        for half in range(2):
            for q in range(16):
                cn = half * 16 + q
                for h in range(2):
                    S.op("pe", lambda e, h=h, cn=cn, q=q, half=half, cp=cp: e.transpose(
                        out=psRb[half][R[h], q * 64:(q + 1) * 64], in_=vT2[R[h], cp, cn * 64:(cn + 1) * 64],
                        identity=ident[R[h], R[h]]), r=[("vT", cp), ("ident",)], w=[kRb[half]])
            S.op("act", lambda e, half=half: e.activation(
                out=Vst[:, half * 16:(half + 1) * 16, :].rearrange("p n v -> p (n v)"), in_=psRb[half][:, 0:1024],
                func=AF.Copy), r=[kRb[half]], w=[("Vst", half)])
        for d in range(2):
            S.op("pool", lambda e, d=d: e.memset(H32[:, d, :], 0.0), w=[("H32", d)])
            S.op("pool", lambda e, d=d: e.memset(Hbf[:, d, 0, :], 0.0), w=[("Hbf", d, 0)])
        zipper([pre(0, 0), pre(1, 0), pre(0, 1), pre(1, 1), pre(0, 2), pre(1, 2)])
        cont = []
        for n in range(32):
            newp = [pre(0, n + 3), pre(1, n + 3)] if n + 3 < 32 else []
            ch = [chain(0, n), chain(1, n)]
            order = []
            for d in range(2):
                order.append(ch[d])
                if newp:
                    order.append(newp[d])
            order += cont
            for _ in range(4):
                for g in order:
                    try:
                        next(g)
                    except StopIteration:
                        pass
            cont = newp
        for g in cont:
            for _ in range(8):
                try:
                    next(g)
                except StopIteration:
                    break
        allw = [("wkv",)]
        S.op("dve", lambda e: e.tensor_tensor(out=wkv[:], in0=wkvd[:, 0, :, :], in1=wkvd[:, 1, :, :], op=ALU.add),
             r=[("wkvd", d, n) for d in range(2) for n in range(32)], w=allw)
        S.op("dve", lambda e: e.tensor_reduce(out=st4[:, 0, :], in_=wkv[:], axis=AX.X, op=ALU.add),
             r=allw, w=[("st4", 0)])
        S.op("act", lambda e: e.activation(out=sqw[:].rearrange("p n v -> p (n v)"),
                                           in_=wkv[:].rearrange("p n v -> p (n v)"), func=AF.Square),
             r=allw, w=[("sqw",)])
        S.op("dve", lambda e: e.tensor_reduce(out=st4[:, 1, :], in_=sqw[:], axis=AX.X, op=ALU.add),
             r=[("sqw",)], w=[("st4", 1)])
        S.op("dve", lambda e: e.tensor_scalar(out=st4[:, 0, :], in0=st4[:, 0, :], scalar1=1.0 / 64, scalar2=None,
                                              op0=ALU.mult), r=[("st4", 0)], w=[("st4", 0)])
        S.op("dve", lambda e: e.tensor_tensor(out=st4[:, 2, :], in0=st4[:, 0, :], in1=st4[:, 0, :], op=ALU.mult),
             r=[("st4", 0)], w=[("st4", 2)])
        S.op("dve", lambda e: e.scalar_tensor_tensor(out=st4[:, 3, :], in0=st4[:, 1, :], scalar=1.0 / 64,
                                                     in1=st4[:, 2, :], op0=ALU.mult, op1=ALU.subtract),
             r=[("st4", 1), ("st4", 2)], w=[("st4", 3)])
        S.op("dve", lambda e: e.tensor_scalar(out=st4[:, 3, :], in0=st4[:, 3, :], scalar1=LNX_EPS, scalar2=None,
                                              op0=ALU.add), r=[("st4", 3)], w=[("st4", 3)])
        S.op("pool", lambda e: e.tensor_tensor(out=st4[:, 3, :], in0=st4[:, 3, :],
                                               in1=m05f[:, 0:1].to_broadcast([128, 32]), op=ALU.pow),
             r=[("st4", 3), ("m05f",)], w=[("st4", 3)])
        S.op("dve", lambda e: e.tensor_tensor(out=sqw[:], in0=wkv[:],
                                              in1=st4[:, 0, :].unsqueeze(2).to_broadcast([128, 32, 64]),
                                              op=ALU.subtract), r=allw + [("st4", 0)], w=[("sqw",)])
        S.op("dve", lambda e: e.tensor_tensor(out=lnb[:], in0=sqw[:],
                                              in1=st4[:, 3, :].unsqueeze(2).to_broadcast([128, 32, 64]),
                                              op=ALU.mult), r=[("sqw",), ("st4", 3)], w=[("lnb",)])
        for half in range(2):
            pb_ = psRb[half]
            pk = kRb[half]
            for q in range(16):
                n = half * 16 + q
                for h in range(2):
                    S.op("pe", lambda e, h=h, n=n, q=q, pb_=pb_: e.transpose(
                        out=pb_[R[h], q * 64:(q + 1) * 64], in_=lnb[R[h], n, :], identity=ident[R[h], R[h]]),
                        r=[("lnb",), ("ident",)], w=[pk])
            hs = slice(half * 1024, (half + 1) * 1024)
            S.op("dve", lambda e, half=half, pb_=pb_, c=c: e.tensor_scalar(
                out=yf[:, half, :], in0=pb_[:, 0:1024], scalar1=lgT[:, c:c + 1], scalar2=lbT[:, c:c + 1],
                op0=ALU.mult, op1=ALU.add), r=[pk, ("lgT",), ("lbT",)], w=[("yf", half)])
            S.op("pool", lambda e, half=half, hs=hs, cp=cp: e.tensor_tensor(out=yf[:, half, :], in0=yf[:, half, :],
                                                                           in1=bon2[:, cp, hs], op=ALU.add),
                 r=[("yf", half), ("bon", cp)], w=[("yf", half)])
            S.op("dve", lambda e, half=half, hs=hs, cp=cp: e.tensor_tensor(out=yTa[:, hs], in0=yf[:, half, :],
                                                                          in1=gat2[:, cp, hs], op=ALU.mult),
                 r=[("yf", half), ("gat", cp)], w=[("yTa", half)])
        S.dma("sp", scr["yT"][0][:, c, :], yTa[:], r=[("yTa", 0), ("yTa", 1)], w=[("scr_yT0", c)])
    B.end()


def finalnorm_phase(B, x_d, out_d, g_d, ntok):
    S = B.S
    B.begin()
    m05 = B.sb("m05", [128, 8], F32)
    S.op("pool", lambda e: e.memset(m05[:], -0.5), w=[("m05",)])
    gb = B.sb("gb", [128, D], F32)
    S.dma("sp", gb[:], g_d.partition_broadcast(128), w=[("gb",)])
    xt = B.sb("xt", [128, 2, D], F32)
    junk = B.sb("junk", [128, D], BF16)
    ss = B.sb("ss", [128, 8], F32)
    xo = B.sb("xo", [128, 2, D], F32)
    for t in range(ntok // 128):
        sl = t % 2
        c8 = t % 8
        r0 = t * 128
        S.dma("sp", xt[:, sl, :], x_d[r0:r0 + 128, :], w=[("xt", sl)])
        S.op("act", lambda e, sl=sl, c8=c8: e.activation(
            out=junk[:], in_=xt[:, sl, :], func=AF.Square, accum_out=ss[:, c8:c8 + 1]),
            r=[("xt", sl)], w=[("junk",), ("ss", c8)])
        S.op("dve", lambda e, c8=c8: e.tensor_scalar(
            out=ss[:, c8:c8 + 1], in0=ss[:, c8:c8 + 1], scalar1=1.0 / D, scalar2=1e-6,
            op0=ALU.mult, op1=ALU.add), r=[("ss", c8)], w=[("ss", c8)])
        S.op("pool", lambda e, c8=c8: e.tensor_tensor(
            out=ss[:, c8:c8 + 1], in0=ss[:, c8:c8 + 1], in1=m05[:, 0:1], op=ALU.pow),
            r=[("ss", c8), ("m05",)], w=[("ss", c8)])
        S.op("act", lambda e, sl=sl, c8=c8: e.activation(
            out=xo[:, sl, :], in_=xt[:, sl, :], func=AF.Copy, scale=ss[:, c8:c8 + 1]),
            r=[("xt", sl), ("ss", c8)], w=[("xo", sl)])
        S.op("dve", lambda e, sl=sl: e.tensor_tensor(out=xo[:, sl, :], in0=xo[:, sl, :], in1=gb[:], op=ALU.mult),
             r=[("xo", sl), ("gb",)], w=[("xo", sl)])
        S.dma("sp", out_d[r0:r0 + 128, :], xo[:, sl, :], r=[("xo", sl)], w=[("out", r0)])
    B.end()


def ffn_full_phase(B, x_in, x_out, wup_d, wdn_d, g_d, C, ntok):
    B.begin()
    load_consts(B, C["ident"])
    gT = load_colvec(B, "g_ffn", g_d, 8)
    ffn_phase(B, x_in, x_out, wup_d, wdn_d, "g_ffn", gT, ntok, "xres", "xres")
    B.end()


def build_full(nlayers=4, nseq=2, debug=False):
    ntok = nseq * SEQ
    B = Builder()
    B.debug_scratch = False
    W, C = declare_io(B, ntok)
    x = B.dram_in("x", [ntok, D])
    out = B.dram_out("out", [ntok, D])
    xres = B.dram_tmp("xres", [ntok, D])
    scr = declare_scratch(B)
    declare_rwkv_scratch(B, scr)
    for l in range(nlayers):
        ffn_full_phase(B, x if l == 0 else xres, xres, W["ffn1_w_up"][l], W["ffn1_w_down"][l], W["ffn1_norm"][l], C, ntok)
        Wl = dict(w0=W["rwkv_w0"][l], w2=W["rwkv_w2"][l], a0=W["rwkv_a0"][l], a2=W["rwkv_a2"][l],
                  g2=W["rwkv_g2"][l], k_k=W["rwkv_k_k"][l], k_a=W["rwkv_k_a"][l], r_k=W["rwkv_r_k"][l],
                  lnx_g=W["rwkv_lnx_g"][l], lnx_b=W["rwkv_lnx_b"][l])
        for sq in range(nseq):
            tok0 = sq * SEQ
            mixnorm_phase(B, xres, tok0, W["mix_norm"][l], C, scr)
            rwkvproj_phase(B, W["w_in"][l], W["rwkv_mu"][l], scr)
            qkvproj_phase(B, W["w_in"][l], W["ga_q_norm"][l], W["ga_k_norm"][l], C, scr)
            attn_phase(B, W["wa_sink"][l], C, scr)
            rwkvprep_phase(B, Wl, C, scr)
            rwkvscan_phase(B, Wl, C, scr)
            merge_phase(B, xres, tok0, W["w_in"][l], [W["w_o_rwkv"][l], W["w_o_ga"][l], W["w_o_wa"][l]],
                        W["w_out"][l], scr)
        ffn_full_phase(B, xres, xres, W["ffn2_w_up"][l], W["ffn2_w_down"][l], W["ffn2_norm"][l], C, ntok)
    finalnorm_phase(B, xres, out, W["final_norm"], ntok)
    nc = B.finish()
    return nc, B


_CACHE = {}


def kernel(**inputs):
    x = np.ascontiguousarray(np.asarray(inputs["x"], dtype=np.float32))
    ncores = 8
    if "nc" not in _CACHE:
        _CACHE["nc"] = build_full()[0]
        _CACHE["consts"] = make_consts()
    nc = _CACHE["nc"]
    consts = _CACHE["consts"]
    shared = {name: np.ascontiguousarray(np.asarray(inputs[name], dtype=np.float32)) for name, _ in W_SPECS}
    shared.update({"c_" + k: v for k, v in consts.items()})
    xs = x.reshape(ncores, 2 * SEQ, D)
    in_maps = []
    for c in range(ncores):
        m = dict(shared)
        m["x"] = np.ascontiguousarray(xs[c])
        in_maps.append(m)
    res = run_bass_kernel_spmd(nc, in_maps, core_ids=list(range(ncores)))
    outs = [np.asarray(r["out"], dtype=np.float32) for r in res.results]
    return np.stack(outs, 0).reshape(16, SEQ, D)
```
